# Optimizing a Trainium2 kernel written in Bass

```python
import math
import jax, jax.numpy as jnp
from jax import lax
import numpy as np

D_MODEL = 1024
BATCH = 16
SEQ = 4096
DEPTH = 1

D_MIX = D_MODEL
MLA_HEADS = 4
MLA_NOPE = 128
MLA_ROPE = 64
MLA_QK = MLA_NOPE + MLA_ROPE
MLA_V = 128
MLA_Q_LORA = 384
MLA_KV_LORA = 256
DIFF_HEADS = 4
DIFF_DQK = 64
DIFF_DV = 2 * DIFF_DQK
IN_MLA = MLA_Q_LORA + MLA_KV_LORA + MLA_ROPE
IN_DIFF_Q = DIFF_HEADS * 2 * DIFF_DQK
IN_DIFF_V = DIFF_HEADS * DIFF_DV
IN_COLS = IN_MLA + 2 * IN_DIFF_Q + IN_DIFF_V
ROPE_THETA = 10000.0
Q_BLOCK = 128
RMS_EPS = 1e-6
PEER_HEADS = 8
PEER_NKEYS = 128
PEER_EXPERTS = PEER_NKEYS * PEER_NKEYS
PEER_DK_HALF = 128
PEER_TOPK = 16
PEER_CHUNK = 128
ADA_INIT = 0.2

kernel_name = "hybrid_mla_diffattn_peer_adaln"


def rms_norm(x, g):
    xf = x.astype(jnp.float32)
    y = xf * lax.rsqrt(jnp.mean(xf * xf, axis=-1, keepdims=True) + RMS_EPS)
    return (y * g.astype(jnp.float32)).astype(x.dtype)


def rope_tables(seq, dim):
    half = dim // 2
    inv = 1.0 / (ROPE_THETA ** (jnp.arange(half, dtype=jnp.float32) / half))
    ang = jnp.arange(seq, dtype=jnp.float32)[:, None] * inv[None, :]
    return jnp.cos(ang), jnp.sin(ang)


def apply_rope(x, cos, sin):
    half = x.shape[-1] // 2
    xf = x.astype(jnp.float32)
    x1, x2 = xf[..., :half], xf[..., half:]
    return jnp.concatenate([x1 * cos - x2 * sin, x2 * cos + x1 * sin], axis=-1).astype(x.dtype)


def causal_attention(qs, ks, coeffs, v, scale):
    B, H, S, dv = v.shape
    nb = S // Q_BLOCK
    qbs = tuple(q.reshape(B, H, nb, Q_BLOCK, q.shape[-1]).transpose(2, 0, 1, 3, 4) for q in qs)
    kpos = jnp.arange(S)

    def one_block(args):
        i, qb = args
        qpos = i * Q_BLOCK + jnp.arange(Q_BLOCK)
        mask = kpos[None, :] <= qpos[:, None]
        w = None
        for qi, ki, ci in zip(qb, ks, coeffs):
            s = jnp.einsum('bhqd,bhkd->bhqk', qi, ki).astype(jnp.float32) * scale
            p = jax.nn.softmax(jnp.where(mask, s, -jnp.inf), axis=-1)
            w = ci * p if w is None else w + ci * p
        return jnp.einsum('bhqk,bhkd->bhqd', w.astype(v.dtype), v)

    out = lax.map(one_block, (jnp.arange(nb), qbs))
    return out.transpose(1, 2, 0, 3, 4).reshape(B, H, S, dv)


def peer_ffn(h, w_q, sub_keys, u_tab, v_tab):
    T, D = h.shape
    hc = h.reshape(T // PEER_CHUNK, PEER_CHUNK, D)

    def one_chunk(hb):
        C = hb.shape[0]
        q = (hb @ w_q).reshape(C, PEER_HEADS, 2, PEER_DK_HALF)
        s = jnp.einsum('thpd,hpnd->thpn', q, sub_keys).astype(jnp.float32)
        sv, si = lax.top_k(s, PEER_TOPK)
        cand = sv[:, :, 0, :, None] + sv[:, :, 1, None, :]
        cidx = si[:, :, 0, :, None] * PEER_NKEYS + si[:, :, 1, None, :]
        cand = cand.reshape(C, PEER_HEADS, PEER_TOPK * PEER_TOPK)
        cidx = cidx.reshape(C, PEER_HEADS, PEER_TOPK * PEER_TOPK)
        fv, fpos = lax.top_k(cand, PEER_TOPK)
        eidx = jnp.take_along_axis(cidx, fpos, axis=-1).reshape(C, PEER_HEADS * PEER_TOPK)
        g = jax.nn.softmax(fv, axis=-1).reshape(C, PEER_HEADS * PEER_TOPK)
        u = u_tab[eidx]
        a = jax.nn.gelu(jnp.einsum('ted,td->te', u, hb).astype(jnp.float32), approximate=False)
        coef = (g * a).astype(hb.dtype)
        vv = v_tab[eidx]
        return jnp.einsum('te,ted->td', coef, vv)

    return lax.map(one_chunk, hc).reshape(T, D)


def setup_inputs(seed: int = 0) -> dict:
    key = jax.random.key(seed)
    keys = iter(jax.random.split(key, 40))

    def nrm(shape, scale):
        return jax.random.normal(next(keys), shape, jnp.float32) * scale

    def gain(shape):
        return 1.0 + nrm(shape, 0.02)

    L = DEPTH
    return {
        'x': nrm((BATCH, SEQ, D_MODEL), 1.0),
        'c': nrm((BATCH, D_MODEL), 1.0),
        'ada_w': nrm((L, D_MODEL, 6 * D_MODEL), ADA_INIT * D_MODEL ** -0.5),
        'ada_b': nrm((L, 6 * D_MODEL), 0.02),
        'norm1_g': gain((L, D_MODEL)),
        'w_in': nrm((L, D_MODEL, IN_COLS), D_MODEL ** -0.5),
        'mla_q_lat_g': gain((L, MLA_Q_LORA)),
        'mla_w_q_up': nrm((L, MLA_Q_LORA, MLA_HEADS * MLA_QK), MLA_Q_LORA ** -0.5),
        'mla_kv_lat_g': gain((L, MLA_KV_LORA)),
        'mla_w_kv_up': nrm((L, MLA_KV_LORA, MLA_HEADS * (MLA_NOPE + MLA_V)), MLA_KV_LORA ** -0.5),
        'mla_q_g': gain((L, MLA_QK)),
        'mla_k_g': gain((L, MLA_QK)),
        'diff_q_g': gain((L, DIFF_DQK)),
        'diff_k_g': gain((L, DIFF_DQK)),
        'diff_lq1': nrm((L, DIFF_DQK), 0.1),
        'diff_lk1': nrm((L, DIFF_DQK), 0.1),
        'diff_lq2': nrm((L, DIFF_DQK), 0.1),
        'diff_lk2': nrm((L, DIFF_DQK), 0.1),
        'diff_subln_g': gain((L, DIFF_DV)),
        'w_out': nrm((L, D_MIX, D_MODEL), D_MIX ** -0.5),
        'norm2_g': gain((L, D_MODEL)),
        'peer_w_q': nrm((L, D_MODEL, PEER_HEADS * 2 * PEER_DK_HALF), D_MODEL ** -0.5),
        'peer_sub_keys': nrm((L, PEER_HEADS, 2, PEER_NKEYS, PEER_DK_HALF), PEER_DK_HALF ** -0.5),
        'peer_u': nrm((L, PEER_EXPERTS, D_MODEL), D_MODEL ** -0.5),
        'peer_v': nrm((L, PEER_EXPERTS, D_MODEL), PEER_HEADS ** -0.5),
    }


def reference(x, c, ada_w, ada_b, norm1_g, w_in, mla_q_lat_g, mla_w_q_up, mla_kv_lat_g,
              mla_w_kv_up, mla_q_g, mla_k_g, diff_q_g, diff_k_g, diff_lq1, diff_lk1, diff_lq2,
              diff_lk2, diff_subln_g, w_out, norm2_g, peer_w_q, peer_sub_keys, peer_u, peer_v):
    B, S, D = x.shape
    cos_m, sin_m = rope_tables(S, MLA_ROPE)
    cos_d, sin_d = rope_tables(S, DIFF_DQK)

    for l in range(DEPTH):
        mod = jax.nn.silu(c) @ ada_w[l] + ada_b[l]
        sh1, sc1, g1, sh2, sc2, g2 = [m[:, None, :] for m in jnp.split(mod, 6, axis=-1)]

        h = rms_norm(x, norm1_g[l]) * (1.0 + sc1) + sh1
        proj = h @ w_in[l]
        o0 = MLA_Q_LORA
        o1 = o0 + MLA_KV_LORA
        o2 = o1 + MLA_ROPE
        o3 = o2 + IN_DIFF_Q
        o4 = o3 + IN_DIFF_Q
        q_lat, kv_lat, kpe_raw = proj[..., :o0], proj[..., o0:o1], proj[..., o1:o2]
        dq, dk, dv = proj[..., o2:o3], proj[..., o3:o4], proj[..., o4:]

        q = (rms_norm(q_lat, mla_q_lat_g[l]) @ mla_w_q_up[l]).reshape(B, S, MLA_HEADS, MLA_QK)
        q = q.transpose(0, 2, 1, 3)
        q_nope = rms_norm(q[..., :MLA_NOPE], mla_q_g[l, :MLA_NOPE])
        q_pe = apply_rope(rms_norm(q[..., MLA_NOPE:], mla_q_g[l, MLA_NOPE:]), cos_m, sin_m)
        kv = (rms_norm(kv_lat, mla_kv_lat_g[l]) @ mla_w_kv_up[l]).reshape(B, S, MLA_HEADS, MLA_NOPE + MLA_V)
        kv = kv.transpose(0, 2, 1, 3)
        k_nope = rms_norm(kv[..., :MLA_NOPE], mla_k_g[l, :MLA_NOPE])
        v_m = kv[..., MLA_NOPE:]
        k_pe = apply_rope(rms_norm(kpe_raw, mla_k_g[l, MLA_NOPE:])[:, None], cos_m, sin_m)
        k_m = jnp.concatenate([k_nope, jnp.broadcast_to(k_pe, (B, MLA_HEADS, S, MLA_ROPE))], axis=-1)
        q_m = jnp.concatenate([q_nope, q_pe], axis=-1)
        out_m = causal_attention((q_m,), (k_m,), (1.0,), v_m, MLA_QK ** -0.5)

        qd = dq.reshape(B, S, DIFF_HEADS, 2, DIFF_DQK).transpose(0, 2, 3, 1, 4)
        kd = dk.reshape(B, S, DIFF_HEADS, 2, DIFF_DQK).transpose(0, 2, 3, 1, 4)
        qd = apply_rope(rms_norm(qd, diff_q_g[l]), cos_d, sin_d)
        kd = apply_rope(rms_norm(kd, diff_k_g[l]), cos_d, sin_d)
        vd = dv.reshape(B, S, DIFF_HEADS, DIFF_DV).transpose(0, 2, 1, 3)
        lambda_init = 0.8 - 0.6 * math.exp(-0.3 * l)
        lam = (jnp.exp(jnp.sum(diff_lq1[l].astype(jnp.float32) * diff_lk1[l].astype(jnp.float32)))
               - jnp.exp(jnp.sum(diff_lq2[l].astype(jnp.float32) * diff_lk2[l].astype(jnp.float32)))
               + lambda_init)
        out_d = causal_attention((qd[:, :, 0], qd[:, :, 1]), (kd[:, :, 0], kd[:, :, 1]),
                                 (1.0, -lam), vd, DIFF_DQK ** -0.5)
        out_d = rms_norm(out_d, diff_subln_g[l]) * (1.0 - lambda_init)

        mixed = jnp.concatenate([out_m.transpose(0, 2, 1, 3).reshape(B, S, MLA_HEADS * MLA_V),
                                 out_d.transpose(0, 2, 1, 3).reshape(B, S, DIFF_HEADS * DIFF_DV)], axis=-1)
        x = x + g1 * (mixed @ w_out[l])

        h2 = rms_norm(x, norm2_g[l]) * (1.0 + sc2) + sh2
        y = peer_ffn(h2.reshape(B * S, D), peer_w_q[l], peer_sub_keys[l], peer_u[l], peer_v[l])
        x = x + g2 * y.reshape(B, S, D)
    return x
```

```python
import math
import numpy as np
from contextlib import ExitStack
import concourse.bass as bass
import concourse.mybir as mybir
from concourse.bass_utils import run_bass_kernel_spmd

F32 = mybir.dt.float32
BF16 = mybir.dt.bfloat16
U32 = mybir.dt.uint32
I32 = mybir.dt.int32
ALU = mybir.AluOpType
AF = mybir.ActivationFunctionType
AX = mybir.AxisListType

ENGS = ("pe", "act", "dve", "pool", "sp")
CHUNK = 16000
EPS = 1e-6
D = 1024
NEXP = 16384


class Tok:
    __slots__ = ("name", "w", "r", "dsem", "dcount", "excl")

    def __init__(self, name):
        self.name = name
        self.excl = False
        self.w = None
        self.r = {}
        self.dsem = None
        self.dcount = 0


class Prog:
    def __init__(self, nc):
        self.nc = nc
        self.es = ExitStack()
        self.ops = {e: [] for e in ENGS}
        self.cnt = {e: 0 for e in ENGS}
        self.esems = {e: [] for e in ENGS}
        self.seen = {e: {} for e in ENGS}
        self.semobj = {}
        self.nsem = 0
        self.final_events = []
        self.dma_toks = []
        self.nins = 0

    def sbuf(self, name, shape, dt):
        return self.es.enter_context(self.nc.sbuf_tensor(name, list(shape), dt))

    def psum(self, name, shape, dt):
        return self.es.enter_context(self.nc.psum_tensor(name, list(shape), dt))

    def newsem(self, name):
        self.nsem += 1
        return self.es.enter_context(self.nc.semaphore(name))

    def tok(self, name):
        return Tok(name)

    def toks(self, name, n):
        return [Tok(f"{name}{i}") for i in range(n)]

    def _eng_event(self, eng):
        c = self.cnt[eng]
        ch, v = divmod(c, CHUNK)
        while len(self.esems[eng]) <= ch:
            s = self.newsem(f"e_{eng}_{len(self.esems[eng])}")
            self.esems[eng].append(s)
            self.semobj[(eng, len(self.esems[eng]) - 1)] = s
        self.cnt[eng] = c + 1
        return ((eng, ch), v + 1)

    def _collect(self, eng, reads, writes, same_eng_sync):
        deps = {}

        def add(ev):
            if ev is None:
                return
            k, v = ev
            if deps.get(k, 0) < v:
                deps[k] = v

        for b in reads:
            add(b.w)
        for b in writes:
            add(b.w)
            for k, v in b.r.items():
                add((k, v))
        waits = []
        seen = self.seen[eng]
        for k, v in deps.items():
            if (not same_eng_sync) and k[0] == eng:
                continue
            if seen.get(k, 0) >= v:
                continue
            seen[k] = v
            waits.append((k, v))
        return waits

    def _record(self, ev, reads, writes):
        k, v = ev
        for b in reads:
            if b.r.get(k, 0) < v:
                b.r[k] = v
        for b in writes:
            b.w = ev
            b.r = {}

    def op(self, eng, fn, reads=(), writes=(), sync_same=None):
        if sync_same is None:
            sync_same = eng != "pe"
        ex = [b for b in reads if b.excl]
        if ex:
            writes = list(writes) + ex
        waits = self._collect(eng, reads, writes, sync_same)
        ev = self._eng_event(eng)
        self._record(ev, reads, writes)
        self.ops[eng].append((waits, fn, ev, 1))
        self.nins += 1

    def dma(self, eng, fn, owner, reads=(), writes=(), final=False):
        waits = self._collect(eng, reads, writes, False)
        if owner.dsem is None:
            owner.dsem = ("dma", self.nsem)
            self.semobj[owner.dsem] = self.newsem(f"d{self.nsem}")
            self.dma_toks.append(owner)
        owner.dcount += 16
        ev = (owner.dsem, owner.dcount)
        self._record(ev, reads, writes)
        self.ops[eng].append((waits, fn, ev, 16))
        if final:
            self.final_events.append(ev)
        self.nins += 1

    def barrier(self):
        evs = {}
        for e in ENGS:
            c = self.cnt[e]
            if c == 0:
                continue
            ch, v = divmod(c - 1, CHUNK)
            evs[(e, ch)] = v + 1
        for t in self.dma_toks:
            evs[t.dsem] = t.dcount
        for e in ENGS:
            waits = []
            for k, v in evs.items():
                if k[0] == e:
                    continue
                if self.seen[e].get(k, 0) >= v:
                    continue
                self.seen[e][k] = v
                waits.append((k, v))
            self.ops[e].append((waits, None, None, 0))

    def emit(self):
        nc = self.nc
        fw = {}
        for k, v in self.final_events:
            fw[k] = max(fw.get(k, 0), v)
        self.ops["sp"].append((list(fw.items()), None, None, 0))
        semobj = self.semobj

        def run(engobj, lst):
            for waits, fn, ev, inc in lst:
                for k, v in waits:
                    engobj.wait_ge(semobj[k], v)
                if fn is None:
                    continue
                ins = fn(engobj)
                ins.then_inc(semobj[ev[0]], inc)

        with nc.Block() as block:
            @block.tensor
            def _(e):
                run(e, self.ops["pe"])

            @block.scalar
            def _(e):
                run(e, self.ops["act"])

            @block.vector
            def _(e):
                run(e, self.ops["dve"])

            @block.gpsimd
            def _(e):
                run(e, self.ops["pool"])

            @block.sync
            def _(e):
                run(e, self.ops["sp"])
        self.es.close()

    def mm(self, out, lhsT, rhs, start, stop, reads, writes):
        self.op("pe", lambda e: e.matmul(out, lhsT=lhsT, rhs=rhs, start=start, stop=stop), reads, writes)

    def tr(self, out, in_, ident, reads, writes):
        self.op("pe", lambda e: e.transpose(out=out, in_=in_, identity=ident), reads, writes)

    def act(self, out, in_, func, reads, writes, bias=None, scale=None, accum=None):
        kw = {}
        if bias is not None:
            kw["bias"] = bias
        if scale is not None:
            kw["scale"] = scale
        if accum is not None:
            kw["accum_out"] = accum
        self.op("act", lambda e: e.activation(out=out, in_=in_, func=func, **kw), reads, writes)

    def tt(self, eng, out, in0, in1, op, reads, writes):
        self.op(eng, lambda e: e.tensor_tensor(out=out, in0=in0, in1=in1, op=op), reads, writes)

    def ts(self, eng, out, in0, s1, op0, reads, writes, s2=None, op1=None):
        if op1 is None:
            self.op(eng, lambda e: e.tensor_scalar(out=out, in0=in0, scalar1=s1, scalar2=None, op0=op0), reads, writes)
        else:
            self.op(eng, lambda e: e.tensor_scalar(out=out, in0=in0, scalar1=s1, scalar2=s2, op0=op0, op1=op1), reads, writes)

    def stt(self, out, in0, scalar, in1, op0, op1, reads, writes, accum=None):
        if accum is None:
            self.op("dve", lambda e: e.scalar_tensor_tensor(out=out, in0=in0, scalar=scalar, in1=in1, op0=op0, op1=op1), reads, writes)
        else:
            self.op("dve", lambda e: e.scalar_tensor_tensor(out=out, in0=in0, scalar=scalar, in1=in1, op0=op0, op1=op1, accum_out=accum), reads, writes)

    def red(self, out, in_, reads, writes, op=ALU.add):
        self.op("dve", lambda e: e.tensor_reduce(out=out, in_=in_, axis=AX.X, op=op), reads, writes)

    def cp(self, eng, out, in_, reads, writes):
        if eng == "act":
            self.op("act", lambda e: e.copy(out=out, in_=in_), reads, writes)
        else:
            self.op(eng, lambda e: e.tensor_copy(out=out, in_=in_), reads, writes)

    def recip(self, out, in_, reads, writes):
        self.op("dve", lambda e: e.reciprocal(out=out, in_=in_), reads, writes)

    def memset(self, eng, ap, val, writes):
        self.op(eng, lambda e: e.memset(ap, val), (), writes)

    def ld(self, eng, out, in_, owner, reads=(), writes=None, final=False):
        if writes is None:
            writes = [owner]
        self.dma(eng, lambda e: e.dma_start(out=out, in_=in_), owner, reads, writes, final)


class Arena:
    def __init__(self, ap, words):
        self.ap = ap
        self.words = words
        self.off = 0

    def f32(self, n):
        n = (n + 1) // 2 * 2
        a = self.ap[:, self.off:self.off + n]
        self.off += n
        self.hw = max(getattr(self, "hw", 0), self.off)
        assert self.off <= self.words, f"arena overflow {self.off} > {self.words}"
        return a

    def bf16(self, n):
        w = (n + 1) // 2
        return self.f32(w).bitcast(BF16)[:, 0:n]

    def u32(self, n):
        return self.f32(n).bitcast(U32)


G_N1 = 0
G_QLAT = G_N1 + 1024
G_KVLAT = G_QLAT + 384
G_KPE = G_KVLAT + 256
G_DQK = G_KPE + 64
G_Q12 = G_DQK + 1024
G_K4 = G_Q12 + 768
G_SUB = G_K4 + 512
G_N2 = G_SUB + 128
G_LQ = G_N2 + 1024
G_TOT = G_LQ + 256

C_ID = 0
C_TRI = 128
C_IOTA = 256
C_ONES = 272
C_THR = 400
C_TOT = 416

LAMBDA_INIT = 0.8 - 0.6 * math.exp(-0.3 * 0)


def build(S, NSEQ, do_a1=True, do_a2=True, do_b=True, stage_lim=9):
    NT = S // 128
    NCH = S // 512
    TT = NSEQ * S
    nc = bass.Bass("TRN2", target_bir_lowering=False)
    P = Prog(nc)

    def din(name, shape, dt=F32):
        return nc.dram_tensor(name, list(shape), dt, kind="ExternalInput").ap()

    x_d = din("x", [TT, D])
    cT_d = din("cT", [128, NSEQ, 8])
    adaw_d = din("ada_w", [D, 6 * D])
    adab_d = din("ada_b", [1, 6 * D])
    win_d = din("w_in", [D, 2240])
    wq_d = din("w_q_up", [384, 768])
    wkv_d = din("w_kv_up", [256, 1024])
    wout_d = din("w_out", [D, D])
    pwq_d = din("peer_w_q", [D, 2048])
    keys_d = din("keysT", [128, 16, 128])
    u_d = din("peer_u", [NEXP, D])
    v_d = din("peer_v", [NEXP, D])
    gains_d = din("gains", [128, G_TOT])
    rope_d = din("rope", [128, NT, 64])
    consts_d = din("consts", [128, C_TOT])
    out_d = nc.dram_tensor("out", [TT, D], F32, kind="ExternalOutput").ap()
    x1_d = nc.dram_tensor("x1s", [TT, D], F32, kind="Internal").ap()
    tx1 = [P.tok(f"x1d{i}") for i in range(TT // 128)]

    AW = 52000
    arena_t = P.sbuf("arena", [128, AW], F32)
    A = Arena(arena_t, AW)
    banks = [P.psum(f"pb{i}", [128, 512], F32) for i in range(8)]
    tb = P.toks("bank", 8)
    for t_ in tb:
        t_.excl = True

    def bankbf(i):
        return banks[i][:, :].bitcast(BF16)

    consts = A.f32(C_TOT)
    t_consts = P.tok("consts")
    P.ld("sp", consts, consts_d[:, :], t_consts)
    ident_f = consts[:, C_ID:C_ID + 128]
    iota16 = consts[:, C_IOTA:C_IOTA + 16]
    ones_f = consts[:, C_ONES:C_ONES + 128]
    thr16 = consts[:, C_THR:C_THR + 16]
    identb = A.bf16(128)
    trib = A.bf16(128)
    t_cb = P.tok("constsb")
    P.cp("dve", identb, ident_f, [t_consts], [t_cb])
    P.cp("dve", trib, consts[:, C_TRI:C_TRI + 128], [t_consts], [t_cb])
    cT = A.f32(NSEQ * 8)
    t_cT = P.tok("cT")
    P.ld("sp", cT, cT_d.rearrange("p b k -> p (b k)"), t_cT)
    adab = [A.f32(512), A.f32(512)]
    t_adab = P.toks("adab", 2)
    base_off = A.off

    def small(n):
        return A.f32(n)

    def load_cast_weight(dst_bf, src_dram_rows, ncols, stage, t_stage, t_dst, k, eng_i):
        st = stage[eng_i % len(stage)]
        ts_ = t_stage[eng_i % len(stage)]
        P.ld("sp", st[:, 0:ncols], src_dram_rows, ts_)
        eng = ("act", "dve", "pool")[eng_i % 3]
        P.cp(eng, dst_bf, st[:, 0:ncols], [ts_], [t_dst])

    def compute_mod(b, chunks, dests, t_dests, wst, t_wst, mbank):
        sil = small(8)
        crep = A.f32(8 * 128)
        t_sil = P.tok("sil")
        t_crep = P.tok("crep")
        P.act(sil, cT[:, b * 8:(b + 1) * 8], AF.Silu, [t_cT], [t_sil])
        P.cp("dve", crep.rearrange("p (k m) -> p k m", k=8), sil.unsqueeze(2).to_broadcast([128, 8, 128]), [t_sil], [t_crep])
        adaw_v = adaw_d.rearrange("(k p) n -> p k n", p=128)
        for i, ch in enumerate(chunks):
            w = wst[i % 2]
            tw = t_wst[i % 2]
            P.ld("sp", w.rearrange("p (k n) -> p k n", k=8), adaw_v[:, :, ch * 512:(ch + 1) * 512], tw)
            bk = mbank[i % 2]
            P.ld("sp", adab[i % 2][0:1, :], adab_d[:, ch * 512:(ch + 1) * 512], t_adab[i % 2])
            for k in range(8):
                P.mm(banks[bk][:, :], crep[:, k * 128:(k + 1) * 128], w[:, k * 512:(k + 1) * 512], k == 0, False, [t_crep, tw], [tb[bk]])
            P.mm(banks[bk][:, :], ones_f[0:1, :], adab[i % 2][0:1, :], False, True, [t_consts, t_adab[i % 2]], [tb[bk]])
            P.cp("act", dests[i], banks[bk][:, :], [tb[bk]], [t_dests[i]])

    def rope(src, dst, G, cs, t_src, t_dst, t_cs, tmp, t_tmp):
        cosb = cs[:, 0:32].unsqueeze(1).to_broadcast([128, G, 32])
        sinb = cs[:, 32:64].unsqueeze(1).to_broadcast([128, G, 32])
        x1 = src[:, :, 0:32]
        x2 = src[:, :, 32:64]
        t1 = tmp[:, 0:G * 32].rearrange("p (g d) -> p g d", g=G)
        t2 = tmp[:, G * 32:2 * G * 32].rearrange("p (g d) -> p g d", g=G)
        t3 = tmp[:, 2 * G * 32:3 * G * 32].rearrange("p (g d) -> p g d", g=G)
        t4 = tmp[:, 3 * G * 32:4 * G * 32].rearrange("p (g d) -> p g d", g=G)
        P.tt("dve", t1, x1, cosb, ALU.mult, [t_src, t_cs], [t_tmp[0]])
        P.tt("dve", t2, x2, sinb, ALU.mult, [t_src, t_cs], [t_tmp[1]])
        P.tt("dve", dst[:, :, 0:32], t1, t2, ALU.subtract, [t_tmp[0], t_tmp[1]], [t_dst])
        P.tt("pool", t3, x2, cosb, ALU.mult, [t_src, t_cs], [t_tmp[2]])
        P.tt("pool", t4, x1, sinb, ALU.mult, [t_src, t_cs], [t_tmp[3]])
        P.tt("pool", dst[:, :, 32:64], t3, t4, ALU.add, [t_tmp[2], t_tmp[3]], [t_dst])

    def rstd_from_ss(ss, n, t_ss, scale):
        P.act(ss, ss, AF.Sqrt, [t_ss], [t_ss], bias=EPS, scale=scale)
        P.recip(ss, ss, [t_ss], [t_ss])

    if do_a1 or do_a2:
        A.off = base_off
        ropes = A.f32(NT * 64)
        t_rope = P.tok("rope")
        P.ld("sp", ropes.rearrange("p (t d) -> p t d", t=NT), rope_d[:, :, :], t_rope)
        lam2 = small(2)
        neglam = small(2)
        t_lam = P.tok("lam")
        gsub = A.f32(128)
        G1 = A.f32(1024)
        SH1 = A.f32(1024)
        GATE1 = A.f32(1024)
        t_mod = P.tok("mod1")
        kv_off = A.off
        lq = A.f32(256)
        t_lq = P.tok("lq")
        P.ld("sp", lq, gains_d[:, G_LQ:G_LQ + 256], t_lq)
        prod = A.f32(128)
        lq4 = lq.rearrange("p (a b d) -> p a b d", a=2, b=2)
        P.tt("dve", prod.rearrange("p (a d) -> p a d", a=2), lq4[:, :, 0, :], lq4[:, :, 1, :], ALU.mult, [t_lq], [t_lam])
        P.red(lam2[:, 0:2], prod.rearrange("p (a d) -> p a d", a=2), [t_lam], [t_lam])
        P.act(lam2[:, 0:2], lam2[:, 0:2], AF.Exp, [t_lam], [t_lam])
        P.tt("dve", neglam[:, 0:1], lam2[:, 1:2], lam2[:, 0:1], ALU.subtract, [t_lam], [t_lam])
        P.ts("dve", neglam[:, 0:1], neglam[:, 0:1], -LAMBDA_INIT, ALU.add, [t_lam], [t_lam])
        gs_t = A.f32(128)
        t_gs = P.tok("gs_t")
        P.ld("sp", gs_t, gains_d[:, G_SUB:G_SUB + 128], t_gs)
        P.ts("dve", gsub, gs_t, 1.0 - LAMBDA_INIT, ALU.mult, [t_gs], [t_lam])

        for s in range(NSEQ):
            A.off = kv_off
            P.barrier()
            wst = [A.f32(4096), A.f32(4096)]
            t_wst = P.toks("wst", 2)
            mtmp = [A.f32(512), A.f32(512)]
            t_mtmp = P.toks("mtmp", 2)
            gn1 = A.f32(1024)
            t_gn1 = P.tok("gn1")
            P.ld("sp", gn1, gains_d[:, G_N1:G_N1 + 1024], t_gn1)
            dests = [SH1[:, 0:512], SH1[:, 512:1024], mtmp[0], mtmp[1], GATE1[:, 0:512], GATE1[:, 512:1024]]
            t_dests = [t_mod, t_mod, t_mtmp[0], t_mtmp[1], t_mod, t_mod]
            compute_mod(s, [0, 1, 2, 3, 4, 5], dests, t_dests, wst, t_wst, [0, 1])
            for i in range(2):
                P.stt(G1[:, i * 512:(i + 1) * 512], mtmp[i], 1.0, gn1[:, i * 512:(i + 1) * 512], ALU.add, ALU.mult, [t_mtmp[i], t_gn1], [t_mod])

            for phase in ("a1", "a2"):
                if phase == "a1" and not do_a1:
                    continue
                if phase == "a2" and not do_a2:
                    continue
                A.off = kv_off
                P.barrier()
                a1 = phase == "a1"
                t_gA = P.tok("gA")
                t_w = P.tok("wA")
                gseg = {}

                def gload(name, col, n):
                    buf = A.f32(n)
                    P.ld("sp", buf, gains_d[:, col:col + n], t_gA)
                    gseg[name] = buf

                if a1:
                    gload("qlat", G_QLAT, 384); gload("kvlat", G_KVLAT, 256); gload("kpe", G_KPE, 64)
                    gload("q12", G_Q12, 768); gload("k4", G_K4, 512)
                    WCOL0, WN = 0, 704
                else:
                    gload("dqk", G_DQK, 1024)
                    WCOL0, WN = 704, 1536
                win_b = A.bf16(8 * WN)
                wout_b = A.bf16(4 * 1024)
                wofs = 0 if a1 else 512
                if a1:
                    wq_b = A.bf16(3 * 768)
                    wkv_b = A.bf16(2 * 1024)
                mark = A.off
                stage = [A.f32(1536), A.f32(1536)]
                t_stage = P.toks("stage", 2)
                ei = 0
                for k in range(8):
                    load_cast_weight(win_b[:, k * WN:(k + 1) * WN], win_d[k * 128:(k + 1) * 128, WCOL0:WCOL0 + WN], WN, stage, t_stage, t_w, k, ei); ei += 1
                for k in range(4):
                    load_cast_weight(wout_b[:, k * 1024:(k + 1) * 1024], wout_d[wofs + k * 128:wofs + (k + 1) * 128, :], 1024, stage, t_stage, t_w, k, ei); ei += 1
                if a1:
                    for k in range(3):
                        load_cast_weight(wq_b[:, k * 768:(k + 1) * 768], wq_d[k * 128:(k + 1) * 128, :], 768, stage, t_stage, t_w, k, ei); ei += 1
                    for k in range(2):
                        load_cast_weight(wkv_b[:, k * 1024:(k + 1) * 1024], wkv_d[k * 128:(k + 1) * 128, :], 1024, stage, t_stage, t_w, k, ei); ei += 1
                P.barrier()
                A.off = mark
                if a1:
                    kTn = A.bf16(4 * S)
                    kTp = A.bf16(S)
                    Vv = A.bf16(NT * 4 * 130)
                else:
                    kTn = A.bf16(4 * S)
                    kTp = None
                    Vv = A.bf16(NT * 4 * 130)
                kTn_v = kTn.rearrange("p (h s) -> p h s", h=4)
                Vv_v = Vv.rearrange("p (t h d) -> p t h d", t=NT, h=4)
                t_kv = P.toks("kv", NT)
                t_vinit = P.tok("vinit")
                P.memset("pool", Vv, 1.0, [t_vinit] + t_kv)
                xt = [A.f32(1024), A.f32(1024)]
                t_xt = P.toks("xt", 2)
                junk = A.bf16(1024)
                t_junk = P.tok("junk")
                sm = [A.f32(64), A.f32(64)]
                t_sm = P.toks("sm", 2)
                tmpf = A.f32(1024)
                t_tmpf = P.tok("tmpf")
                hb = A.bf16(1024)
                t_hb = P.tok("hb")
                hT = A.bf16(1024)
                t_hT = P.tok("hT")
                NPJ = 704 if a1 else 1536
                proj = A.f32(NPJ)
                t_proj = P.tok("proj")
                sq = A.f32(768 if a1 else 1024)
                t_sq = P.tok("sq")
                nrm = A.f32(64 if a1 else 1024)
                t_nrm = P.tok("nrm")
                rtmp = A.f32(4 * (4 if a1 else 16) * 32)
                t_rtmp = P.toks("rtmp", 4)
                rot = A.bf16(64 if a1 else 1024)
                t_rot = P.tok("rot")
                if a1:
                    qn = A.bf16(640)
                    t_qn = P.tok("qn")
                    qnT = A.bf16(640)
                    t_qnT = P.tok("qnT")
                    qf = A.f32(768)
                    t_qf = P.tok("qf")
                    qpe = A.f32(256)
                    t_qpe = P.tok("qpe")
                    qm = A.bf16(768)
                    t_qm = P.tok("qm")
                    kf = A.f32(512)
                    t_kf = P.tok("kf")
                    kn = A.bf16(512)
                    t_kn = P.tok("kn")
                    qTn = A.bf16(4 * 512)
                    qTp = A.bf16(4 * 512)
                    qTp_v = qTp.rearrange("p (h s) -> p h s", h=4)
                else:
                    qTn = A.bf16(4 * 512)
                qTn_v = qTn.rearrange("p (h s) -> p h s", h=4)
                t_qT = P.tok("qT")
                PT = [A.bf16(512) for _ in range(3)]
                t_PT = P.toks("PT", 3)
                mixed = A.bf16(4 * 512)
                mixed_v = mixed.rearrange("p (q c) -> p q c", q=4)
                t_mixed = P.toks("mixed", 4)
                mT = A.bf16(512)
                t_mT = P.tok("mT")
                xr = xt
                t_xr = t_xt
                ytmp = A.f32(1024)
                t_ytmp = P.tok("ytmp")
                osm = A.f32(64)
                t_osm = P.tok("osm")
                oa = A.f32(128)
                t_oa = P.tok("oa")
                od = A.f32(128)
                t_od = P.tok("od")
                ti = 0
                pti = 0
                for c in range(NCH if stage_lim >= 1 else 0):
                    for j in range(4):
                        tl = c * 4 + j
                        row0 = s * S + tl * 128
                        xb = xt[ti % 2]; txb = t_xt[ti % 2]
                        smb = sm[ti % 2]; tsm = t_sm[ti % 2]
                        ti += 1
                        P.ld("sp", xb, x_d[row0:row0 + 128, :], txb)
                        P.act(junk, xb, AF.Square, [txb], [t_junk, tsm], accum=smb[:, 0:1])
                        rstd_from_ss(smb[:, 0:1], 1, tsm, 1.0 / 1024)
                        P.stt(tmpf, xb, smb[:, 0:1], G1, ALU.mult, ALU.mult, [txb, tsm, t_mod], [t_tmpf])
                        P.tt("pool", hb, tmpf, SH1, ALU.add, [t_tmpf, t_mod], [t_hb])
                        Tb = bankbf(6)
                        for k in range(8):
                            P.tr(Tb[:, k * 128:(k + 1) * 128], hb[:, k * 128:(k + 1) * 128], identb, [t_hb, t_cb], [tb[6]])
                        P.cp("act", hT, Tb, [tb[6]], [t_hT])
                        if a1:
                            chunks = [(0, 512), (512, 704)]
                            wcol0 = 0
                        else:
                            chunks = [(0, 512), (512, 1024), (1024, 1536)]
                            wcol0 = 704
                        for ci, (c0, c1) in enumerate(chunks):
                            bk = ci
                            for k in range(8):
                                P.mm(banks[bk][:, 0:c1 - c0], hT[:, k * 128:(k + 1) * 128],
                                     win_b[:, k * WN + c0:k * WN + c1], k == 0, k == 7, [t_hT, t_w], [tb[bk]])
                            P.cp("act" if ci % 2 == 0 else "dve", proj[:, c0:c1], banks[bk][:, 0:c1 - c0], [tb[bk]], [t_proj])
                        cs = ropes[:, tl * 64:(tl + 1) * 64]
                        if a1:
                            P.act(sq[:, 0:704], proj[:, 0:704], AF.Square, [t_proj], [t_sq])
                            ss11 = smb[:, 4:15]
                            P.red(ss11, sq[:, 0:704].rearrange("p (g d) -> p g d", d=64), [t_sq], [tsm])
                            ss3 = smb[:, 16:19]
                            P.red(ss3[:, 0:1], ss11[:, 0:6], [tsm], [tsm])
                            P.red(ss3[:, 1:2], ss11[:, 6:10], [tsm], [tsm])
                            P.ts("dve", ss3[:, 0:1], ss3[:, 0:1], 1.0 / 384, ALU.mult, [tsm], [tsm])
                            P.ts("dve", ss3[:, 1:2], ss3[:, 1:2], 1.0 / 256, ALU.mult, [tsm], [tsm])
                            P.ts("dve", ss3[:, 2:3], ss11[:, 10:11], 1.0 / 64, ALU.mult, [tsm], [tsm])
                            rstd_from_ss(ss3, 3, tsm, 1.0)
                            P.stt(qn[:, 0:384], proj[:, 0:384], ss3[:, 0:1], gseg["qlat"], ALU.mult, ALU.mult, [t_proj, tsm, t_gA], [t_qn])
                            P.stt(qn[:, 384:640], proj[:, 384:640], ss3[:, 1:2], gseg["kvlat"], ALU.mult, ALU.mult, [t_proj, tsm, t_gA], [t_qn])
                            P.stt(nrm[:, 0:64], proj[:, 640:704], ss3[:, 2:3], gseg["kpe"], ALU.mult, ALU.mult, [t_proj, tsm, t_gA], [t_nrm])
                            rope(nrm[:, 0:64].rearrange("p (g d) -> p g d", g=1), rot[:, 0:64].rearrange("p (g d) -> p g d", g=1), 1, cs,
                                 t_nrm, t_rot, t_rope, rtmp, t_rtmp)
                            Tb = bankbf(7)
                            for k in range(5):
                                P.tr(Tb[:, k * 128:(k + 1) * 128], qn[:, k * 128:(k + 1) * 128], identb, [t_qn, t_cb], [tb[7]])
                            P.cp("act", qnT, Tb[:, 0:640], [tb[7]], [t_qnT])
                            for cc in range(2):
                                for k in range(3):
                                    P.mm(banks[cc][:, 0:384], qnT[:, k * 128:(k + 1) * 128], wq_b[:, k * 768 + cc * 384:k * 768 + (cc + 1) * 384],
                                         k == 0, k == 2, [t_qnT, t_w], [tb[cc]])
                            for cc in range(2):
                                for k in range(2):
                                    P.mm(banks[2 + cc][:, :], qnT[:, (3 + k) * 128:(4 + k) * 128], wkv_b[:, k * 1024 + cc * 512:k * 1024 + (cc + 1) * 512],
                                         k == 0, k == 1, [t_qnT, t_w], [tb[2 + cc]])
                            for cc in range(2):
                                P.act(sq[:, cc * 384:(cc + 1) * 384], banks[cc][:, 0:384], AF.Square, [tb[cc]], [t_sq])
                            ss12 = smb[:, 20:32]
                            P.red(ss12, sq[:, 0:768].rearrange("p (g d) -> p g d", d=64), [t_sq], [tsm])
                            ss12v = ss12.rearrange("p (h t) -> p h t", t=3)
                            r12 = smb[:, 32:44]
                            r12v = r12.rearrange("p (h t) -> p h t", t=3)
                            P.tt("dve", r12v[:, :, 0], ss12v[:, :, 0], ss12v[:, :, 1], ALU.add, [tsm], [tsm])
                            P.ts("dve", r12v[:, :, 0], r12v[:, :, 0], 1.0 / 128, ALU.mult, [tsm], [tsm])
                            P.cp("dve", r12v[:, :, 1], r12v[:, :, 0], [tsm], [tsm])
                            P.ts("dve", r12v[:, :, 2], ss12v[:, :, 2], 1.0 / 64, ALU.mult, [tsm], [tsm])
                            rstd_from_ss(r12, 12, tsm, 1.0)
                            for cc in range(2):
                                P.tt("dve", qf[:, cc * 384:(cc + 1) * 384].rearrange("p (g d) -> p g d", d=64),
                                     banks[cc][:, 0:384].rearrange("p (g d) -> p g d", d=64),
                                     r12[:, cc * 6:(cc + 1) * 6].unsqueeze(2).to_broadcast([128, 6, 64]), ALU.mult, [tb[cc], tsm], [t_qf])
                            qfv = qf.rearrange("p (h d) -> p h d", h=4)
                            gq = gseg["q12"].rearrange("p (h d) -> p h d", h=4)
                            qmv = qm.rearrange("p (h d) -> p h d", h=4)
                            P.tt("dve", qmv[:, :, 0:128], qfv[:, :, 0:128], gq[:, :, 0:128], ALU.mult, [t_qf, t_gA], [t_qm])
                            qpev = qpe.rearrange("p (h d) -> p h d", h=4)
                            P.tt("pool", qpev, qfv[:, :, 128:192], gq[:, :, 128:192], ALU.mult, [t_qf, t_gA], [t_qpe])
                            rope(qpev, qmv[:, :, 128:192], 4, cs, t_qpe, t_qm, t_rope, rtmp, t_rtmp)
                            for cc in range(2):
                                kvv = banks[2 + cc][:, :].rearrange("p (h d) -> p h d", h=2)
                                P.act(sq[:, cc * 256:(cc + 1) * 256].rearrange("p (h d) -> p h d", h=2), kvv[:, :, 0:128], AF.Square, [tb[2 + cc]], [t_sq])
                                P.cp("act", Vv_v[:, tl, 2 * cc:2 * cc + 2, 0:128], kvv[:, :, 128:256], [tb[2 + cc], t_vinit], [t_kv[tl]])
                            ssk = smb[:, 44:48]
                            P.red(ssk, sq[:, 0:512].rearrange("p (h d) -> p h d", h=4), [t_sq], [tsm])
                            rstd_from_ss(ssk, 4, tsm, 1.0 / 128)
                            for cc in range(2):
                                kvv = banks[2 + cc][:, :].rearrange("p (h d) -> p h d", h=2)
                                P.tt("dve", kf[:, cc * 256:(cc + 1) * 256].rearrange("p (h d) -> p h d", h=2), kvv[:, :, 0:128],
                                     ssk[:, 2 * cc:2 * cc + 2].unsqueeze(2).to_broadcast([128, 2, 128]), ALU.mult, [tb[2 + cc], tsm], [t_kf])
                            P.tt("pool", kn, kf, gseg["k4"], ALU.mult, [t_kf, t_gA], [t_kn])
                            Tb = bankbf(6)
                            for h in range(4):
                                P.tr(Tb[:, h * 128:(h + 1) * 128], qmv[:, h, 0:128], identb, [t_qm, t_cb], [tb[6]])
                                P.tr(Tb[:, 512 + h * 128:512 + (h + 1) * 128], kn[:, h * 128:(h + 1) * 128], identb, [t_kn, t_cb], [tb[6]])
                            P.cp("act", qTn_v[:, :, j * 128:(j + 1) * 128], Tb[:, 0:512].rearrange("p (h t) -> p h t", h=4), [tb[6]], [t_qT])
                            P.cp("dve", kTn_v[:, :, tl * 128:(tl + 1) * 128], Tb[:, 512:1024].rearrange("p (h t) -> p h t", h=4), [tb[6]], [t_kv[tl]])
                            Tb = bankbf(7)
                            for h in range(4):
                                P.tr(Tb[0:64, h * 128:(h + 1) * 128], qmv[:, h, 128:192], identb, [t_qm, t_cb], [tb[7]])
                            P.tr(Tb[0:64, 512:640], rot[:, 0:64], identb, [t_rot, t_cb], [tb[7]])
                            P.cp("act", qTp_v[0:64, :, j * 128:(j + 1) * 128], Tb[0:64, 0:512].rearrange("p (h t) -> p h t", h=4), [tb[7]], [t_qT])
                            P.cp("dve", kTp[0:64, tl * 128:(tl + 1) * 128], Tb[0:64, 512:640], [tb[7]], [t_kv[tl]])
                        else:
                            P.act(sq[:, 0:1024], proj[:, 0:1024], AF.Square, [t_proj], [t_sq])
                            ss16 = smb[:, 4:20]
                            P.red(ss16, sq[:, 0:1024].rearrange("p (g d) -> p g d", d=64), [t_sq], [tsm])
                            rstd_from_ss(ss16, 16, tsm, 1.0 / 64)
                            P.tt("dve", tmpf.rearrange("p (g d) -> p g d", d=64), proj[:, 0:1024].rearrange("p (g d) -> p g d", d=64),
                                 ss16.unsqueeze(2).to_broadcast([128, 16, 64]), ALU.mult, [t_proj, tsm], [t_tmpf])
                            P.tt("pool", nrm, tmpf, gseg["dqk"], ALU.mult, [t_tmpf, t_gA], [t_nrm])
                            rope(nrm.rearrange("p (g d) -> p g d", d=64), rot.rearrange("p (g d) -> p g d", d=64), 16, cs,
                                 t_nrm, t_rot, t_rope, rtmp, t_rtmp)
                            P.cp("act", Vv_v[:, tl, :, 0:128], proj[:, 1024:1536].rearrange("p (h d) -> p h d", h=4), [t_proj, t_vinit], [t_kv[tl]])
                            Tb = bankbf(6)
                            for h in range(8):
                                P.tr(Tb[:, h * 128:(h + 1) * 128], rot[:, h * 128:(h + 1) * 128], identb, [t_rot, t_cb], [tb[6]])
                            P.cp("act", qTn_v[:, :, j * 128:(j + 1) * 128], Tb[:, 0:512].rearrange("p (h t) -> p h t", h=4), [tb[6]], [t_qT])
                            P.cp("dve", kTn_v[:, :, tl * 128:(tl + 1) * 128], Tb[:, 512:1024].rearrange("p (h t) -> p h t", h=4), [tb[6]], [t_kv[tl]])

                    nkb = 4 * c + 4
                    if stage_lim < 2:
                        continue
                    for h in range(4):
                        nmap = 1 if a1 else 2
                        for m in range(nmap):
                            ob = (2, 3) if m == 0 else (4, 5)
                            for kb in range(nkb):
                                jd = kb - 4 * c
                                q0 = 0 if jd < 0 else jd * 128
                                sb = kb % 2
                                if a1:
                                    P.mm(banks[sb][:, q0:512], kTn_v[:, h, kb * 128:(kb + 1) * 128], qTn_v[:, h, q0:512], True, False,
                                         [t_kv[kb], t_qT], [tb[sb]])
                                    P.mm(banks[sb][:, q0:512], kTp[0:64, kb * 128:(kb + 1) * 128], qTp_v[0:64, h, q0:512], False, True,
                                         [t_kv[kb], t_qT], [tb[sb]])
                                    scale = 192 ** -0.5
                                else:
                                    P.mm(banks[sb][:, q0:512], kTn_v[m * 64:(m + 1) * 64, h, kb * 128:(kb + 1) * 128],
                                         qTn_v[m * 64:(m + 1) * 64, h, q0:512], True, True, [t_kv[kb], t_qT], [tb[sb]])
                                    scale = 64 ** -0.5
                                pb = PT[pti % 3]; tpb = t_PT[pti % 3]; pti += 1
                                P.act(pb[:, q0:512], banks[sb][:, q0:512], AF.Exp, [tb[sb]], [tpb], scale=scale)
                                if jd >= 0:
                                    P.tt("pool", pb[:, q0:q0 + 128], pb[:, q0:q0 + 128], trib, ALU.mult, [tpb, t_cb], [tpb])
                                for qb in range(max(jd, 0), 4):
                                    bk = ob[qb // 2]
                                    ov = banks[bk][:, 0:260].rearrange("p (a d) -> p a d", a=2)
                                    P.mm(ov[:, qb % 2, 0:129], pb[:, qb * 128:(qb + 1) * 128], Vv_v[:, kb, h, 0:129],
                                         kb == 0 and qb % 2 == 0, kb == 4 * c + qb, [tpb, t_kv[kb]], [tb[bk]])
                        for qb in range(4):
                            if a1:
                                bk = (2, 3)[qb // 2]
                                ov = banks[bk][:, 0:260].rearrange("p (a d) -> p a d", a=2)
                                P.recip(osm[:, 0:1], ov[:, qb % 2, 128:129], [tb[bk]], [t_osm])
                                P.ts("dve", mixed_v[:, qb, h * 128:(h + 1) * 128], ov[:, qb % 2, 0:128], osm[:, 0:1], ALU.mult, [tb[bk], t_osm], [t_mixed[qb]])
                            else:
                                b1 = (2, 3)[qb // 2]
                                b2 = (4, 5)[qb // 2]
                                o1 = banks[b1][:, 0:260].rearrange("p (a d) -> p a d", a=2)
                                o2 = banks[b2][:, 0:260].rearrange("p (a d) -> p a d", a=2)
                                P.recip(osm[:, 0:1], o1[:, qb % 2, 128:129], [tb[b1]], [t_osm])
                                P.recip(osm[:, 1:2], o2[:, qb % 2, 128:129], [tb[b2]], [t_osm])
                                P.ts("dve", oa, o2[:, qb % 2, 0:128], osm[:, 1:2], ALU.mult, [tb[b2], t_osm, t_lam], [t_oa], s2=neglam[:, 0:1], op1=ALU.mult)
                                P.stt(od, o1[:, qb % 2, 0:128], osm[:, 0:1], oa, ALU.mult, ALU.add, [tb[b1], t_osm, t_oa], [t_od])
                                P.act(oa, od, AF.Square, [t_od], [t_oa, t_osm], accum=osm[:, 2:3])
                                rstd_from_ss(osm[:, 2:3], 1, t_osm, 1.0 / 128)
                                P.stt(mixed_v[:, qb, h * 128:(h + 1) * 128], od, osm[:, 2:3], gsub, ALU.mult, ALU.mult, [t_od, t_osm, t_lam], [t_mixed[qb]])
                    if stage_lim < 3:
                        continue
                    for qb in range(4):
                        tl = c * 4 + qb
                        row0 = s * S + tl * 128
                        gt = row0 // 128
                        Tb = bankbf(6)
                        for k in range(4):
                            P.tr(Tb[:, k * 128:(k + 1) * 128], mixed_v[:, qb, k * 128:(k + 1) * 128], identb, [t_mixed[qb], t_cb], [tb[6]])
                        P.cp("act", mT, Tb[:, 0:512], [tb[6]], [t_mT])
                        xrb = xr[qb % 2]; txr = t_xr[qb % 2]
                        if a1:
                            P.ld("sp", xrb, x_d[row0:row0 + 128, :], txr)
                        else:
                            P.ld("sp", xrb, x1_d[row0:row0 + 128, :], txr, reads=[tx1[gt]])
                        for cc in range(2):
                            bk = 7 if cc == 0 else 6
                            bk = (7, 1)[cc]
                            for k in range(4):
                                P.mm(banks[bk][:, :], mT[:, k * 128:(k + 1) * 128],
                                     wout_b[:, k * 1024 + cc * 512:k * 1024 + (cc + 1) * 512],
                                     k == 0, k == 3, [t_mT, t_w], [tb[bk]])
                            P.tt("dve", ytmp[:, cc * 512:(cc + 1) * 512], banks[bk][:, :], GATE1[:, cc * 512:(cc + 1) * 512], ALU.mult, [tb[bk], t_mod], [t_ytmp])
                        P.tt("pool", xrb, xrb, ytmp, ALU.add, [txr, t_ytmp], [txr])
                        P.ld("sp", x1_d[row0:row0 + 128, :], xrb, txr, reads=[txr], writes=[tx1[gt]])

    if do_b:
        A.off = base_off
        P.barrier()
        gB = A.f32(1024)
        t_gB = P.tok("gB")
        P.ld("sp", gB, gains_d[:, G_N2:G_N2 + 1024], t_gB)
        pwq_b = A.bf16(8 * 2048)
        keys_b = A.bf16(16 * 128)
        t_wB = P.tok("wB")
        wst = [A.f32(4096), A.f32(4096)]
        t_wst = P.toks("wstB", 2)
        stage = wst
        t_stage = t_wst
        ei = 0
        for k in range(8):
            load_cast_weight(pwq_b[:, k * 2048:(k + 1) * 2048], pwq_d[k * 128:(k + 1) * 128, :], 2048, stage, t_stage, t_wB, k, ei); ei += 1
        load_cast_weight(keys_b, keys_d.rearrange("p c n -> p (c n)"), 2048, stage, t_stage, t_wB, 0, ei); ei += 1
        G2 = A.f32(1024)
        SH2 = A.f32(1024)
        GATE2 = A.f32(1024)
        t_mod2 = P.tok("mod2")
        mtmp = [A.f32(512), A.f32(512)]
        t_mtmp = P.toks("mtmpB", 2)
        xt = [A.f32(1024), A.f32(1024)]
        t_xt = P.toks("xB", 2)
        junk = A.bf16(1024)
        t_junk = P.tok("junkB")
        sm = A.f32(64)
        t_sm = P.tok("smB")
        h2f = A.f32(1024)
        t_h2f = P.tok("h2f")
        h2b = A.bf16(1024)
        t_h2b = P.tok("h2b")
        h2T = A.bf16(1024)
        t_h2T = P.tok("h2T")
        qT = A.bf16(16 * 128)
        t_qTB = P.tok("qTB")
        sc = A.f32(2048)
        t_sc = P.tok("sc")
        sc2 = A.f32(256)
        t_sc2 = P.tok("sc2")
        sv = A.f32(256)
        t_sv = P.tok("sv")
        si = A.u32(256)
        t_si = P.tok("si")
        sif = A.f32(256)
        t_sif = P.tok("sif")
        cand = A.f32(2048)
        t_cand = P.tok("cand")
        cand2 = A.f32(256)
        t_cand2 = P.tok("cand2")
        fv = A.f32(128)
        t_fv = P.tok("fv")
        fpos = A.u32(128)
        t_fpos = P.tok("fpos")
        k0f = A.f32(128)
        k1f = A.f32(128)
        t_k = P.tok("k01")
        oh = A.f32(2048)
        t_oh = P.tok("oh")
        e0 = A.f32(128)
        e1 = A.f32(128)
        t_e = P.tok("e01")
        eidx = A.u32(128)
        t_eidx = P.tok("eidx")
        gw = A.f32(128)
        t_gw = P.tok("gw")
        araw = A.f32(128)
        t_araw = P.tok("araw")
        coef = A.f32(128)
        t_coef = P.tok("coef")
        NG = 4
        ug = [A.f32(1024) for _ in range(NG)]
        t_ug = P.toks("ug", NG)
        vg = [A.f32(1024) for _ in range(NG)]
        t_vg = P.toks("vg", NG)
        dg = [A.f32(128) for _ in range(4)]
        t_dg = P.toks("dg", 4)
        ujunk = A.bf16(1024)
        t_ujunk = P.tok("ujunk")
        yout = A.f32(1024)
        t_yout = P.tok("yout")
        gi = 0
        di = 0
        for s in range(NSEQ):
            compute_mod(s, [6, 7, 8, 9, 10, 11],
                        [SH2[:, 0:512], SH2[:, 512:1024], mtmp[0], mtmp[1], GATE2[:, 0:512], GATE2[:, 512:1024]],
                        [t_mod2, t_mod2, t_mtmp[0], t_mtmp[1], t_mod2, t_mod2], wst, t_wst, [0, 1])
            for i in range(2):
                P.stt(G2[:, i * 512:(i + 1) * 512], mtmp[i], 1.0, gB[:, i * 512:(i + 1) * 512], ALU.add, ALU.mult, [t_mtmp[i], t_gB], [t_mod2])
            for tl in range(NT):
                row0 = s * S + tl * 128
                gt = row0 // 128
                xb = xt[gt % 2]; txb = t_xt[gt % 2]
                src = x1_d if (do_a1 or do_a2) else x_d
                P.ld("sp", xb, src[row0:row0 + 128, :], txb, reads=[tx1[gt]])
                P.act(junk, xb, AF.Square, [txb], [t_junk, t_sm], accum=sm[:, 0:1])
                rstd_from_ss(sm[:, 0:1], 1, t_sm, 1.0 / 1024)
                P.stt(h2f, xb, sm[:, 0:1], G2, ALU.mult, ALU.mult, [txb, t_sm, t_mod2], [t_h2f])
                P.tt("pool", h2f, h2f, SH2, ALU.add, [t_h2f, t_mod2], [t_h2f])
                P.cp("act", h2b, h2f, [t_h2f], [t_h2b])
                Tb = bankbf(6)
                for k in range(8):
                    P.tr(Tb[:, k * 128:(k + 1) * 128], h2b[:, k * 128:(k + 1) * 128], identb, [t_h2b, t_cb], [tb[6]])
                P.cp("act", h2T, Tb, [tb[6]], [t_h2T])
                for g4 in range(4):
                    for cc in range(4):
                        c16 = g4 * 4 + cc
                        for k in range(8):
                            P.mm(banks[g4][:, cc * 128:(cc + 1) * 128], pwq_b[:, k * 2048 + c16 * 128:k * 2048 + (c16 + 1) * 128],
                                 h2T[:, k * 128:(k + 1) * 128], k == 0, k == 7, [t_wB, t_h2T], [tb[g4]])
                    P.cp("act" if g4 % 2 == 0 else "dve", qT[:, g4 * 512:(g4 + 1) * 512], banks[g4][:, :], [tb[g4]], [t_qTB])
                for g4 in range(4):
                    bk = (4, 5, 7, 0)[g4]
                    for cc in range(4):
                        c16 = g4 * 4 + cc
                        P.mm(banks[bk][:, cc * 128:(cc + 1) * 128], qT[:, c16 * 128:(c16 + 1) * 128], keys_b[:, c16 * 128:(c16 + 1) * 128],
                             True, True, [t_qTB, t_wB], [tb[bk]])
                    P.cp("act" if g4 % 2 == 0 else "dve", sc[:, g4 * 512:(g4 + 1) * 512], banks[bk][:, :], [tb[bk]], [t_sc])
                svv = sv.rearrange("p (c k) -> p c k", c=16)
                siv = si.rearrange("p (c k) -> p c k", c=16)
                for c16 in range(16):
                    scc = sc[:, c16 * 128:(c16 + 1) * 128]
                    P.op("dve", (lambda o, i_: (lambda e: e.max(out=o, in_=i_)))(svv[:, c16, 0:8], scc), [t_sc], [t_sv])
                    P.op("dve", (lambda o, r, i_: (lambda e: e.match_replace(out=o, in_to_replace=r, in_values=i_, imm_value=-1e30)))(sc2[:, 0:128], svv[:, c16, 0:8], scc), [t_sc, t_sv], [t_sc2])
                    P.op("dve", (lambda o, i_: (lambda e: e.max(out=o, in_=i_)))(svv[:, c16, 8:16], sc2[:, 0:128]), [t_sc2], [t_sv])
                    P.op("dve", (lambda o, m_, i_: (lambda e: e.max_index(out=o, in_max=m_, in_values=i_)))(siv[:, c16, 0:8], svv[:, c16, 0:8], scc), [t_sc, t_sv], [t_si])
                    P.op("dve", (lambda o, m_, i_: (lambda e: e.max_index(out=o, in_max=m_, in_values=i_)))(siv[:, c16, 8:16], svv[:, c16, 8:16], scc), [t_sc, t_sv], [t_si])
                P.cp("dve", sif, si, [t_si], [t_sif])
                sv4 = sv.rearrange("p (h t k) -> p h t k", h=8, t=2)
                candv = cand.rearrange("p (h a b) -> p h a b", h=8, a=16)
                P.tt("dve", candv, sv4[:, :, 0, :].unsqueeze(3).to_broadcast([128, 8, 16, 16]),
                     sv4[:, :, 1, :].unsqueeze(2).to_broadcast([128, 8, 16, 16]), ALU.add, [t_sv], [t_cand])
                fvv = fv.rearrange("p (h k) -> p h k", h=8)
                fpv = fpos.rearrange("p (h k) -> p h k", h=8)
                for h in range(8):
                    ch_ = cand[:, h * 256:(h + 1) * 256]
                    P.op("dve", (lambda o, i_: (lambda e: e.max(out=o, in_=i_)))(fvv[:, h, 0:8], ch_), [t_cand], [t_fv])
                    P.op("dve", (lambda o, r, i_: (lambda e: e.match_replace(out=o, in_to_replace=r, in_values=i_, imm_value=-1e30)))(cand2, fvv[:, h, 0:8], ch_), [t_cand, t_fv], [t_cand2])
                    P.op("dve", (lambda o, i_: (lambda e: e.max(out=o, in_=i_)))(fvv[:, h, 8:16], cand2), [t_cand2], [t_fv])
                    P.op("dve", (lambda o, m_, i_: (lambda e: e.max_index(out=o, in_max=m_, in_values=i_)))(fpv[:, h, 0:8], fvv[:, h, 0:8], ch_), [t_cand, t_fv], [t_fpos])
                    P.op("dve", (lambda o, m_, i_: (lambda e: e.max_index(out=o, in_max=m_, in_values=i_)))(fpv[:, h, 8:16], fvv[:, h, 8:16], ch_), [t_cand, t_fv], [t_fpos])
                P.cp("dve", k1f, fpos, [t_fpos], [t_k])
                P.tt("dve", oh.rearrange("p (a b) -> p a b", b=16), k1f.unsqueeze(2).to_broadcast([128, 128, 16]),
                     thr16.unsqueeze(1).to_broadcast([128, 128, 16]), ALU.is_ge, [t_k, t_consts], [t_oh])
                P.red(k0f, oh.rearrange("p (a b) -> p a b", b=16), [t_oh], [t_k])
                P.stt(k1f, k0f, -16.0, k1f, ALU.mult, ALU.add, [t_k], [t_k])
                sif4 = sif.rearrange("p (h t k) -> p h t k", h=8, t=2)
                for t_, (kf_, e_) in enumerate(((k0f, e0), (k1f, e1))):
                    P.tt("dve", oh.rearrange("p (a b) -> p a b", b=16), kf_.unsqueeze(2).to_broadcast([128, 128, 16]),
                         iota16.unsqueeze(1).to_broadcast([128, 128, 16]), ALU.is_equal, [t_k, t_consts], [t_oh])
                    P.tt("dve", oh.rearrange("p (h a b) -> p h a b", h=8, a=16), oh.rearrange("p (h a b) -> p h a b", h=8, a=16),
                         sif4[:, :, t_, :].unsqueeze(2).to_broadcast([128, 8, 16, 16]), ALU.mult, [t_oh, t_sif], [t_oh])
                    P.red(e_, oh.rearrange("p (a b) -> p a b", b=16), [t_oh], [t_e])
                P.stt(e0, e0, 128.0, e1, ALU.mult, ALU.add, [t_e], [t_e])
                P.cp("dve", eidx, e0, [t_e], [t_eidx])
                P.tt("dve", gw.rearrange("p (h k) -> p h k", h=8), fvv, fvv[:, :, 0:1].to_broadcast([128, 8, 16]), ALU.subtract, [t_fv], [t_gw])
                P.act(gw, gw, AF.Exp, [t_gw], [t_gw])
                P.red(sm[:, 8:16], gw.rearrange("p (h k) -> p h k", h=8), [t_gw], [t_sm])
                P.recip(sm[:, 8:16], sm[:, 8:16], [t_sm], [t_sm])
                P.tt("dve", gw.rearrange("p (h k) -> p h k", h=8), gw.rearrange("p (h k) -> p h k", h=8),
                     sm[:, 8:16].unsqueeze(2).to_broadcast([128, 8, 16]), ALU.mult, [t_gw, t_sm], [t_gw])
                for sl in range(128):
                    ub = ug[gi % NG]; tub = t_ug[gi % NG]; gi += 1
                    P.dma("pool", (lambda o, ix: (lambda e: e.indirect_dma_start(out=o, out_offset=None, in_=u_d[:, :],
                          in_offset=bass.IndirectOffsetOnAxis(ap=ix, axis=0))))(ub, eidx[:, sl:sl + 1]), tub, reads=[t_eidx], writes=[tub])
                    P.stt(ujunk, ub, 1.0, h2f, ALU.mult, ALU.mult, [tub, t_h2f], [t_ujunk, t_araw], accum=araw[:, sl:sl + 1])
                P.act(coef, araw, AF.Gelu, [t_araw], [t_coef])
                P.tt("dve", coef, coef, gw, ALU.mult, [t_coef, t_gw], [t_coef])
                for sl in range(128):
                    vb = vg[gi % NG]; tvb = t_vg[gi % NG]; gi += 1
                    P.dma("pool", (lambda o, ix: (lambda e: e.indirect_dma_start(out=o, out_offset=None, in_=v_d[:, :],
                          in_offset=bass.IndirectOffsetOnAxis(ap=ix, axis=0))))(vb, eidx[:, sl:sl + 1]), tvb, reads=[t_eidx], writes=[tvb])
                    db = dg[di % 4]; tdb = t_dg[di % 4]; di += 1
                    P.ts("dve", db, ident_f, coef[:, sl:sl + 1], ALU.mult, [t_consts, t_coef], [tdb])
                    for cc in range(2):
                        bk = (2, 3)[cc]
                        P.mm(banks[bk][:, :], db, vb[:, cc * 512:(cc + 1) * 512], sl == 0, sl == 127, [tdb, tvb], [tb[bk]])
                for cc in range(2):
                    bk = (2, 3)[cc]
                    P.tt("dve", yout[:, cc * 512:(cc + 1) * 512], banks[bk][:, :], GATE2[:, cc * 512:(cc + 1) * 512], ALU.mult, [tb[bk], t_mod2], [t_yout])
                P.tt("pool", yout, yout, xb, ALU.add, [t_yout, txb], [t_yout])
                P.ld("sp", out_d[row0:row0 + 128, :], yout, t_yout, reads=[t_yout], writes=[], final=True)
    else:
        A.off = base_off
        P.barrier()
        cb = [A.f32(1024), A.f32(1024)]
        t_cbuf = P.toks("cpb", 2)
        for gt in range(TT // 128):
            P.ld("sp", cb[gt % 2], x1_d[gt * 128:(gt + 1) * 128, :], t_cbuf[gt % 2], reads=[tx1[gt]])
            P.ld("sp", out_d[gt * 128:(gt + 1) * 128, :], cb[gt % 2], t_cbuf[gt % 2], reads=[t_cbuf[gt % 2]], writes=[], final=True)

    P.arena_hw = A.hw
    P.emit()
    return nc, P


def rope_table(S):
    half = 32
    inv = (1.0 / (10000.0 ** (np.arange(half, dtype=np.float32) / np.float32(half)))).astype(np.float32)
    ang = np.arange(S, dtype=np.float32)[:, None] * inv[None, :]
    cs = np.concatenate([np.cos(ang), np.sin(ang)], axis=-1).astype(np.float32)
    return np.ascontiguousarray(cs.reshape(S // 128, 128, 64).transpose(1, 0, 2))


def make_consts():
    c = np.zeros((128, C_TOT), np.float32)
    c[:, C_ID:C_ID + 128] = np.eye(128, dtype=np.float32)
    k = np.arange(128)
    c[:, C_TRI:C_TRI + 128] = (k[:, None] <= k[None, :]).astype(np.float32)
    c[:, C_IOTA:C_IOTA + 16] = np.arange(16, dtype=np.float32)[None, :]
    c[:, C_ONES:C_ONES + 128] = 1.0
    c[:, C_THR:C_THR + 15] = 16.0 * np.arange(1, 16, dtype=np.float32)[None, :]
    c[:, C_THR + 15] = 1e9
    return c


def host_layout(inp, S, NSEQ, ncores):
    f = lambda a: np.ascontiguousarray(np.asarray(a, dtype=np.float32))
    x = f(inp["x"])
    c = f(inp["c"])
    rep = lambda v: np.broadcast_to(f(v).reshape(1, -1), (128, f(v).size))
    mqg = f(inp["mla_q_g"])[0]
    mkg = f(inp["mla_k_g"])[0]
    gains = np.concatenate([
        rep(inp["norm1_g"][0]), rep(inp["mla_q_lat_g"][0]), rep(inp["mla_kv_lat_g"][0]),
        rep(mkg[128:192]),
        rep(np.tile(f(inp["diff_q_g"])[0], 8)), rep(np.tile(f(inp["diff_k_g"])[0], 8)),
        rep(np.tile(mqg, 4)), rep(np.tile(mkg[:128], 4)),
        rep(inp["diff_subln_g"][0]), rep(inp["norm2_g"][0]),
        rep(inp["diff_lq1"][0]), rep(inp["diff_lk1"][0]), rep(inp["diff_lq2"][0]), rep(inp["diff_lk2"][0]),
    ], axis=1)
    gains = np.ascontiguousarray(gains, dtype=np.float32)
    assert gains.shape[1] == G_TOT
    keysT = np.ascontiguousarray(f(inp["peer_sub_keys"])[0].reshape(16, 128, 128).transpose(2, 0, 1))
    shared = {
        "ada_w": f(inp["ada_w"])[0], "ada_b": f(inp["ada_b"])[0].reshape(1, -1),
        "w_in": f(inp["w_in"])[0], "w_q_up": f(inp["mla_w_q_up"])[0], "w_kv_up": f(inp["mla_w_kv_up"])[0],
        "w_out": f(inp["w_out"])[0], "peer_w_q": f(inp["peer_w_q"])[0], "keysT": keysT,
        "peer_u": f(inp["peer_u"])[0], "peer_v": f(inp["peer_v"])[0],
        "gains": gains, "rope": rope_table(S), "consts": make_consts(),
    }
    maps = []
    for i in range(ncores):
        xs = np.ascontiguousarray(x[i * NSEQ:(i + 1) * NSEQ].reshape(NSEQ * S, D))
        cs = c[i * NSEQ:(i + 1) * NSEQ]
        cT = np.ascontiguousarray(cs.reshape(NSEQ, 8, 128).transpose(2, 0, 1))
        m = dict(shared)
        m["x"] = xs
        m["cT"] = cT
        maps.append(m)
    return maps


_CACHE = {}


def kernel(**inputs):
    B, S, _ = inputs["x"].shape
    ncores = 8
    NSEQ = B // ncores
    key = (S, NSEQ)
    if key not in _CACHE:
        _CACHE[key] = build(S, NSEQ)[0]
    nc = _CACHE[key]
    maps = host_layout(inputs, S, NSEQ, ncores)
    res = run_bass_kernel_spmd(nc, maps, core_ids=list(range(ncores)))
    out = np.stack([r["out"].reshape(NSEQ, S, D) for r in res.results], axis=0).reshape(B, S, D)
    return out.astype(np.float32)
```

```python
import math
import numpy as np
from contextlib import ExitStack
import concourse.bass as bass
import concourse.mybir as mybir
from concourse.bass_utils import run_bass_kernel_spmd

F32 = mybir.dt.float32
BF16 = mybir.dt.bfloat16
U32 = mybir.dt.uint32
I32 = mybir.dt.int32
ALU = mybir.AluOpType
AF = mybir.ActivationFunctionType
AX = mybir.AxisListType

ENGS = ("pe", "act", "dve", "pool", "sp")
CHUNK = 16000
EPS = 1e-6
D = 1024
NEXP = 16384


class Tok:
    __slots__ = ("name", "w", "r", "dsem", "dcount", "excl")

    def __init__(self, name):
        self.name = name
        self.excl = False
        self.w = None
        self.r = {}
        self.dsem = None
        self.dcount = 0


class Prog:
    def __init__(self, nc):
        self.nc = nc
        self.es = ExitStack()
        self.ops = {e: [] for e in ENGS}
        self.cnt = {e: 0 for e in ENGS}
        self.esems = {e: [] for e in ENGS}
        self.seen = {e: {} for e in ENGS}
        self.semobj = {}
        self.nsem = 0
        self.final_events = []
        self.dma_toks = []
        self.nins = 0

    def sbuf(self, name, shape, dt):
        return self.es.enter_context(self.nc.sbuf_tensor(name, list(shape), dt))

    def psum(self, name, shape, dt):
        return self.es.enter_context(self.nc.psum_tensor(name, list(shape), dt))

    def newsem(self, name):
        self.nsem += 1
        return self.es.enter_context(self.nc.semaphore(name))

    def tok(self, name):
        return Tok(name)

    def toks(self, name, n):
        return [Tok(f"{name}{i}") for i in range(n)]

    def _eng_event(self, eng):
        c = self.cnt[eng]
        ch, v = divmod(c, CHUNK)
        while len(self.esems[eng]) <= ch:
            s = self.newsem(f"e_{eng}_{len(self.esems[eng])}")
            self.esems[eng].append(s)
            self.semobj[(eng, len(self.esems[eng]) - 1)] = s
        self.cnt[eng] = c + 1
        return ((eng, ch), v + 1)

    def _collect(self, eng, reads, writes, same_eng_sync):
        deps = {}

        def add(ev):
            if ev is None:
                return
            k, v = ev
            if deps.get(k, 0) < v:
                deps[k] = v

        for b in reads:
            add(b.w)
        for b in writes:
            add(b.w)
            for k, v in b.r.items():
                add((k, v))
        waits = []
        seen = self.seen[eng]
        for k, v in deps.items():
            if (not same_eng_sync) and k[0] == eng:
                continue
            if seen.get(k, 0) >= v:
                continue
            seen[k] = v
            waits.append((k, v))
        return waits

    def _record(self, ev, reads, writes):
        k, v = ev
        for b in reads:
            if b.r.get(k, 0) < v:
                b.r[k] = v
        for b in writes:
            b.w = ev
            b.r = {}

    def op(self, eng, fn, reads=(), writes=(), sync_same=None):
        if sync_same is None:
            sync_same = eng != "pe"
        ex = [b for b in reads if b.excl]
        if ex:
            writes = list(writes) + ex
        waits = self._collect(eng, reads, writes, sync_same)
        ev = self._eng_event(eng)
        self._record(ev, reads, writes)
        self.ops[eng].append((waits, fn, ev, 1))
        self.nins += 1

    def dma(self, eng, fn, owner, reads=(), writes=(), final=False):
        waits = self._collect(eng, reads, writes, False)
        if owner.dsem is None:
            owner.dsem = ("dma", self.nsem)
            self.semobj[owner.dsem] = self.newsem(f"d{self.nsem}")
            self.dma_toks.append(owner)
        owner.dcount += 16
        ev = (owner.dsem, owner.dcount)
        self._record(ev, reads, writes)
        self.ops[eng].append((waits, fn, ev, 16))
        if final:
            self.final_events.append(ev)
        self.nins += 1

    def barrier(self):
        evs = {}
        for e in ENGS:
            c = self.cnt[e]
            if c == 0:
                continue
            ch, v = divmod(c - 1, CHUNK)
            evs[(e, ch)] = v + 1
        for t in self.dma_toks:
            evs[t.dsem] = t.dcount
        for e in ENGS:
            waits = []
            for k, v in evs.items():
                if k[0] == e:
                    continue
                if self.seen[e].get(k, 0) >= v:
                    continue
                self.seen[e][k] = v
                waits.append((k, v))
            self.ops[e].append((waits, None, None, 0))

    def emit(self):
        nc = self.nc
        fw = {}
        for k, v in self.final_events:
            fw[k] = max(fw.get(k, 0), v)
        self.ops["sp"].append((list(fw.items()), None, None, 0))
        semobj = self.semobj

        def run(engobj, lst):
            for waits, fn, ev, inc in lst:
                for k, v in waits:
                    engobj.wait_ge(semobj[k], v)
                if fn is None:
                    continue
                ins = fn(engobj)
                ins.then_inc(semobj[ev[0]], inc)

        with nc.Block() as block:
            @block.tensor
            def _(e):
                run(e, self.ops["pe"])

            @block.scalar
            def _(e):
                run(e, self.ops["act"])

            @block.vector
            def _(e):
                run(e, self.ops["dve"])

            @block.gpsimd
            def _(e):
                run(e, self.ops["pool"])

            @block.sync
            def _(e):
                run(e, self.ops["sp"])
        self.es.close()

    def mm(self, out, lhsT, rhs, start, stop, reads, writes):
        self.op("pe", lambda e: e.matmul(out, lhsT=lhsT, rhs=rhs, start=start, stop=stop), reads, writes)

    def tr(self, out, in_, ident, reads, writes):
        self.op("pe", lambda e: e.transpose(out=out, in_=in_, identity=ident), reads, writes)

    def act(self, out, in_, func, reads, writes, bias=None, scale=None, accum=None):
        kw = {}
        if bias is not None:
            kw["bias"] = bias
        if scale is not None:
            kw["scale"] = scale
        if accum is not None:
            kw["accum_out"] = accum
        self.op("act", lambda e: e.activation(out=out, in_=in_, func=func, **kw), reads, writes)

    def tt(self, eng, out, in0, in1, op, reads, writes):
        self.op(eng, lambda e: e.tensor_tensor(out=out, in0=in0, in1=in1, op=op), reads, writes)

    def ts(self, eng, out, in0, s1, op0, reads, writes, s2=None, op1=None):
        if op1 is None:
            self.op(eng, lambda e: e.tensor_scalar(out=out, in0=in0, scalar1=s1, scalar2=None, op0=op0), reads, writes)
        else:
            self.op(eng, lambda e: e.tensor_scalar(out=out, in0=in0, scalar1=s1, scalar2=s2, op0=op0, op1=op1), reads, writes)

    def stt(self, out, in0, scalar, in1, op0, op1, reads, writes, accum=None):
        if accum is None:
            self.op("dve", lambda e: e.scalar_tensor_tensor(out=out, in0=in0, scalar=scalar, in1=in1, op0=op0, op1=op1), reads, writes)
        else:
            self.op("dve", lambda e: e.scalar_tensor_tensor(out=out, in0=in0, scalar=scalar, in1=in1, op0=op0, op1=op1, accum_out=accum), reads, writes)

    def red(self, out, in_, reads, writes, op=ALU.add):
        self.op("dve", lambda e: e.tensor_reduce(out=out, in_=in_, axis=AX.X, op=op), reads, writes)

    def cp(self, eng, out, in_, reads, writes):
        if eng == "act":
            self.op("act", lambda e: e.copy(out=out, in_=in_), reads, writes)
        else:
            self.op(eng, lambda e: e.tensor_copy(out=out, in_=in_), reads, writes)

    def recip(self, out, in_, reads, writes):
        self.op("dve", lambda e: e.reciprocal(out=out, in_=in_), reads, writes)

    def memset(self, eng, ap, val, writes):
        self.op(eng, lambda e: e.memset(ap, val), (), writes)

    def ld(self, eng, out, in_, owner, reads=(), writes=None, final=False):
        if writes is None:
            writes = [owner]
        self.dma(eng, lambda e: e.dma_start(out=out, in_=in_), owner, reads, writes, final)


class Arena:
    def __init__(self, ap, words):
        self.ap = ap
        self.words = words
        self.off = 0

    def f32(self, n):
        n = (n + 1) // 2 * 2
        a = self.ap[:, self.off:self.off + n]
        self.off += n
        self.hw = max(getattr(self, "hw", 0), self.off)
        assert self.off <= self.words, f"arena overflow {self.off} > {self.words}"
        return a

    def bf16(self, n):
        w = (n + 1) // 2
        return self.f32(w).bitcast(BF16)[:, 0:n]

    def u32(self, n):
        return self.f32(n).bitcast(U32)


G_N1 = 0
G_QLAT = G_N1 + 1024
G_KVLAT = G_QLAT + 384
G_KPE = G_KVLAT + 256
G_DQK = G_KPE + 64
G_Q12 = G_DQK + 1024
G_K4 = G_Q12 + 768
G_SUB = G_K4 + 512
G_N2 = G_SUB + 128
G_LQ = G_N2 + 1024
G_TOT = G_LQ + 256

C_ID = 0
C_TRI = 128
C_IOTA = 256
C_ONES = 272
C_THR = 400
C_TOT = 416

LAMBDA_INIT = 0.8 - 0.6 * math.exp(-0.3 * 0)


def build(S, NSEQ, do_a1=True, do_a2=True, do_b=True, stage_lim=9):
    NT = S // 128
    NCH = S // 512
    TT = NSEQ * S
    nc = bass.Bass("TRN2", target_bir_lowering=False)
    P = Prog(nc)

    def din(name, shape, dt=F32):
        return nc.dram_tensor(name, list(shape), dt, kind="ExternalInput").ap()

    x_d = din("x", [TT, D])
    cT_d = din("cT", [128, NSEQ, 8])
    adaw_d = din("ada_w", [D, 6 * D])
    adab_d = din("ada_b", [1, 6 * D])
    win_d = din("w_in", [D, 2240])
    wq_d = din("w_q_up", [384, 768])
    wkv_d = din("w_kv_up", [256, 1024])
    wout_d = din("w_out", [D, D])
    pwq_d = din("peer_w_q", [D, 2048])
    keys_d = din("keysT", [128, 16, 128])
    u_d = din("peer_u", [NEXP, D])
    v_d = din("peer_v", [NEXP, D])
    gains_d = din("gains", [128, G_TOT])
    rope_d = din("rope", [128, NT, 64])
    consts_d = din("consts", [128, C_TOT])
    out_d = nc.dram_tensor("out", [TT, D], F32, kind="ExternalOutput").ap()
    x1_d = nc.dram_tensor("x1s", [TT, D], F32, kind="Internal").ap()
    tx1 = [P.tok(f"x1d{i}") for i in range(TT // 128)]

    AW = 52000
    arena_t = P.sbuf("arena", [128, AW], F32)
    A = Arena(arena_t, AW)
    banks = [P.psum(f"pb{i}", [128, 512], F32) for i in range(8)]
    tb = P.toks("bank", 8)
    for t_ in tb:
        t_.excl = True

    def bankbf(i):
        return banks[i][:, :].bitcast(BF16)

    consts = A.f32(C_TOT)
    t_consts = P.tok("consts")
    P.ld("sp", consts, consts_d[:, :], t_consts)
    ident_f = consts[:, C_ID:C_ID + 128]
    iota16 = consts[:, C_IOTA:C_IOTA + 16]
    ones_f = consts[:, C_ONES:C_ONES + 128]
    thr16 = consts[:, C_THR:C_THR + 16]
    identb = A.bf16(128)
    trib = A.bf16(128)
    t_cb = P.tok("constsb")
    P.cp("dve", identb, ident_f, [t_consts], [t_cb])
    P.cp("dve", trib, consts[:, C_TRI:C_TRI + 128], [t_consts], [t_cb])
    cT = A.f32(NSEQ * 8)
    t_cT = P.tok("cT")
    P.ld("sp", cT, cT_d.rearrange("p b k -> p (b k)"), t_cT)
    adab = [A.f32(512), A.f32(512)]
    t_adab = P.toks("adab", 2)
    base_off = A.off

    def small(n):
        return A.f32(n)

    def load_cast_weight(dst_bf, src_dram_rows, ncols, stage, t_stage, t_dst, k, eng_i):
        st = stage[eng_i % len(stage)]
        ts_ = t_stage[eng_i % len(stage)]
        P.ld("sp", st[:, 0:ncols], src_dram_rows, ts_)
        eng = ("act", "dve", "pool")[eng_i % 3]
        P.cp(eng, dst_bf, st[:, 0:ncols], [ts_], [t_dst])

    def compute_mod(b, chunks, dests, t_dests, wst, t_wst, mbank):
        sil = small(8)
        crep = A.f32(8 * 128)
        t_sil = P.tok("sil")
        t_crep = P.tok("crep")
        P.act(sil, cT[:, b * 8:(b + 1) * 8], AF.Silu, [t_cT], [t_sil])
        P.cp("dve", crep.rearrange("p (k m) -> p k m", k=8), sil.unsqueeze(2).to_broadcast([128, 8, 128]), [t_sil], [t_crep])
        adaw_v = adaw_d.rearrange("(k p) n -> p k n", p=128)
        for i, ch in enumerate(chunks):
            w = wst[i % 2]
            tw = t_wst[i % 2]
            P.ld("sp", w.rearrange("p (k n) -> p k n", k=8), adaw_v[:, :, ch * 512:(ch + 1) * 512], tw)
            bk = mbank[i % 2]
            P.ld("sp", adab[i % 2][0:1, :], adab_d[:, ch * 512:(ch + 1) * 512], t_adab[i % 2])
            for k in range(8):
                P.mm(banks[bk][:, :], crep[:, k * 128:(k + 1) * 128], w[:, k * 512:(k + 1) * 512], k == 0, False, [t_crep, tw], [tb[bk]])
            P.mm(banks[bk][:, :], ones_f[0:1, :], adab[i % 2][0:1, :], False, True, [t_consts, t_adab[i % 2]], [tb[bk]])
            P.cp("act", dests[i], banks[bk][:, :], [tb[bk]], [t_dests[i]])

    def rope(src, dst, G, cs, t_src, t_dst, t_cs, tmp, t_tmp):
        cosb = cs[:, 0:32].unsqueeze(1).to_broadcast([128, G, 32])
        sinb = cs[:, 32:64].unsqueeze(1).to_broadcast([128, G, 32])
        x1 = src[:, :, 0:32]
        x2 = src[:, :, 32:64]
        t1 = tmp[:, 0:G * 32].rearrange("p (g d) -> p g d", g=G)
        t2 = tmp[:, G * 32:2 * G * 32].rearrange("p (g d) -> p g d", g=G)
        t3 = tmp[:, 2 * G * 32:3 * G * 32].rearrange("p (g d) -> p g d", g=G)
        t4 = tmp[:, 3 * G * 32:4 * G * 32].rearrange("p (g d) -> p g d", g=G)
        P.tt("dve", t1, x1, cosb, ALU.mult, [t_src, t_cs], [t_tmp[0]])
        P.tt("dve", t2, x2, sinb, ALU.mult, [t_src, t_cs], [t_tmp[1]])
        P.tt("dve", dst[:, :, 0:32], t1, t2, ALU.subtract, [t_tmp[0], t_tmp[1]], [t_dst])
        P.tt("pool", t3, x2, cosb, ALU.mult, [t_src, t_cs], [t_tmp[2]])
        P.tt("pool", t4, x1, sinb, ALU.mult, [t_src, t_cs], [t_tmp[3]])
        P.tt("pool", dst[:, :, 32:64], t3, t4, ALU.add, [t_tmp[2], t_tmp[3]], [t_dst])

    def rstd_from_ss(ss, n, t_ss, scale):
        P.act(ss, ss, AF.Sqrt, [t_ss], [t_ss], bias=EPS, scale=scale)
        P.recip(ss, ss, [t_ss], [t_ss])

    uv_d = nc.dram_tensor("uvtab", [NEXP, 2048], BF16, kind="Internal").ap()
    t_uv = P.tok("uvtab")
    if do_b:
        A.off = base_off
        RB = 4
        uview = u_d.rearrange("(r p) d -> p r d", p=128)
        vview = v_d.rearrange("(r p) d -> p r d", p=128)
        uvview = uv_d.rearrange("(r p) d -> p r d", p=128)
        fu = [A.f32(RB * 1024) for _ in range(2)]
        fvv_ = [A.f32(RB * 1024) for _ in range(2)]
        fo = [A.bf16(RB * 2048) for _ in range(2)]
        t_fu = P.toks("fu", 2)
        t_fv_ = P.toks("fv_", 2)
        t_fo = P.toks("fo", 2)
        for it in range(NEXP // 128 // RB):
            b_ = it % 2
            P.ld("sp", fu[b_].rearrange("p (r d) -> p r d", r=RB), uview[:, it * RB:(it + 1) * RB, :], t_fu[b_])
            P.ld("sp", fvv_[b_].rearrange("p (r d) -> p r d", r=RB), vview[:, it * RB:(it + 1) * RB, :], t_fv_[b_])
            fov = fo[b_].rearrange("p (r d) -> p r d", r=RB)
            P.cp("act", fov[:, :, 0:1024], fu[b_].rearrange("p (r d) -> p r d", r=RB), [t_fu[b_]], [t_fo[b_]])
            P.cp("dve", fov[:, :, 1024:2048], fvv_[b_].rearrange("p (r d) -> p r d", r=RB), [t_fv_[b_]], [t_fo[b_]])
            P.ld("sp", uvview[:, it * RB:(it + 1) * RB, :], fov, t_fo[b_], reads=[t_fo[b_]], writes=[t_uv])
        P.barrier()

    if do_a1 or do_a2:
        A.off = base_off
        ropes = A.f32(NT * 64)
        t_rope = P.tok("rope")
        P.ld("sp", ropes.rearrange("p (t d) -> p t d", t=NT), rope_d[:, :, :], t_rope)
        lam2 = small(2)
        neglam = small(2)
        t_lam = P.tok("lam")
        gsub = A.f32(128)
        G1 = A.f32(1024)
        SH1 = A.f32(1024)
        GATE1 = A.f32(1024)
        t_mod = P.tok("mod1")
        kv_off = A.off
        lq = A.f32(256)
        t_lq = P.tok("lq")
        P.ld("sp", lq, gains_d[:, G_LQ:G_LQ + 256], t_lq)
        prod = A.f32(128)
        lq4 = lq.rearrange("p (a b d) -> p a b d", a=2, b=2)
        P.tt("dve", prod.rearrange("p (a d) -> p a d", a=2), lq4[:, :, 0, :], lq4[:, :, 1, :], ALU.mult, [t_lq], [t_lam])
        P.red(lam2[:, 0:2], prod.rearrange("p (a d) -> p a d", a=2), [t_lam], [t_lam])
        P.act(lam2[:, 0:2], lam2[:, 0:2], AF.Exp, [t_lam], [t_lam])
        P.tt("dve", neglam[:, 0:1], lam2[:, 1:2], lam2[:, 0:1], ALU.subtract, [t_lam], [t_lam])
        P.ts("dve", neglam[:, 0:1], neglam[:, 0:1], -LAMBDA_INIT, ALU.add, [t_lam], [t_lam])
        gs_t = A.f32(128)
        t_gs = P.tok("gs_t")
        P.ld("sp", gs_t, gains_d[:, G_SUB:G_SUB + 128], t_gs)
        P.ts("dve", gsub, gs_t, 1.0 - LAMBDA_INIT, ALU.mult, [t_gs], [t_lam])

        for s in range(NSEQ):
            A.off = kv_off
            P.barrier()
            wst = [A.f32(4096), A.f32(4096)]
            t_wst = P.toks("wst", 2)
            mtmp = [A.f32(512), A.f32(512)]
            t_mtmp = P.toks("mtmp", 2)
            gn1 = A.f32(1024)
            t_gn1 = P.tok("gn1")
            P.ld("sp", gn1, gains_d[:, G_N1:G_N1 + 1024], t_gn1)
            dests = [SH1[:, 0:512], SH1[:, 512:1024], mtmp[0], mtmp[1], GATE1[:, 0:512], GATE1[:, 512:1024]]
            t_dests = [t_mod, t_mod, t_mtmp[0], t_mtmp[1], t_mod, t_mod]
            compute_mod(s, [0, 1, 2, 3, 4, 5], dests, t_dests, wst, t_wst, [0, 1])
            for i in range(2):
                P.stt(G1[:, i * 512:(i + 1) * 512], mtmp[i], 1.0, gn1[:, i * 512:(i + 1) * 512], ALU.add, ALU.mult, [t_mtmp[i], t_gn1], [t_mod])

            for phase in ("a1", "a2"):
                if phase == "a1" and not do_a1:
                    continue
                if phase == "a2" and not do_a2:
                    continue
                A.off = kv_off
                P.barrier()
                a1 = phase == "a1"
                t_gA = P.tok("gA")
                t_w = P.tok("wA")
                gseg = {}

                def gload(name, col, n):
                    buf = A.f32(n)
                    P.ld("sp", buf, gains_d[:, col:col + n], t_gA)
                    gseg[name] = buf

                if a1:
                    gload("qlat", G_QLAT, 384); gload("kvlat", G_KVLAT, 256); gload("kpe", G_KPE, 64)
                    gload("q12", G_Q12, 768); gload("k4", G_K4, 512)
                    WCOL0, WN = 0, 704
                else:
                    gload("dqk", G_DQK, 1024)
                    WCOL0, WN = 704, 1536
                win_b = A.bf16(8 * WN)
                wout_b = A.bf16(4 * 1024)
                wofs = 0 if a1 else 512
                if a1:
                    wq_b = A.bf16(3 * 768)
                    wkv_b = A.bf16(2 * 1024)
                mark = A.off
                stage = [A.f32(1536), A.f32(1536)]
                t_stage = P.toks("stage", 2)
                ei = 0
                for k in range(8):
                    load_cast_weight(win_b[:, k * WN:(k + 1) * WN], win_d[k * 128:(k + 1) * 128, WCOL0:WCOL0 + WN], WN, stage, t_stage, t_w, k, ei); ei += 1
                for k in range(4):
                    load_cast_weight(wout_b[:, k * 1024:(k + 1) * 1024], wout_d[wofs + k * 128:wofs + (k + 1) * 128, :], 1024, stage, t_stage, t_w, k, ei); ei += 1
                if a1:
                    for k in range(3):
                        load_cast_weight(wq_b[:, k * 768:(k + 1) * 768], wq_d[k * 128:(k + 1) * 128, :], 768, stage, t_stage, t_w, k, ei); ei += 1
                    for k in range(2):
                        load_cast_weight(wkv_b[:, k * 1024:(k + 1) * 1024], wkv_d[k * 128:(k + 1) * 128, :], 1024, stage, t_stage, t_w, k, ei); ei += 1
                P.barrier()
                A.off = mark
                if a1:
                    kTn = A.bf16(4 * S)
                    kTp = A.bf16(S)
                    Vv = A.bf16(NT * 4 * 130)
                else:
                    kTn = A.bf16(4 * S)
                    kTp = None
                    Vv = A.bf16(NT * 4 * 130)
                kTn_v = kTn.rearrange("p (h s) -> p h s", h=4)
                Vv_v = Vv.rearrange("p (t h d) -> p t h d", t=NT, h=4)
                t_kv = P.toks("kv", NT)
                t_vinit = P.tok("vinit")
                P.memset("pool", Vv, 1.0, [t_vinit] + t_kv)
                xt = [A.f32(1024), A.f32(1024)]
                t_xt = P.toks("xt", 2)
                junk = A.bf16(1024)
                t_junk = P.tok("junk")
                sm = [A.f32(64), A.f32(64)]
                t_sm = P.toks("sm", 2)
                tmpf = A.f32(1024)
                t_tmpf = P.tok("tmpf")
                hb = A.bf16(1024)
                t_hb = P.tok("hb")
                hT = A.bf16(1024)
                t_hT = P.tok("hT")
                NPJ = 704 if a1 else 1536
                proj = A.f32(NPJ)
                t_proj = P.tok("proj")
                sq = A.f32(768 if a1 else 1024)
                t_sq = P.tok("sq")
                nrm = A.f32(64 if a1 else 1024)
                t_nrm = P.tok("nrm")
                rtmp = A.f32(4 * (4 if a1 else 16) * 32)
                t_rtmp = P.toks("rtmp", 4)
                rot = A.bf16(64 if a1 else 1024)
                t_rot = P.tok("rot")
                if a1:
                    qn = A.bf16(640)
                    t_qn = P.tok("qn")
                    qnT = A.bf16(640)
                    t_qnT = P.tok("qnT")
                    qf = A.f32(768)
                    t_qf = P.tok("qf")
                    qpe = A.f32(256)
                    t_qpe = P.tok("qpe")
                    qm = A.bf16(768)
                    t_qm = P.tok("qm")
                    kf = A.f32(512)
                    t_kf = P.tok("kf")
                    kn = A.bf16(512)
                    t_kn = P.tok("kn")
                    qTn = A.bf16(4 * 512)
                    qTp = A.bf16(4 * 512)
                    qTp_v = qTp.rearrange("p (h s) -> p h s", h=4)
                else:
                    qTn = A.bf16(4 * 512)
                qTn_v = qTn.rearrange("p (h s) -> p h s", h=4)
                t_qT = P.tok("qT")
                PT = [A.bf16(512) for _ in range(3)]
                t_PT = P.toks("PT", 3)
                mixed = A.bf16(4 * 512)
                mixed_v = mixed.rearrange("p (q c) -> p q c", q=4)
                t_mixed = P.toks("mixed", 4)
                mT = A.bf16(512)
                t_mT = P.tok("mT")
                xr = xt
                t_xr = t_xt
                ytmp = A.f32(1024)
                t_ytmp = P.tok("ytmp")
                osm = A.f32(64)
                t_osm = P.tok("osm")
                oa = A.f32(128)
                t_oa = P.tok("oa")
                od = A.f32(128)
                t_od = P.tok("od")
                ti = 0
                pti = 0
                for c in range(NCH if stage_lim >= 1 else 0):
                    for j in range(4):
                        tl = c * 4 + j
                        row0 = s * S + tl * 128
                        xb = xt[ti % 2]; txb = t_xt[ti % 2]
                        smb = sm[ti % 2]; tsm = t_sm[ti % 2]
                        ti += 1
                        P.ld("sp", xb, x_d[row0:row0 + 128, :], txb)
                        P.act(junk, xb, AF.Square, [txb], [t_junk, tsm], accum=smb[:, 0:1])
                        rstd_from_ss(smb[:, 0:1], 1, tsm, 1.0 / 1024)
                        P.stt(tmpf, xb, smb[:, 0:1], G1, ALU.mult, ALU.mult, [txb, tsm, t_mod], [t_tmpf])
                        P.tt("pool", hb, tmpf, SH1, ALU.add, [t_tmpf, t_mod], [t_hb])
                        Tb = bankbf(6)
                        for k in range(8):
                            P.tr(Tb[:, k * 128:(k + 1) * 128], hb[:, k * 128:(k + 1) * 128], identb, [t_hb, t_cb], [tb[6]])
                        P.cp("act", hT, Tb, [tb[6]], [t_hT])
                        if a1:
                            chunks = [(0, 512), (512, 704)]
                            wcol0 = 0
                        else:
                            chunks = [(0, 512), (512, 1024), (1024, 1536)]
                            wcol0 = 704
                        for ci, (c0, c1) in enumerate(chunks):
                            bk = ci
                            for k in range(8):
                                P.mm(banks[bk][:, 0:c1 - c0], hT[:, k * 128:(k + 1) * 128],
                                     win_b[:, k * WN + c0:k * WN + c1], k == 0, k == 7, [t_hT, t_w], [tb[bk]])
                            P.cp("act" if ci % 2 == 0 else "dve", proj[:, c0:c1], banks[bk][:, 0:c1 - c0], [tb[bk]], [t_proj])
                        cs = ropes[:, tl * 64:(tl + 1) * 64]
                        if a1:
                            P.act(sq[:, 0:704], proj[:, 0:704], AF.Square, [t_proj], [t_sq])
                            ss11 = smb[:, 4:15]
                            P.red(ss11, sq[:, 0:704].rearrange("p (g d) -> p g d", d=64), [t_sq], [tsm])
                            ss3 = smb[:, 16:19]
                            P.red(ss3[:, 0:1], ss11[:, 0:6], [tsm], [tsm])
                            P.red(ss3[:, 1:2], ss11[:, 6:10], [tsm], [tsm])
                            P.ts("dve", ss3[:, 0:1], ss3[:, 0:1], 1.0 / 384, ALU.mult, [tsm], [tsm])
                            P.ts("dve", ss3[:, 1:2], ss3[:, 1:2], 1.0 / 256, ALU.mult, [tsm], [tsm])
                            P.ts("dve", ss3[:, 2:3], ss11[:, 10:11], 1.0 / 64, ALU.mult, [tsm], [tsm])
                            rstd_from_ss(ss3, 3, tsm, 1.0)
                            P.stt(qn[:, 0:384], proj[:, 0:384], ss3[:, 0:1], gseg["qlat"], ALU.mult, ALU.mult, [t_proj, tsm, t_gA], [t_qn])
                            P.stt(qn[:, 384:640], proj[:, 384:640], ss3[:, 1:2], gseg["kvlat"], ALU.mult, ALU.mult, [t_proj, tsm, t_gA], [t_qn])
                            P.stt(nrm[:, 0:64], proj[:, 640:704], ss3[:, 2:3], gseg["kpe"], ALU.mult, ALU.mult, [t_proj, tsm, t_gA], [t_nrm])
                            rope(nrm[:, 0:64].rearrange("p (g d) -> p g d", g=1), rot[:, 0:64].rearrange("p (g d) -> p g d", g=1), 1, cs,
                                 t_nrm, t_rot, t_rope, rtmp, t_rtmp)
                            Tb = bankbf(7)
                            for k in range(5):
                                P.tr(Tb[:, k * 128:(k + 1) * 128], qn[:, k * 128:(k + 1) * 128], identb, [t_qn, t_cb], [tb[7]])
                            P.cp("act", qnT, Tb[:, 0:640], [tb[7]], [t_qnT])
                            for cc in range(2):
                                for k in range(3):
                                    P.mm(banks[cc][:, 0:384], qnT[:, k * 128:(k + 1) * 128], wq_b[:, k * 768 + cc * 384:k * 768 + (cc + 1) * 384],
                                         k == 0, k == 2, [t_qnT, t_w], [tb[cc]])
                            for cc in range(2):
                                for k in range(2):
                                    P.mm(banks[2 + cc][:, :], qnT[:, (3 + k) * 128:(4 + k) * 128], wkv_b[:, k * 1024 + cc * 512:k * 1024 + (cc + 1) * 512],
                                         k == 0, k == 1, [t_qnT, t_w], [tb[2 + cc]])
                            for cc in range(2):
                                P.act(sq[:, cc * 384:(cc + 1) * 384], banks[cc][:, 0:384], AF.Square, [tb[cc]], [t_sq])
                            ss12 = smb[:, 20:32]
                            P.red(ss12, sq[:, 0:768].rearrange("p (g d) -> p g d", d=64), [t_sq], [tsm])
                            ss12v = ss12.rearrange("p (h t) -> p h t", t=3)
                            r12 = smb[:, 32:44]
                            r12v = r12.rearrange("p (h t) -> p h t", t=3)
                            P.tt("dve", r12v[:, :, 0], ss12v[:, :, 0], ss12v[:, :, 1], ALU.add, [tsm], [tsm])
                            P.ts("dve", r12v[:, :, 0], r12v[:, :, 0], 1.0 / 128, ALU.mult, [tsm], [tsm])
                            P.cp("dve", r12v[:, :, 1], r12v[:, :, 0], [tsm], [tsm])
                            P.ts("dve", r12v[:, :, 2], ss12v[:, :, 2], 1.0 / 64, ALU.mult, [tsm], [tsm])
                            rstd_from_ss(r12, 12, tsm, 1.0)
                            for cc in range(2):
                                P.tt("dve", qf[:, cc * 384:(cc + 1) * 384].rearrange("p (g d) -> p g d", d=64),
                                     banks[cc][:, 0:384].rearrange("p (g d) -> p g d", d=64),
                                     r12[:, cc * 6:(cc + 1) * 6].unsqueeze(2).to_broadcast([128, 6, 64]), ALU.mult, [tb[cc], tsm], [t_qf])
                            qfv = qf.rearrange("p (h d) -> p h d", h=4)
                            gq = gseg["q12"].rearrange("p (h d) -> p h d", h=4)
                            qmv = qm.rearrange("p (h d) -> p h d", h=4)
                            P.tt("dve", qmv[:, :, 0:128], qfv[:, :, 0:128], gq[:, :, 0:128], ALU.mult, [t_qf, t_gA], [t_qm])
                            qpev = qpe.rearrange("p (h d) -> p h d", h=4)
                            P.tt("pool", qpev, qfv[:, :, 128:192], gq[:, :, 128:192], ALU.mult, [t_qf, t_gA], [t_qpe])
                            rope(qpev, qmv[:, :, 128:192], 4, cs, t_qpe, t_qm, t_rope, rtmp, t_rtmp)
                            for cc in range(2):
                                kvv = banks[2 + cc][:, :].rearrange("p (h d) -> p h d", h=2)
                                P.act(sq[:, cc * 256:(cc + 1) * 256].rearrange("p (h d) -> p h d", h=2), kvv[:, :, 0:128], AF.Square, [tb[2 + cc]], [t_sq])
                                P.cp("act", Vv_v[:, tl, 2 * cc:2 * cc + 2, 0:128], kvv[:, :, 128:256], [tb[2 + cc], t_vinit], [t_kv[tl]])
                            ssk = smb[:, 44:48]
                            P.red(ssk, sq[:, 0:512].rearrange("p (h d) -> p h d", h=4), [t_sq], [tsm])
                            rstd_from_ss(ssk, 4, tsm, 1.0 / 128)
                            for cc in range(2):
                                kvv = banks[2 + cc][:, :].rearrange("p (h d) -> p h d", h=2)
                                P.tt("dve", kf[:, cc * 256:(cc + 1) * 256].rearrange("p (h d) -> p h d", h=2), kvv[:, :, 0:128],
                                     ssk[:, 2 * cc:2 * cc + 2].unsqueeze(2).to_broadcast([128, 2, 128]), ALU.mult, [tb[2 + cc], tsm], [t_kf])
                            P.tt("pool", kn, kf, gseg["k4"], ALU.mult, [t_kf, t_gA], [t_kn])
                            Tb = bankbf(6)
                            for h in range(4):
                                P.tr(Tb[:, h * 128:(h + 1) * 128], qmv[:, h, 0:128], identb, [t_qm, t_cb], [tb[6]])
                                P.tr(Tb[:, 512 + h * 128:512 + (h + 1) * 128], kn[:, h * 128:(h + 1) * 128], identb, [t_kn, t_cb], [tb[6]])
                            P.cp("act", qTn_v[:, :, j * 128:(j + 1) * 128], Tb[:, 0:512].rearrange("p (h t) -> p h t", h=4), [tb[6]], [t_qT])
                            P.cp("dve", kTn_v[:, :, tl * 128:(tl + 1) * 128], Tb[:, 512:1024].rearrange("p (h t) -> p h t", h=4), [tb[6]], [t_kv[tl]])
                            Tb = bankbf(7)
                            for h in range(4):
                                P.tr(Tb[0:64, h * 128:(h + 1) * 128], qmv[:, h, 128:192], identb, [t_qm, t_cb], [tb[7]])
                            P.tr(Tb[0:64, 512:640], rot[:, 0:64], identb, [t_rot, t_cb], [tb[7]])
                            P.cp("act", qTp_v[0:64, :, j * 128:(j + 1) * 128], Tb[0:64, 0:512].rearrange("p (h t) -> p h t", h=4), [tb[7]], [t_qT])
                            P.cp("dve", kTp[0:64, tl * 128:(tl + 1) * 128], Tb[0:64, 512:640], [tb[7]], [t_kv[tl]])
                        else:
                            P.act(sq[:, 0:1024], proj[:, 0:1024], AF.Square, [t_proj], [t_sq])
                            ss16 = smb[:, 4:20]
                            P.red(ss16, sq[:, 0:1024].rearrange("p (g d) -> p g d", d=64), [t_sq], [tsm])
                            rstd_from_ss(ss16, 16, tsm, 1.0 / 64)
                            P.tt("dve", tmpf.rearrange("p (g d) -> p g d", d=64), proj[:, 0:1024].rearrange("p (g d) -> p g d", d=64),
                                 ss16.unsqueeze(2).to_broadcast([128, 16, 64]), ALU.mult, [t_proj, tsm], [t_tmpf])
                            P.tt("pool", nrm, tmpf, gseg["dqk"], ALU.mult, [t_tmpf, t_gA], [t_nrm])
                            rope(nrm.rearrange("p (g d) -> p g d", d=64), rot.rearrange("p (g d) -> p g d", d=64), 16, cs,
                                 t_nrm, t_rot, t_rope, rtmp, t_rtmp)
                            P.cp("act", Vv_v[:, tl, :, 0:128], proj[:, 1024:1536].rearrange("p (h d) -> p h d", h=4), [t_proj, t_vinit], [t_kv[tl]])
                            Tb = bankbf(6)
                            for h in range(8):
                                P.tr(Tb[:, h * 128:(h + 1) * 128], rot[:, h * 128:(h + 1) * 128], identb, [t_rot, t_cb], [tb[6]])
                            P.cp("act", qTn_v[:, :, j * 128:(j + 1) * 128], Tb[:, 0:512].rearrange("p (h t) -> p h t", h=4), [tb[6]], [t_qT])
                            P.cp("dve", kTn_v[:, :, tl * 128:(tl + 1) * 128], Tb[:, 512:1024].rearrange("p (h t) -> p h t", h=4), [tb[6]], [t_kv[tl]])

                    nkb = 4 * c + 4
                    if stage_lim < 2:
                        continue
                    for h in range(4):
                        nmap = 1 if a1 else 2
                        for m in range(nmap):
                            ob = (2, 3) if m == 0 else (4, 5)
                            for kb in range(nkb):
                                jd = kb - 4 * c
                                q0 = 0 if jd < 0 else jd * 128
                                sb = kb % 2
                                if a1:
                                    P.mm(banks[sb][:, q0:512], kTn_v[:, h, kb * 128:(kb + 1) * 128], qTn_v[:, h, q0:512], True, False,
                                         [t_kv[kb], t_qT], [tb[sb]])
                                    P.mm(banks[sb][:, q0:512], kTp[0:64, kb * 128:(kb + 1) * 128], qTp_v[0:64, h, q0:512], False, True,
                                         [t_kv[kb], t_qT], [tb[sb]])
                                    scale = 192 ** -0.5
                                else:
                                    P.mm(banks[sb][:, q0:512], kTn_v[m * 64:(m + 1) * 64, h, kb * 128:(kb + 1) * 128],
                                         qTn_v[m * 64:(m + 1) * 64, h, q0:512], True, True, [t_kv[kb], t_qT], [tb[sb]])
                                    scale = 64 ** -0.5
                                pb = PT[pti % 3]; tpb = t_PT[pti % 3]; pti += 1
                                P.act(pb[:, q0:512], banks[sb][:, q0:512], AF.Exp, [tb[sb]], [tpb], scale=scale)
                                if jd >= 0:
                                    P.tt("pool", pb[:, q0:q0 + 128], pb[:, q0:q0 + 128], trib, ALU.mult, [tpb, t_cb], [tpb])
                                for qb in range(max(jd, 0), 4):
                                    bk = ob[qb // 2]
                                    ov = banks[bk][:, 0:260].rearrange("p (a d) -> p a d", a=2)
                                    P.mm(ov[:, qb % 2, 0:129], pb[:, qb * 128:(qb + 1) * 128], Vv_v[:, kb, h, 0:129],
                                         kb == 0 and qb % 2 == 0, kb == 4 * c + qb, [tpb, t_kv[kb]], [tb[bk]])
                        for qb in range(4):
                            if a1:
                                bk = (2, 3)[qb // 2]
                                ov = banks[bk][:, 0:260].rearrange("p (a d) -> p a d", a=2)
                                P.recip(osm[:, 0:1], ov[:, qb % 2, 128:129], [tb[bk]], [t_osm])
                                P.ts("dve", mixed_v[:, qb, h * 128:(h + 1) * 128], ov[:, qb % 2, 0:128], osm[:, 0:1], ALU.mult, [tb[bk], t_osm], [t_mixed[qb]])
                            else:
                                b1 = (2, 3)[qb // 2]
                                b2 = (4, 5)[qb // 2]
                                o1 = banks[b1][:, 0:260].rearrange("p (a d) -> p a d", a=2)
                                o2 = banks[b2][:, 0:260].rearrange("p (a d) -> p a d", a=2)
                                P.recip(osm[:, 0:1], o1[:, qb % 2, 128:129], [tb[b1]], [t_osm])
                                P.recip(osm[:, 1:2], o2[:, qb % 2, 128:129], [tb[b2]], [t_osm])
                                P.ts("dve", oa, o2[:, qb % 2, 0:128], osm[:, 1:2], ALU.mult, [tb[b2], t_osm, t_lam], [t_oa], s2=neglam[:, 0:1], op1=ALU.mult)
                                P.stt(od, o1[:, qb % 2, 0:128], osm[:, 0:1], oa, ALU.mult, ALU.add, [tb[b1], t_osm, t_oa], [t_od])
                                P.act(oa, od, AF.Square, [t_od], [t_oa, t_osm], accum=osm[:, 2:3])
                                rstd_from_ss(osm[:, 2:3], 1, t_osm, 1.0 / 128)
                                P.stt(mixed_v[:, qb, h * 128:(h + 1) * 128], od, osm[:, 2:3], gsub, ALU.mult, ALU.mult, [t_od, t_osm, t_lam], [t_mixed[qb]])
                    if stage_lim < 3:
                        continue
                    for qb in range(4):
                        tl = c * 4 + qb
                        row0 = s * S + tl * 128
                        gt = row0 // 128
                        Tb = bankbf(6)
                        for k in range(4):
                            P.tr(Tb[:, k * 128:(k + 1) * 128], mixed_v[:, qb, k * 128:(k + 1) * 128], identb, [t_mixed[qb], t_cb], [tb[6]])
                        P.cp("act", mT, Tb[:, 0:512], [tb[6]], [t_mT])
                        xrb = xr[qb % 2]; txr = t_xr[qb % 2]
                        if a1:
                            P.ld("sp", xrb, x_d[row0:row0 + 128, :], txr)
                        else:
                            P.ld("sp", xrb, x1_d[row0:row0 + 128, :], txr, reads=[tx1[gt]])
                        for cc in range(2):
                            bk = 7 if cc == 0 else 6
                            bk = (7, 1)[cc]
                            for k in range(4):
                                P.mm(banks[bk][:, :], mT[:, k * 128:(k + 1) * 128],
                                     wout_b[:, k * 1024 + cc * 512:k * 1024 + (cc + 1) * 512],
                                     k == 0, k == 3, [t_mT, t_w], [tb[bk]])
                            P.tt("dve", ytmp[:, cc * 512:(cc + 1) * 512], banks[bk][:, :], GATE1[:, cc * 512:(cc + 1) * 512], ALU.mult, [tb[bk], t_mod], [t_ytmp])
                        P.tt("pool", xrb, xrb, ytmp, ALU.add, [txr, t_ytmp], [txr])
                        P.ld("sp", x1_d[row0:row0 + 128, :], xrb, txr, reads=[txr], writes=[tx1[gt]])

    if do_b:
        A.off = base_off
        P.barrier()
        NTT = TT // 128
        gB = A.f32(1024)
        t_gB = P.tok("gB")
        P.ld("sp", gB, gains_d[:, G_N2:G_N2 + 1024], t_gB)
        pwq_b = A.bf16(8 * 2048)
        keys_b = A.bf16(16 * 128)
        t_wB = P.tok("wB")
        wst = [A.f32(4096), A.f32(4096)]
        t_wst = P.toks("wstB", 2)
        stage = wst
        t_stage = t_wst
        ei = 0
        for k in range(8):
            load_cast_weight(pwq_b[:, k * 2048:(k + 1) * 2048], pwq_d[k * 128:(k + 1) * 128, :], 2048, stage, t_stage, t_wB, k, ei); ei += 1
        load_cast_weight(keys_b, keys_d.rearrange("p c n -> p (c n)"), 2048, stage, t_stage, t_wB, 0, ei); ei += 1
        G2 = A.f32(1024)
        SH2 = A.f32(1024)
        GATE2 = [A.f32(1024) for _ in range(NSEQ)]
        t_mod2 = P.tok("mod2")
        t_gate2 = P.toks("gate2", NSEQ)
        mtmp = [A.f32(512), A.f32(512)]
        t_mtmp = P.toks("mtmpB", 2)
        xt = [A.f32(1024), A.f32(1024)]
        t_xt = P.toks("xB", 2)
        junk = A.bf16(1024)
        t_junk = P.tok("junkB")
        sm = A.f32(64)
        t_sm = P.tok("smB")
        h2f = A.f32(1024)
        t_h2f = P.tok("h2f")
        h2b = [A.bf16(1024), A.bf16(1024)]
        t_h2b = P.toks("h2b", 2)
        h2T = A.bf16(1024)
        t_h2T = P.tok("h2T")
        qT = A.bf16(16 * 128)
        t_qTB = P.tok("qTB")
        sc = wst[0][:, 0:2048]
        t_sc = t_wst[0]
        sc2 = A.f32(256)
        t_sc2 = P.tok("sc2")
        sv = A.f32(256)
        t_sv = P.tok("sv")
        si = A.u32(256)
        t_si = P.tok("si")
        sif = A.f32(256)
        t_sif = P.tok("sif")
        cand = wst[0][:, 2048:4096]
        t_cand = t_wst[0]
        cand2 = A.f32(256)
        t_cand2 = P.tok("cand2")
        fv = A.f32(128)
        t_fv = P.tok("fv")
        fpos = A.u32(128)
        t_fpos = P.tok("fpos")
        k0f = A.f32(128)
        k1f = A.f32(128)
        t_k = P.tok("k01")
        oh = wst[1][:, 0:2048]
        t_oh = t_wst[1]
        e0 = A.f32(128)
        e1 = A.f32(128)
        t_e = P.tok("e01")
        eidx = [A.u32(128), A.u32(128)]
        t_eidx = P.toks("eidx", 2)
        gw = [A.f32(128), A.f32(128)]
        t_gw = P.toks("gw", 2)
        araw = A.f32(128)
        cg = A.f32(128)
        NR = 8
        t_ar = P.toks("araw", NR)
        t_cf = P.toks("cg", NR)
        NG = 8
        uvb = [A.bf16(2048) for _ in range(NG)]
        t_uvb = P.toks("uvb", NG)
        dg = [A.bf16(128) for _ in range(4)]
        t_dg = P.toks("dg", 4)
        ujunk = A.bf16(1024)
        t_ujunk = P.tok("ujunk")
        yout = A.f32(1024)
        t_yout = P.tok("yout")
        src = x1_d if (do_a1 or do_a2) else x_d

        def pre(gt):
            s, tl = divmod(gt, NT)
            par = gt % 2
            row0 = gt * 128
            if tl == 0:
                compute_mod(s, [6, 7, 8, 9, 10, 11],
                            [SH2[:, 0:512], SH2[:, 512:1024], mtmp[0], mtmp[1], GATE2[s][:, 0:512], GATE2[s][:, 512:1024]],
                            [t_mod2, t_mod2, t_mtmp[0], t_mtmp[1], t_gate2[s], t_gate2[s]], wst, t_wst, [7, 0])
                for i in range(2):
                    P.stt(G2[:, i * 512:(i + 1) * 512], mtmp[i], 1.0, gB[:, i * 512:(i + 1) * 512], ALU.add, ALU.mult, [t_mtmp[i], t_gB], [t_mod2])
                yield
            xb = xt[par]; txb = t_xt[par]
            hb = h2b[par]; thb = t_h2b[par]
            P.ld("sp", xb, src[row0:row0 + 128, :], txb, reads=[tx1[gt]])
            P.act(junk, xb, AF.Square, [txb], [t_junk, t_sm], accum=sm[:, 0:1])
            rstd_from_ss(sm[:, 0:1], 1, t_sm, 1.0 / 1024)
            P.stt(h2f, xb, sm[:, 0:1], G2, ALU.mult, ALU.mult, [txb, t_sm, t_mod2], [t_h2f])
            P.tt("dve", hb, h2f, SH2, ALU.add, [t_h2f, t_mod2], [thb])
            yield
            Tb = bankbf(6)
            for k in range(8):
                P.tr(Tb[:, k * 128:(k + 1) * 128], hb[:, k * 128:(k + 1) * 128], identb, [thb, t_cb], [tb[6]])
            P.cp("act", h2T, Tb, [tb[6]], [t_h2T])
            yield
            for g4 in range(4):
                bk = (7, 0)[g4 % 2]
                for cc in range(4):
                    c16 = g4 * 4 + cc
                    for k in range(8):
                        P.mm(banks[bk][:, cc * 128:(cc + 1) * 128], pwq_b[:, k * 2048 + c16 * 128:k * 2048 + (c16 + 1) * 128],
                             h2T[:, k * 128:(k + 1) * 128], k == 0 and cc == 0, k == 7, [t_wB, t_h2T], [tb[bk]])
                P.cp("act", qT[:, g4 * 512:(g4 + 1) * 512], banks[bk][:, :], [tb[bk]], [t_qTB])
                yield
            for g4 in range(4):
                bk = (1, 7)[g4 % 2]
                for cc in range(4):
                    c16 = g4 * 4 + cc
                    P.mm(banks[bk][:, cc * 128:(cc + 1) * 128], qT[:, c16 * 128:(c16 + 1) * 128], keys_b[:, c16 * 128:(c16 + 1) * 128],
                         cc == 0, True, [t_qTB, t_wB], [tb[bk]])
                P.cp("act", sc[:, g4 * 512:(g4 + 1) * 512], banks[bk][:, :], [tb[bk]], [t_sc])
                yield
            svv = sv.rearrange("p (c k) -> p c k", c=16)
            siv = si.rearrange("p (c k) -> p c k", c=16)
            for c16 in range(16):
                scc = sc[:, c16 * 128:(c16 + 1) * 128]
                P.op("dve", (lambda o, i_: (lambda e: e.max(out=o, in_=i_)))(svv[:, c16, 0:8], scc), [t_sc], [t_sv])
                P.op("dve", (lambda o, r, i_: (lambda e: e.match_replace(out=o, in_to_replace=r, in_values=i_, imm_value=-1e30)))(sc2[:, 0:128], svv[:, c16, 0:8], scc), [t_sc, t_sv], [t_sc2])
                P.op("dve", (lambda o, i_: (lambda e: e.max(out=o, in_=i_)))(svv[:, c16, 8:16], sc2[:, 0:128]), [t_sc2], [t_sv])
                P.op("dve", (lambda o, m_, i_: (lambda e: e.max_index(out=o, in_max=m_, in_values=i_)))(siv[:, c16, 0:8], svv[:, c16, 0:8], scc), [t_sc, t_sv], [t_si])
                P.op("dve", (lambda o, m_, i_: (lambda e: e.max_index(out=o, in_max=m_, in_values=i_)))(siv[:, c16, 8:16], svv[:, c16, 8:16], scc), [t_sc, t_sv], [t_si])
                yield
            P.cp("dve", sif, si, [t_si], [t_sif])
            sv4 = sv.rearrange("p (h t k) -> p h t k", h=8, t=2)
            candv = cand.rearrange("p (h a b) -> p h a b", h=8, a=16)
            P.tt("dve", candv, sv4[:, :, 0, :].unsqueeze(3).to_broadcast([128, 8, 16, 16]),
                 sv4[:, :, 1, :].unsqueeze(2).to_broadcast([128, 8, 16, 16]), ALU.add, [t_sv], [t_cand])
            yield
            fvv = fv.rearrange("p (h k) -> p h k", h=8)
            fpv = fpos.rearrange("p (h k) -> p h k", h=8)
            for h in range(8):
                ch_ = cand[:, h * 256:(h + 1) * 256]
                P.op("dve", (lambda o, i_: (lambda e: e.max(out=o, in_=i_)))(fvv[:, h, 0:8], ch_), [t_cand], [t_fv])
                P.op("dve", (lambda o, r, i_: (lambda e: e.match_replace(out=o, in_to_replace=r, in_values=i_, imm_value=-1e30)))(cand2, fvv[:, h, 0:8], ch_), [t_cand, t_fv], [t_cand2])
                P.op("dve", (lambda o, i_: (lambda e: e.max(out=o, in_=i_)))(fvv[:, h, 8:16], cand2), [t_cand2], [t_fv])
                P.op("dve", (lambda o, m_, i_: (lambda e: e.max_index(out=o, in_max=m_, in_values=i_)))(fpv[:, h, 0:8], fvv[:, h, 0:8], ch_), [t_cand, t_fv], [t_fpos])
                P.op("dve", (lambda o, m_, i_: (lambda e: e.max_index(out=o, in_max=m_, in_values=i_)))(fpv[:, h, 8:16], fvv[:, h, 8:16], ch_), [t_cand, t_fv], [t_fpos])
                yield
            P.cp("dve", k1f, fpos, [t_fpos], [t_k])
            P.tt("dve", oh.rearrange("p (a b) -> p a b", b=16), k1f.unsqueeze(2).to_broadcast([128, 128, 16]),
                 thr16.unsqueeze(1).to_broadcast([128, 128, 16]), ALU.is_ge, [t_k, t_consts], [t_oh])
            P.red(k0f, oh.rearrange("p (a b) -> p a b", b=16), [t_oh], [t_k])
            P.stt(k1f, k0f, -16.0, k1f, ALU.mult, ALU.add, [t_k], [t_k])
            yield
            sif4 = sif.rearrange("p (h t k) -> p h t k", h=8, t=2)
            for t_, (kf_, e_) in enumerate(((k0f, e0), (k1f, e1))):
                P.tt("dve", oh.rearrange("p (a b) -> p a b", b=16), kf_.unsqueeze(2).to_broadcast([128, 128, 16]),
                     iota16.unsqueeze(1).to_broadcast([128, 128, 16]), ALU.is_equal, [t_k, t_consts], [t_oh])
                P.tt("dve", oh.rearrange("p (h a b) -> p h a b", h=8, a=16), oh.rearrange("p (h a b) -> p h a b", h=8, a=16),
                     sif4[:, :, t_, :].unsqueeze(2).to_broadcast([128, 8, 16, 16]), ALU.mult, [t_oh, t_sif], [t_oh])
                P.red(e_, oh.rearrange("p (a b) -> p a b", b=16), [t_oh], [t_e])
                yield
            P.stt(e0, e0, 128.0, e1, ALU.mult, ALU.add, [t_e], [t_e])
            P.cp("dve", eidx[par], e0, [t_e], [t_eidx[par]])
            gwp = gw[par]; tgw = t_gw[par]
            P.tt("dve", gwp.rearrange("p (h k) -> p h k", h=8), fvv, fvv[:, :, 0:1].to_broadcast([128, 8, 16]), ALU.subtract, [t_fv], [tgw])
            P.act(gwp, gwp, AF.Exp, [tgw], [tgw])
            P.red(sm[:, 8:16], gwp.rearrange("p (h k) -> p h k", h=8), [tgw], [t_sm])
            P.recip(sm[:, 8:16], sm[:, 8:16], [t_sm], [t_sm])
            P.tt("dve", gwp.rearrange("p (h k) -> p h k", h=8), gwp.rearrange("p (h k) -> p h k", h=8),
                 sm[:, 8:16].unsqueeze(2).to_broadcast([128, 8, 16]), ALU.mult, [tgw, t_sm], [tgw])
            yield

        cnt = {"gi": 0, "di": 0}

        def slot(gt, sl):
            par = gt % 2
            yb = (2, 3) if par == 0 else (4, 5)
            gi = cnt["gi"]; cnt["gi"] += 1
            ub = uvb[gi % NG]; tub = t_uvb[gi % NG]
            P.dma("pool", (lambda o, ix: (lambda e: e.indirect_dma_start(out=o, out_offset=None, in_=uv_d[:, :],
                  in_offset=bass.IndirectOffsetOnAxis(ap=ix, axis=0))))(ub, eidx[par][:, sl:sl + 1]), tub, reads=[t_eidx[par], t_uv], writes=[tub])
            tar = t_ar[sl % NR]; tcf = t_cf[sl % NR]
            P.stt(ujunk, ub[:, 0:1024], 1.0, h2b[par], ALU.mult, ALU.mult, [tub, t_h2b[par]], [t_ujunk, tar], accum=araw[:, sl:sl + 1])
            P.act(cg[:, sl:sl + 1], araw[:, sl:sl + 1], AF.Gelu, [tar], [tcf])
            di = cnt["di"]; cnt["di"] += 1
            db = dg[di % 4]; tdb = t_dg[di % 4]
            P.ts("dve", db, identb, cg[:, sl:sl + 1], ALU.mult, [t_cb, tcf, t_gw[par]], [tdb], s2=gw[par][:, sl:sl + 1], op1=ALU.mult)
            for cc in range(2):
                bk = yb[cc]
                P.mm(banks[bk][:, :], db, ub[:, 1024 + cc * 512:1024 + (cc + 1) * 512], sl == 0, sl == 127, [tdb, tub], [tb[bk]])

        def fin(gt):
            s, tl = divmod(gt, NT)
            par = gt % 2
            yb = (2, 3) if par == 0 else (4, 5)
            row0 = gt * 128
            for cc in range(2):
                bk = yb[cc]
                P.tt("dve", yout[:, cc * 512:(cc + 1) * 512], banks[bk][:, :], GATE2[s][:, cc * 512:(cc + 1) * 512], ALU.mult, [tb[bk], t_gate2[s]], [t_yout])
            P.tt("pool", yout, yout, xt[par], ALU.add, [t_yout, t_xt[par]], [t_yout])
            P.ld("sp", out_d[row0:row0 + 128, :], yout, t_yout, reads=[t_yout], writes=[], final=True)

        for _ in pre(0):
            pass
        for gt in range(NTT):
            gen = pre(gt + 1) if gt + 1 < NTT else None
            for sl in range(128):
                slot(gt, sl)
                if gen is not None and sl % 2 == 1:
                    next(gen, None)
            if gen is not None:
                for _ in gen:
                    pass
            fin(gt)
    else:
        A.off = base_off
        P.barrier()
        cb = [A.f32(1024), A.f32(1024)]
        t_cbuf = P.toks("cpb", 2)
        for gt in range(TT // 128):
            P.ld("sp", cb[gt % 2], x1_d[gt * 128:(gt + 1) * 128, :], t_cbuf[gt % 2], reads=[tx1[gt]])
            P.ld("sp", out_d[gt * 128:(gt + 1) * 128, :], cb[gt % 2], t_cbuf[gt % 2], reads=[t_cbuf[gt % 2]], writes=[], final=True)

    P.arena_hw = A.hw
    P.emit()
    return nc, P


def rope_table(S):
    half = 32
    inv = (1.0 / (10000.0 ** (np.arange(half, dtype=np.float32) / np.float32(half)))).astype(np.float32)
    ang = np.arange(S, dtype=np.float32)[:, None] * inv[None, :]
    cs = np.concatenate([np.cos(ang), np.sin(ang)], axis=-1).astype(np.float32)
    return np.ascontiguousarray(cs.reshape(S // 128, 128, 64).transpose(1, 0, 2))


def make_consts():
    c = np.zeros((128, C_TOT), np.float32)
    c[:, C_ID:C_ID + 128] = np.eye(128, dtype=np.float32)
    k = np.arange(128)
    c[:, C_TRI:C_TRI + 128] = (k[:, None] <= k[None, :]).astype(np.float32)
    c[:, C_IOTA:C_IOTA + 16] = np.arange(16, dtype=np.float32)[None, :]
    c[:, C_ONES:C_ONES + 128] = 1.0
    c[:, C_THR:C_THR + 15] = 16.0 * np.arange(1, 16, dtype=np.float32)[None, :]
    c[:, C_THR + 15] = 1e9
    return c


def host_layout(inp, S, NSEQ, ncores):
    f = lambda a: np.ascontiguousarray(np.asarray(a, dtype=np.float32))
    x = f(inp["x"])
    c = f(inp["c"])
    rep = lambda v: np.broadcast_to(f(v).reshape(1, -1), (128, f(v).size))
    mqg = f(inp["mla_q_g"])[0]
    mkg = f(inp["mla_k_g"])[0]
    gains = np.concatenate([
        rep(inp["norm1_g"][0]), rep(inp["mla_q_lat_g"][0]), rep(inp["mla_kv_lat_g"][0]),
        rep(mkg[128:192]),
        rep(np.tile(f(inp["diff_q_g"])[0], 8)), rep(np.tile(f(inp["diff_k_g"])[0], 8)),
        rep(np.tile(mqg, 4)), rep(np.tile(mkg[:128], 4)),
        rep(inp["diff_subln_g"][0]), rep(inp["norm2_g"][0]),
        rep(inp["diff_lq1"][0]), rep(inp["diff_lk1"][0]), rep(inp["diff_lq2"][0]), rep(inp["diff_lk2"][0]),
    ], axis=1)
    gains = np.ascontiguousarray(gains, dtype=np.float32)
    assert gains.shape[1] == G_TOT
    keysT = np.ascontiguousarray(f(inp["peer_sub_keys"])[0].reshape(16, 128, 128).transpose(2, 0, 1))
    shared = {
        "ada_w": f(inp["ada_w"])[0], "ada_b": f(inp["ada_b"])[0].reshape(1, -1),
        "w_in": f(inp["w_in"])[0], "w_q_up": f(inp["mla_w_q_up"])[0], "w_kv_up": f(inp["mla_w_kv_up"])[0],
        "w_out": f(inp["w_out"])[0], "peer_w_q": f(inp["peer_w_q"])[0], "keysT": keysT,
        "peer_u": f(inp["peer_u"])[0], "peer_v": f(inp["peer_v"])[0],
        "gains": gains, "rope": rope_table(S), "consts": make_consts(),
    }
    maps = []
    for i in range(ncores):
        xs = np.ascontiguousarray(x[i * NSEQ:(i + 1) * NSEQ].reshape(NSEQ * S, D))
        cs = c[i * NSEQ:(i + 1) * NSEQ]
        cT = np.ascontiguousarray(cs.reshape(NSEQ, 8, 128).transpose(2, 0, 1))
        m = dict(shared)
        m["x"] = xs
        m["cT"] = cT
        maps.append(m)
    return maps


_CACHE = {}


def kernel(**inputs):
    B, S, _ = inputs["x"].shape
    ncores = 8
    NSEQ = B // ncores
    key = (S, NSEQ)
    if key not in _CACHE:
        _CACHE[key] = build(S, NSEQ)[0]
    nc = _CACHE[key]
    maps = host_layout(inputs, S, NSEQ, ncores)
    res = run_bass_kernel_spmd(nc, maps, core_ids=list(range(ncores)))
    out = np.stack([r["out"].reshape(NSEQ, S, D) for r in res.results], axis=0).reshape(B, S, D)
    return out.astype(np.float32)
```

```python
import math
import numpy as np
from contextlib import ExitStack
import concourse.bass as bass
import concourse.mybir as mybir
from concourse.bass_utils import run_bass_kernel_spmd

F32 = mybir.dt.float32
BF16 = mybir.dt.bfloat16
U32 = mybir.dt.uint32
I32 = mybir.dt.int32
ALU = mybir.AluOpType
AF = mybir.ActivationFunctionType
AX = mybir.AxisListType

ENGS = ("pe", "act", "dve", "pool", "sp")
CHUNK = 16000
EPS = 1e-6
D = 1024
NEXP = 16384


class Tok:
    __slots__ = ("name", "w", "r", "dsem", "dcount", "excl")

    def __init__(self, name):
        self.name = name
        self.excl = False
        self.w = None
        self.r = {}
        self.dsem = None
        self.dcount = 0


class Prog:
    def __init__(self, nc):
        self.nc = nc
        self.es = ExitStack()
        self.ops = {e: [] for e in ENGS}
        self.cnt = {e: 0 for e in ENGS}
        self.esems = {e: [] for e in ENGS}
        self.seen = {e: {} for e in ENGS}
        self.semobj = {}
        self.nsem = 0
        self.final_events = []
        self.dma_toks = []
        self.nins = 0

    def sbuf(self, name, shape, dt):
        return self.es.enter_context(self.nc.sbuf_tensor(name, list(shape), dt))

    def psum(self, name, shape, dt):
        return self.es.enter_context(self.nc.psum_tensor(name, list(shape), dt))

    def newsem(self, name):
        self.nsem += 1
        return self.es.enter_context(self.nc.semaphore(name))

    def tok(self, name):
        return Tok(name)

    def toks(self, name, n):
        return [Tok(f"{name}{i}") for i in range(n)]

    def _eng_event(self, eng):
        c = self.cnt[eng]
        ch, v = divmod(c, CHUNK)
        while len(self.esems[eng]) <= ch:
            s = self.newsem(f"e_{eng}_{len(self.esems[eng])}")
            self.esems[eng].append(s)
            self.semobj[(eng, len(self.esems[eng]) - 1)] = s
        self.cnt[eng] = c + 1
        return ((eng, ch), v + 1)

    def _collect(self, eng, reads, writes, same_eng_sync):
        deps = {}

        def add(ev):
            if ev is None:
                return
            k, v = ev
            if deps.get(k, 0) < v:
                deps[k] = v

        for b in reads:
            add(b.w)
        for b in writes:
            add(b.w)
            for k, v in b.r.items():
                add((k, v))
        waits = []
        seen = self.seen[eng]
        for k, v in deps.items():
            if (not same_eng_sync) and k[0] == eng:
                continue
            if seen.get(k, 0) >= v:
                continue
            seen[k] = v
            waits.append((k, v))
        return waits

    def _record(self, ev, reads, writes):
        k, v = ev
        for b in reads:
            if b.r.get(k, 0) < v:
                b.r[k] = v
        for b in writes:
            b.w = ev
            b.r = {}

    def op(self, eng, fn, reads=(), writes=(), sync_same=None):
        if sync_same is None:
            sync_same = eng != "pe"
        ex = [b for b in reads if b.excl]
        if ex:
            writes = list(writes) + ex
        waits = self._collect(eng, reads, writes, sync_same)
        ev = self._eng_event(eng)
        self._record(ev, reads, writes)
        self.ops[eng].append((waits, fn, ev, 1))
        self.nins += 1

    def dma(self, eng, fn, owner, reads=(), writes=(), final=False):
        waits = self._collect(eng, reads, writes, False)
        if owner.dsem is None:
            owner.dsem = ("dma", self.nsem)
            self.semobj[owner.dsem] = self.newsem(f"d{self.nsem}")
            self.dma_toks.append(owner)
        owner.dcount += 16
        ev = (owner.dsem, owner.dcount)
        self._record(ev, reads, writes)
        self.ops[eng].append((waits, fn, ev, 16))
        if final:
            self.final_events.append(ev)
        self.nins += 1

    def barrier(self):
        evs = {}
        for e in ENGS:
            c = self.cnt[e]
            if c == 0:
                continue
            ch, v = divmod(c - 1, CHUNK)
            evs[(e, ch)] = v + 1
        for t in self.dma_toks:
            evs[t.dsem] = t.dcount
        for e in ENGS:
            waits = []
            for k, v in evs.items():
                if k[0] == e:
                    continue
                if self.seen[e].get(k, 0) >= v:
                    continue
                self.seen[e][k] = v
                waits.append((k, v))
            self.ops[e].append((waits, None, None, 0))

    def emit(self):
        nc = self.nc
        fw = {}
        for k, v in self.final_events:
            fw[k] = max(fw.get(k, 0), v)
        self.ops["sp"].append((list(fw.items()), None, None, 0))
        semobj = self.semobj

        def run(engobj, lst):
            for waits, fn, ev, inc in lst:
                for k, v in waits:
                    engobj.wait_ge(semobj[k], v)
                if fn is None:
                    continue
                ins = fn(engobj)
                ins.then_inc(semobj[ev[0]], inc)

        with nc.Block() as block:
            @block.tensor
            def _(e):
                run(e, self.ops["pe"])

            @block.scalar
            def _(e):
                run(e, self.ops["act"])

            @block.vector
            def _(e):
                run(e, self.ops["dve"])

            @block.gpsimd
            def _(e):
                run(e, self.ops["pool"])

            @block.sync
            def _(e):
                run(e, self.ops["sp"])
        self.es.close()

    def mm(self, out, lhsT, rhs, start, stop, reads, writes):
        self.op("pe", lambda e: e.matmul(out, lhsT=lhsT, rhs=rhs, start=start, stop=stop), reads, writes)

    def tr(self, out, in_, ident, reads, writes):
        self.op("pe", lambda e: e.transpose(out=out, in_=in_, identity=ident), reads, writes)

    def act(self, out, in_, func, reads, writes, bias=None, scale=None, accum=None):
        kw = {}
        if bias is not None:
            kw["bias"] = bias
        if scale is not None:
            kw["scale"] = scale
        if accum is not None:
            kw["accum_out"] = accum
        self.op("act", lambda e: e.activation(out=out, in_=in_, func=func, **kw), reads, writes)

    def tt(self, eng, out, in0, in1, op, reads, writes):
        self.op(eng, lambda e: e.tensor_tensor(out=out, in0=in0, in1=in1, op=op), reads, writes)

    def ts(self, eng, out, in0, s1, op0, reads, writes, s2=None, op1=None):
        if op1 is None:
            self.op(eng, lambda e: e.tensor_scalar(out=out, in0=in0, scalar1=s1, scalar2=None, op0=op0), reads, writes)
        else:
            self.op(eng, lambda e: e.tensor_scalar(out=out, in0=in0, scalar1=s1, scalar2=s2, op0=op0, op1=op1), reads, writes)

    def stt(self, out, in0, scalar, in1, op0, op1, reads, writes, accum=None):
        if accum is None:
            self.op("dve", lambda e: e.scalar_tensor_tensor(out=out, in0=in0, scalar=scalar, in1=in1, op0=op0, op1=op1), reads, writes)
        else:
            self.op("dve", lambda e: e.scalar_tensor_tensor(out=out, in0=in0, scalar=scalar, in1=in1, op0=op0, op1=op1, accum_out=accum), reads, writes)

    def red(self, out, in_, reads, writes, op=ALU.add):
        self.op("dve", lambda e: e.tensor_reduce(out=out, in_=in_, axis=AX.X, op=op), reads, writes)

    def cp(self, eng, out, in_, reads, writes):
        if eng == "act":
            self.op("act", lambda e: e.copy(out=out, in_=in_), reads, writes)
        else:
            self.op(eng, lambda e: e.tensor_copy(out=out, in_=in_), reads, writes)

    def recip(self, out, in_, reads, writes):
        self.op("dve", lambda e: e.reciprocal(out=out, in_=in_), reads, writes)

    def memset(self, eng, ap, val, writes):
        self.op(eng, lambda e: e.memset(ap, val), (), writes)

    def ld(self, eng, out, in_, owner, reads=(), writes=None, final=False):
        if writes is None:
            writes = [owner]
        self.dma(eng, lambda e: e.dma_start(out=out, in_=in_), owner, reads, writes, final)


class Arena:
    def __init__(self, ap, words):
        self.ap = ap
        self.words = words
        self.off = 0

    def f32(self, n):
        n = (n + 1) // 2 * 2
        a = self.ap[:, self.off:self.off + n]
        self.off += n
        self.hw = max(getattr(self, "hw", 0), self.off)
        assert self.off <= self.words, f"arena overflow {self.off} > {self.words}"
        return a

    def bf16(self, n):
        w = (n + 1) // 2
        return self.f32(w).bitcast(BF16)[:, 0:n]

    def u32(self, n):
        return self.f32(n).bitcast(U32)


G_N1 = 0
G_QLAT = G_N1 + 1024
G_KVLAT = G_QLAT + 384
G_KPE = G_KVLAT + 256
G_DQK = G_KPE + 64
G_Q12 = G_DQK + 1024
G_K4 = G_Q12 + 768
G_SUB = G_K4 + 512
G_N2 = G_SUB + 128
G_LQ = G_N2 + 1024
G_TOT = G_LQ + 256

C_ID = 0
C_TRI = 128
C_IOTA = 256
C_ONES = 272
C_THR = 400
C_TOT = 416

LAMBDA_INIT = 0.8 - 0.6 * math.exp(-0.3 * 0)


def build(S, NSEQ, do_a1=True, do_a2=True, do_b=True, stage_lim=9):
    NT = S // 128
    NCH = S // 512
    TT = NSEQ * S
    nc = bass.Bass("TRN2", target_bir_lowering=False)
    P = Prog(nc)

    def din(name, shape, dt=F32):
        return nc.dram_tensor(name, list(shape), dt, kind="ExternalInput").ap()

    x_d = din("x", [TT, D])
    cT_d = din("cT", [128, NSEQ, 8])
    adaw_d = din("ada_w", [D, 6 * D])
    adab_d = din("ada_b", [1, 6 * D])
    win_d = din("w_in", [D, 2240])
    wq_d = din("w_q_up", [384, 768])
    wkv_d = din("w_kv_up", [256, 1024])
    wout_d = din("w_out", [D, D])
    pwq_d = din("peer_w_q", [D, 2048])
    keys_d = din("keysT", [128, 16, 128])
    u_d = din("peer_u", [NEXP, D])
    v_d = din("peer_v", [NEXP, D])
    gains_d = din("gains", [128, G_TOT])
    rope_d = din("rope", [128, NT, 64])
    consts_d = din("consts", [128, C_TOT])
    out_d = nc.dram_tensor("out", [TT, D], F32, kind="ExternalOutput").ap()
    x1_d = nc.dram_tensor("x1s", [TT, D], F32, kind="Internal").ap()
    tx1 = [P.tok(f"x1d{i}") for i in range(TT // 128)]

    AW = 53200
    arena_t = P.sbuf("arena", [128, AW], F32)
    A = Arena(arena_t, AW)
    banks = [P.psum(f"pb{i}", [128, 512], F32) for i in range(8)]
    tb = P.toks("bank", 8)
    for t_ in tb:
        t_.excl = True

    def bankbf(i):
        return banks[i][:, :].bitcast(BF16)

    consts = A.f32(C_TOT)
    t_consts = P.tok("consts")
    P.ld("sp", consts, consts_d[:, :], t_consts)
    ident_f = consts[:, C_ID:C_ID + 128]
    iota16 = consts[:, C_IOTA:C_IOTA + 16]
    ones_f = consts[:, C_ONES:C_ONES + 128]
    thr16 = consts[:, C_THR:C_THR + 16]
    identb = A.bf16(128)
    trib = A.bf16(128)
    t_cb = P.tok("constsb")
    P.cp("dve", identb, ident_f, [t_consts], [t_cb])
    P.cp("dve", trib, consts[:, C_TRI:C_TRI + 128], [t_consts], [t_cb])
    cT = A.f32(NSEQ * 8)
    t_cT = P.tok("cT")
    P.ld("sp", cT, cT_d.rearrange("p b k -> p (b k)"), t_cT)
    adab = [A.f32(512), A.f32(512)]
    t_adab = P.toks("adab", 2)
    base_off = A.off

    def small(n):
        return A.f32(n)

    def load_cast_weight(dst_bf, src_dram_rows, ncols, stage, t_stage, t_dst, k, eng_i):
        st = stage[eng_i % len(stage)]
        ts_ = t_stage[eng_i % len(stage)]
        P.ld("sp", st[:, 0:ncols], src_dram_rows, ts_)
        eng = ("act", "dve", "pool")[eng_i % 3]
        P.cp(eng, dst_bf, st[:, 0:ncols], [ts_], [t_dst])

    def compute_mod(b, chunks, dests, t_dests, wst, t_wst, mbank):
        sil = small(8)
        crep = A.f32(8 * 128)
        t_sil = P.tok("sil")
        t_crep = P.tok("crep")
        P.act(sil, cT[:, b * 8:(b + 1) * 8], AF.Silu, [t_cT], [t_sil])
        P.cp("dve", crep.rearrange("p (k m) -> p k m", k=8), sil.unsqueeze(2).to_broadcast([128, 8, 128]), [t_sil], [t_crep])
        adaw_v = adaw_d.rearrange("(k p) n -> p k n", p=128)
        for i, ch in enumerate(chunks):
            w = wst[i % 2]
            tw = t_wst[i % 2]
            P.ld("sp", w.rearrange("p (k n) -> p k n", k=8), adaw_v[:, :, ch * 512:(ch + 1) * 512], tw)
            bk = mbank[i % 2]
            P.ld("sp", adab[i % 2][0:1, :], adab_d[:, ch * 512:(ch + 1) * 512], t_adab[i % 2])
            for k in range(8):
                P.mm(banks[bk][:, :], crep[:, k * 128:(k + 1) * 128], w[:, k * 512:(k + 1) * 512], k == 0, False, [t_crep, tw], [tb[bk]])
            P.mm(banks[bk][:, :], ones_f[0:1, :], adab[i % 2][0:1, :], False, True, [t_consts, t_adab[i % 2]], [tb[bk]])
            P.cp("act", dests[i], banks[bk][:, :], [tb[bk]], [t_dests[i]])

    def rope(src, dst, G, cs, t_src, t_dst, t_cs, tmp, t_tmp):
        cosb = cs[:, 0:32].unsqueeze(1).to_broadcast([128, G, 32])
        sinb = cs[:, 32:64].unsqueeze(1).to_broadcast([128, G, 32])
        x1 = src[:, :, 0:32]
        x2 = src[:, :, 32:64]
        t1 = tmp[:, 0:G * 32].rearrange("p (g d) -> p g d", g=G)
        t2 = tmp[:, G * 32:2 * G * 32].rearrange("p (g d) -> p g d", g=G)
        t3 = tmp[:, 2 * G * 32:3 * G * 32].rearrange("p (g d) -> p g d", g=G)
        t4 = tmp[:, 3 * G * 32:4 * G * 32].rearrange("p (g d) -> p g d", g=G)
        P.tt("dve", t1, x1, cosb, ALU.mult, [t_src, t_cs], [t_tmp[0]])
        P.tt("dve", t2, x2, sinb, ALU.mult, [t_src, t_cs], [t_tmp[1]])
        P.tt("dve", dst[:, :, 0:32], t1, t2, ALU.subtract, [t_tmp[0], t_tmp[1]], [t_dst])
        P.tt("pool", t3, x2, cosb, ALU.mult, [t_src, t_cs], [t_tmp[2]])
        P.tt("pool", t4, x1, sinb, ALU.mult, [t_src, t_cs], [t_tmp[3]])
        P.tt("pool", dst[:, :, 32:64], t3, t4, ALU.add, [t_tmp[2], t_tmp[3]], [t_dst])

    def rstd_from_ss(ss, n, t_ss, scale):
        P.act(ss, ss, AF.Sqrt, [t_ss], [t_ss], bias=EPS, scale=scale)
        P.recip(ss, ss, [t_ss], [t_ss])

    uv_d = nc.dram_tensor("uvtab", [NEXP, 2048], BF16, kind="Internal").ap()
    t_uv = P.tok("uvtab")
    if do_b:
        A.off = base_off
        RB = 4
        uview = u_d.rearrange("(r p) d -> p r d", p=128)
        vview = v_d.rearrange("(r p) d -> p r d", p=128)
        uvview = uv_d.rearrange("(r p) d -> p r d", p=128)
        fu = [A.f32(RB * 1024) for _ in range(2)]
        fvv_ = [A.f32(RB * 1024) for _ in range(2)]
        fo = [A.bf16(RB * 2048) for _ in range(2)]
        t_fu = P.toks("fu", 2)
        t_fv_ = P.toks("fv_", 2)
        t_fo = P.toks("fo", 2)
        for it in range(NEXP // 128 // RB):
            b_ = it % 2
            P.ld("sp", fu[b_].rearrange("p (r d) -> p r d", r=RB), uview[:, it * RB:(it + 1) * RB, :], t_fu[b_])
            P.ld("sp", fvv_[b_].rearrange("p (r d) -> p r d", r=RB), vview[:, it * RB:(it + 1) * RB, :], t_fv_[b_])
            fov = fo[b_].rearrange("p (r d) -> p r d", r=RB)
            P.cp("act", fov[:, :, 0:1024], fu[b_].rearrange("p (r d) -> p r d", r=RB), [t_fu[b_]], [t_fo[b_]])
            P.cp("dve", fov[:, :, 1024:2048], fvv_[b_].rearrange("p (r d) -> p r d", r=RB), [t_fv_[b_]], [t_fo[b_]])
            P.ld("sp", uvview[:, it * RB:(it + 1) * RB, :], fov, t_fo[b_], reads=[t_fo[b_]], writes=[t_uv])
        P.barrier()

    if do_a1 or do_a2:
        A.off = base_off
        ropes = A.f32(NT * 64)
        t_rope = P.tok("rope")
        P.ld("sp", ropes.rearrange("p (t d) -> p t d", t=NT), rope_d[:, :, :], t_rope)
        lam2 = small(2)
        neglam = small(2)
        t_lam = P.tok("lam")
        gsub = A.f32(128)
        G1 = A.f32(1024)
        SH1 = A.f32(1024)
        GATE1 = A.f32(1024)
        t_mod = P.tok("mod1")
        kv_off = A.off
        lq = A.f32(256)
        t_lq = P.tok("lq")
        P.ld("sp", lq, gains_d[:, G_LQ:G_LQ + 256], t_lq)
        prod = A.f32(128)
        lq4 = lq.rearrange("p (a b d) -> p a b d", a=2, b=2)
        P.tt("dve", prod.rearrange("p (a d) -> p a d", a=2), lq4[:, :, 0, :], lq4[:, :, 1, :], ALU.mult, [t_lq], [t_lam])
        P.red(lam2[:, 0:2], prod.rearrange("p (a d) -> p a d", a=2), [t_lam], [t_lam])
        P.act(lam2[:, 0:2], lam2[:, 0:2], AF.Exp, [t_lam], [t_lam])
        P.tt("dve", neglam[:, 0:1], lam2[:, 1:2], lam2[:, 0:1], ALU.subtract, [t_lam], [t_lam])
        P.ts("dve", neglam[:, 0:1], neglam[:, 0:1], -LAMBDA_INIT, ALU.add, [t_lam], [t_lam])
        gs_t = A.f32(128)
        t_gs = P.tok("gs_t")
        P.ld("sp", gs_t, gains_d[:, G_SUB:G_SUB + 128], t_gs)
        P.ts("dve", gsub, gs_t, 1.0 - LAMBDA_INIT, ALU.mult, [t_gs], [t_lam])

        for s in range(NSEQ):
            A.off = kv_off
            P.barrier()
            wst = [A.f32(4096), A.f32(4096)]
            t_wst = P.toks("wst", 2)
            mtmp = [A.f32(512), A.f32(512)]
            t_mtmp = P.toks("mtmp", 2)
            gn1 = A.f32(1024)
            t_gn1 = P.tok("gn1")
            P.ld("sp", gn1, gains_d[:, G_N1:G_N1 + 1024], t_gn1)
            dests = [SH1[:, 0:512], SH1[:, 512:1024], mtmp[0], mtmp[1], GATE1[:, 0:512], GATE1[:, 512:1024]]
            t_dests = [t_mod, t_mod, t_mtmp[0], t_mtmp[1], t_mod, t_mod]
            compute_mod(s, [0, 1, 2, 3, 4, 5], dests, t_dests, wst, t_wst, [0, 1])
            for i in range(2):
                P.stt(G1[:, i * 512:(i + 1) * 512], mtmp[i], 1.0, gn1[:, i * 512:(i + 1) * 512], ALU.add, ALU.mult, [t_mtmp[i], t_gn1], [t_mod])

            for phase in ("a1", "a2"):
                if phase == "a1" and not do_a1:
                    continue
                if phase == "a2" and not do_a2:
                    continue
                A.off = kv_off
                P.barrier()
                a1 = phase == "a1"
                t_gA = P.tok("gA")
                t_w = P.tok("wA")
                gseg = {}

                def gload(name, col, n):
                    buf = A.f32(n)
                    P.ld("sp", buf, gains_d[:, col:col + n], t_gA)
                    gseg[name] = buf

                if a1:
                    gload("qlat", G_QLAT, 384); gload("kvlat", G_KVLAT, 256); gload("kpe", G_KPE, 64)
                    gload("q12", G_Q12, 768); gload("k4", G_K4, 512)
                    WCOL0, WN = 0, 704
                else:
                    gload("dqk", G_DQK, 1024)
                    WCOL0, WN = 704, 1536
                win_b = A.bf16(8 * WN)
                wout_b = A.bf16(4 * 1024)
                wofs = 0 if a1 else 512
                if a1:
                    wq_b = A.bf16(3 * 768)
                    wkv_b = A.bf16(2 * 1024)
                mark = A.off
                stage = [A.f32(1536), A.f32(1536)]
                t_stage = P.toks("stage", 2)
                ei = 0
                for k in range(8):
                    load_cast_weight(win_b[:, k * WN:(k + 1) * WN], win_d[k * 128:(k + 1) * 128, WCOL0:WCOL0 + WN], WN, stage, t_stage, t_w, k, ei); ei += 1
                for k in range(4):
                    load_cast_weight(wout_b[:, k * 1024:(k + 1) * 1024], wout_d[wofs + k * 128:wofs + (k + 1) * 128, :], 1024, stage, t_stage, t_w, k, ei); ei += 1
                if a1:
                    for k in range(3):
                        load_cast_weight(wq_b[:, k * 768:(k + 1) * 768], wq_d[k * 128:(k + 1) * 128, :], 768, stage, t_stage, t_w, k, ei); ei += 1
                    for k in range(2):
                        load_cast_weight(wkv_b[:, k * 1024:(k + 1) * 1024], wkv_d[k * 128:(k + 1) * 128, :], 1024, stage, t_stage, t_w, k, ei); ei += 1
                P.barrier()
                A.off = mark
                if a1:
                    kTn = A.bf16(4 * S)
                    kTp = A.bf16(S)
                    Vv = A.bf16(NT * 4 * 130)
                else:
                    kTn = A.bf16(4 * S)
                    kTp = None
                    Vv = A.bf16(NT * 4 * 130)
                kTn_v = kTn.rearrange("p (h s) -> p h s", h=4)
                Vv_v = Vv.rearrange("p (t h d) -> p t h d", t=NT, h=4)
                t_kv = P.toks("kv", NT)
                t_vinit = P.tok("vinit")
                P.memset("pool", Vv, 1.0, [t_vinit] + t_kv)
                xt = [A.f32(1024), A.f32(1024)]
                t_xt = P.toks("xt", 2)
                junk = A.bf16(1024)
                t_junk = P.tok("junk")
                sm = [A.f32(64), A.f32(64)]
                t_sm = P.toks("sm", 2)
                tmpf = A.f32(1024)
                t_tmpf = P.tok("tmpf")
                hb = A.bf16(1024)
                t_hb = P.tok("hb")
                hT = A.bf16(1024)
                t_hT = P.tok("hT")
                NPJ = 704 if a1 else 1536
                proj = A.f32(NPJ)
                t_proj = P.tok("proj")
                sq = A.f32(768 if a1 else 1024)
                t_sq = P.tok("sq")
                nrm = A.f32(64 if a1 else 1024)
                t_nrm = P.tok("nrm")
                rtmp = A.f32(4 * (4 if a1 else 16) * 32)
                t_rtmp = P.toks("rtmp", 4)
                rot = A.bf16(64 if a1 else 1024)
                t_rot = P.tok("rot")
                if a1:
                    qn = A.bf16(640)
                    t_qn = P.tok("qn")
                    qnT = A.bf16(640)
                    t_qnT = P.tok("qnT")
                    qf = A.f32(768)
                    t_qf = P.tok("qf")
                    qpe = A.f32(256)
                    t_qpe = P.tok("qpe")
                    qm = A.bf16(768)
                    t_qm = P.tok("qm")
                    kf = A.f32(512)
                    t_kf = P.tok("kf")
                    kn = A.bf16(512)
                    t_kn = P.tok("kn")
                    qTn = A.bf16(4 * 512)
                    qTp = A.bf16(4 * 512)
                    qTp_v = qTp.rearrange("p (h s) -> p h s", h=4)
                else:
                    qTn = A.bf16(4 * 512)
                qTn_v = qTn.rearrange("p (h s) -> p h s", h=4)
                t_qT = P.tok("qT")
                PT = [A.bf16(512) for _ in range(3)]
                t_PT = P.toks("PT", 3)
                mixed = A.bf16(4 * 512)
                mixed_v = mixed.rearrange("p (q c) -> p q c", q=4)
                t_mixed = P.toks("mixed", 4)
                mT = A.bf16(512)
                t_mT = P.tok("mT")
                xr = xt
                t_xr = t_xt
                ytmp = A.f32(1024)
                t_ytmp = P.tok("ytmp")
                osm = A.f32(64)
                t_osm = P.tok("osm")
                oa = A.f32(128)
                t_oa = P.tok("oa")
                od = A.f32(128)
                t_od = P.tok("od")
                ti = 0
                pti = 0
                for c in range(NCH if stage_lim >= 1 else 0):
                    for j in range(4):
                        tl = c * 4 + j
                        row0 = s * S + tl * 128
                        xb = xt[ti % 2]; txb = t_xt[ti % 2]
                        smb = sm[ti % 2]; tsm = t_sm[ti % 2]
                        ti += 1
                        P.ld("sp", xb, x_d[row0:row0 + 128, :], txb)
                        P.act(junk, xb, AF.Square, [txb], [t_junk, tsm], accum=smb[:, 0:1])
                        rstd_from_ss(smb[:, 0:1], 1, tsm, 1.0 / 1024)
                        P.stt(tmpf, xb, smb[:, 0:1], G1, ALU.mult, ALU.mult, [txb, tsm, t_mod], [t_tmpf])
                        P.tt("pool", hb, tmpf, SH1, ALU.add, [t_tmpf, t_mod], [t_hb])
                        Tb = bankbf(6)
                        for k in range(8):
                            P.tr(Tb[:, k * 128:(k + 1) * 128], hb[:, k * 128:(k + 1) * 128], identb, [t_hb, t_cb], [tb[6]])
                        P.cp("act", hT, Tb, [tb[6]], [t_hT])
                        if a1:
                            chunks = [(0, 512), (512, 704)]
                            wcol0 = 0
                        else:
                            chunks = [(0, 512), (512, 1024), (1024, 1536)]
                            wcol0 = 704
                        for ci, (c0, c1) in enumerate(chunks):
                            bk = ci
                            for k in range(8):
                                P.mm(banks[bk][:, 0:c1 - c0], hT[:, k * 128:(k + 1) * 128],
                                     win_b[:, k * WN + c0:k * WN + c1], k == 0, k == 7, [t_hT, t_w], [tb[bk]])
                            P.cp("act" if ci % 2 == 0 else "dve", proj[:, c0:c1], banks[bk][:, 0:c1 - c0], [tb[bk]], [t_proj])
                        cs = ropes[:, tl * 64:(tl + 1) * 64]
                        if a1:
                            P.act(sq[:, 0:704], proj[:, 0:704], AF.Square, [t_proj], [t_sq])
                            ss11 = smb[:, 4:15]
                            P.red(ss11, sq[:, 0:704].rearrange("p (g d) -> p g d", d=64), [t_sq], [tsm])
                            ss3 = smb[:, 16:19]
                            P.red(ss3[:, 0:1], ss11[:, 0:6], [tsm], [tsm])
                            P.red(ss3[:, 1:2], ss11[:, 6:10], [tsm], [tsm])
                            P.ts("dve", ss3[:, 0:1], ss3[:, 0:1], 1.0 / 384, ALU.mult, [tsm], [tsm])
                            P.ts("dve", ss3[:, 1:2], ss3[:, 1:2], 1.0 / 256, ALU.mult, [tsm], [tsm])
                            P.ts("dve", ss3[:, 2:3], ss11[:, 10:11], 1.0 / 64, ALU.mult, [tsm], [tsm])
                            rstd_from_ss(ss3, 3, tsm, 1.0)
                            P.stt(qn[:, 0:384], proj[:, 0:384], ss3[:, 0:1], gseg["qlat"], ALU.mult, ALU.mult, [t_proj, tsm, t_gA], [t_qn])
                            P.stt(qn[:, 384:640], proj[:, 384:640], ss3[:, 1:2], gseg["kvlat"], ALU.mult, ALU.mult, [t_proj, tsm, t_gA], [t_qn])
                            P.stt(nrm[:, 0:64], proj[:, 640:704], ss3[:, 2:3], gseg["kpe"], ALU.mult, ALU.mult, [t_proj, tsm, t_gA], [t_nrm])
                            rope(nrm[:, 0:64].rearrange("p (g d) -> p g d", g=1), rot[:, 0:64].rearrange("p (g d) -> p g d", g=1), 1, cs,
                                 t_nrm, t_rot, t_rope, rtmp, t_rtmp)
                            Tb = bankbf(7)
                            for k in range(5):
                                P.tr(Tb[:, k * 128:(k + 1) * 128], qn[:, k * 128:(k + 1) * 128], identb, [t_qn, t_cb], [tb[7]])
                            P.cp("act", qnT, Tb[:, 0:640], [tb[7]], [t_qnT])
                            for cc in range(2):
                                for k in range(3):
                                    P.mm(banks[cc][:, 0:384], qnT[:, k * 128:(k + 1) * 128], wq_b[:, k * 768 + cc * 384:k * 768 + (cc + 1) * 384],
                                         k == 0, k == 2, [t_qnT, t_w], [tb[cc]])
                            for cc in range(2):
                                for k in range(2):
                                    P.mm(banks[2 + cc][:, :], qnT[:, (3 + k) * 128:(4 + k) * 128], wkv_b[:, k * 1024 + cc * 512:k * 1024 + (cc + 1) * 512],
                                         k == 0, k == 1, [t_qnT, t_w], [tb[2 + cc]])
                            for cc in range(2):
                                P.act(sq[:, cc * 384:(cc + 1) * 384], banks[cc][:, 0:384], AF.Square, [tb[cc]], [t_sq])
                            ss12 = smb[:, 20:32]
                            P.red(ss12, sq[:, 0:768].rearrange("p (g d) -> p g d", d=64), [t_sq], [tsm])
                            ss12v = ss12.rearrange("p (h t) -> p h t", t=3)
                            r12 = smb[:, 32:44]
                            r12v = r12.rearrange("p (h t) -> p h t", t=3)
                            P.tt("dve", r12v[:, :, 0], ss12v[:, :, 0], ss12v[:, :, 1], ALU.add, [tsm], [tsm])
                            P.ts("dve", r12v[:, :, 0], r12v[:, :, 0], 1.0 / 128, ALU.mult, [tsm], [tsm])
                            P.cp("dve", r12v[:, :, 1], r12v[:, :, 0], [tsm], [tsm])
                            P.ts("dve", r12v[:, :, 2], ss12v[:, :, 2], 1.0 / 64, ALU.mult, [tsm], [tsm])
                            rstd_from_ss(r12, 12, tsm, 1.0)
                            for cc in range(2):
                                P.tt("dve", qf[:, cc * 384:(cc + 1) * 384].rearrange("p (g d) -> p g d", d=64),
                                     banks[cc][:, 0:384].rearrange("p (g d) -> p g d", d=64),
                                     r12[:, cc * 6:(cc + 1) * 6].unsqueeze(2).to_broadcast([128, 6, 64]), ALU.mult, [tb[cc], tsm], [t_qf])
                            qfv = qf.rearrange("p (h d) -> p h d", h=4)
                            gq = gseg["q12"].rearrange("p (h d) -> p h d", h=4)
                            qmv = qm.rearrange("p (h d) -> p h d", h=4)
                            P.tt("dve", qmv[:, :, 0:128], qfv[:, :, 0:128], gq[:, :, 0:128], ALU.mult, [t_qf, t_gA], [t_qm])
                            qpev = qpe.rearrange("p (h d) -> p h d", h=4)
                            P.tt("pool", qpev, qfv[:, :, 128:192], gq[:, :, 128:192], ALU.mult, [t_qf, t_gA], [t_qpe])
                            rope(qpev, qmv[:, :, 128:192], 4, cs, t_qpe, t_qm, t_rope, rtmp, t_rtmp)
                            for cc in range(2):
                                kvv = banks[2 + cc][:, :].rearrange("p (h d) -> p h d", h=2)
                                P.act(sq[:, cc * 256:(cc + 1) * 256].rearrange("p (h d) -> p h d", h=2), kvv[:, :, 0:128], AF.Square, [tb[2 + cc]], [t_sq])
                                P.cp("act", Vv_v[:, tl, 2 * cc:2 * cc + 2, 0:128], kvv[:, :, 128:256], [tb[2 + cc], t_vinit], [t_kv[tl]])
                            ssk = smb[:, 44:48]
                            P.red(ssk, sq[:, 0:512].rearrange("p (h d) -> p h d", h=4), [t_sq], [tsm])
                            rstd_from_ss(ssk, 4, tsm, 1.0 / 128)
                            for cc in range(2):
                                kvv = banks[2 + cc][:, :].rearrange("p (h d) -> p h d", h=2)
                                P.tt("dve", kf[:, cc * 256:(cc + 1) * 256].rearrange("p (h d) -> p h d", h=2), kvv[:, :, 0:128],
                                     ssk[:, 2 * cc:2 * cc + 2].unsqueeze(2).to_broadcast([128, 2, 128]), ALU.mult, [tb[2 + cc], tsm], [t_kf])
                            P.tt("pool", kn, kf, gseg["k4"], ALU.mult, [t_kf, t_gA], [t_kn])
                            Tb = bankbf(6)
                            for h in range(4):
                                P.tr(Tb[:, h * 128:(h + 1) * 128], qmv[:, h, 0:128], identb, [t_qm, t_cb], [tb[6]])
                                P.tr(Tb[:, 512 + h * 128:512 + (h + 1) * 128], kn[:, h * 128:(h + 1) * 128], identb, [t_kn, t_cb], [tb[6]])
                            P.cp("act", qTn_v[:, :, j * 128:(j + 1) * 128], Tb[:, 0:512].rearrange("p (h t) -> p h t", h=4), [tb[6]], [t_qT])
                            P.cp("dve", kTn_v[:, :, tl * 128:(tl + 1) * 128], Tb[:, 512:1024].rearrange("p (h t) -> p h t", h=4), [tb[6]], [t_kv[tl]])
                            Tb = bankbf(7)
                            for h in range(4):
                                P.tr(Tb[0:64, h * 128:(h + 1) * 128], qmv[:, h, 128:192], identb, [t_qm, t_cb], [tb[7]])
                            P.tr(Tb[0:64, 512:640], rot[:, 0:64], identb, [t_rot, t_cb], [tb[7]])
                            P.cp("act", qTp_v[0:64, :, j * 128:(j + 1) * 128], Tb[0:64, 0:512].rearrange("p (h t) -> p h t", h=4), [tb[7]], [t_qT])
                            P.cp("dve", kTp[0:64, tl * 128:(tl + 1) * 128], Tb[0:64, 512:640], [tb[7]], [t_kv[tl]])
                        else:
                            P.act(sq[:, 0:1024], proj[:, 0:1024], AF.Square, [t_proj], [t_sq])
                            ss16 = smb[:, 4:20]
                            P.red(ss16, sq[:, 0:1024].rearrange("p (g d) -> p g d", d=64), [t_sq], [tsm])
                            rstd_from_ss(ss16, 16, tsm, 1.0 / 64)
                            P.tt("dve", tmpf.rearrange("p (g d) -> p g d", d=64), proj[:, 0:1024].rearrange("p (g d) -> p g d", d=64),
                                 ss16.unsqueeze(2).to_broadcast([128, 16, 64]), ALU.mult, [t_proj, tsm], [t_tmpf])
                            P.tt("pool", nrm, tmpf, gseg["dqk"], ALU.mult, [t_tmpf, t_gA], [t_nrm])
                            rope(nrm.rearrange("p (g d) -> p g d", d=64), rot.rearrange("p (g d) -> p g d", d=64), 16, cs,
                                 t_nrm, t_rot, t_rope, rtmp, t_rtmp)
                            P.cp("act", Vv_v[:, tl, :, 0:128], proj[:, 1024:1536].rearrange("p (h d) -> p h d", h=4), [t_proj, t_vinit], [t_kv[tl]])
                            Tb = bankbf(6)
                            for h in range(8):
                                P.tr(Tb[:, h * 128:(h + 1) * 128], rot[:, h * 128:(h + 1) * 128], identb, [t_rot, t_cb], [tb[6]])
                            P.cp("act", qTn_v[:, :, j * 128:(j + 1) * 128], Tb[:, 0:512].rearrange("p (h t) -> p h t", h=4), [tb[6]], [t_qT])
                            P.cp("dve", kTn_v[:, :, tl * 128:(tl + 1) * 128], Tb[:, 512:1024].rearrange("p (h t) -> p h t", h=4), [tb[6]], [t_kv[tl]])

                    nkb = 4 * c + 4
                    if stage_lim < 2:
                        continue
                    for h in range(4):
                        nmap = 1 if a1 else 2
                        for m in range(nmap):
                            ob = (2, 3) if m == 0 else (4, 5)
                            for kb in range(nkb):
                                jd = kb - 4 * c
                                q0 = 0 if jd < 0 else jd * 128
                                sb = kb % 2
                                if a1:
                                    P.mm(banks[sb][:, q0:512], kTn_v[:, h, kb * 128:(kb + 1) * 128], qTn_v[:, h, q0:512], True, False,
                                         [t_kv[kb], t_qT], [tb[sb]])
                                    P.mm(banks[sb][:, q0:512], kTp[0:64, kb * 128:(kb + 1) * 128], qTp_v[0:64, h, q0:512], False, True,
                                         [t_kv[kb], t_qT], [tb[sb]])
                                    scale = 192 ** -0.5
                                else:
                                    P.mm(banks[sb][:, q0:512], kTn_v[m * 64:(m + 1) * 64, h, kb * 128:(kb + 1) * 128],
                                         qTn_v[m * 64:(m + 1) * 64, h, q0:512], True, True, [t_kv[kb], t_qT], [tb[sb]])
                                    scale = 64 ** -0.5
                                pb = PT[pti % 3]; tpb = t_PT[pti % 3]; pti += 1
                                P.act(pb[:, q0:512], banks[sb][:, q0:512], AF.Exp, [tb[sb]], [tpb], scale=scale)
                                if jd >= 0:
                                    P.tt("pool", pb[:, q0:q0 + 128], pb[:, q0:q0 + 128], trib, ALU.mult, [tpb, t_cb], [tpb])
                                for qb in range(max(jd, 0), 4):
                                    bk = ob[qb // 2]
                                    ov = banks[bk][:, 0:260].rearrange("p (a d) -> p a d", a=2)
                                    P.mm(ov[:, qb % 2, 0:129], pb[:, qb * 128:(qb + 1) * 128], Vv_v[:, kb, h, 0:129],
                                         kb == 0 and qb % 2 == 0, kb == 4 * c + qb, [tpb, t_kv[kb]], [tb[bk]])
                        for qb in range(4):
                            if a1:
                                bk = (2, 3)[qb // 2]
                                ov = banks[bk][:, 0:260].rearrange("p (a d) -> p a d", a=2)
                                P.recip(osm[:, 0:1], ov[:, qb % 2, 128:129], [tb[bk]], [t_osm])
                                P.ts("dve", mixed_v[:, qb, h * 128:(h + 1) * 128], ov[:, qb % 2, 0:128], osm[:, 0:1], ALU.mult, [tb[bk], t_osm], [t_mixed[qb]])
                            else:
                                b1 = (2, 3)[qb // 2]
                                b2 = (4, 5)[qb // 2]
                                o1 = banks[b1][:, 0:260].rearrange("p (a d) -> p a d", a=2)
                                o2 = banks[b2][:, 0:260].rearrange("p (a d) -> p a d", a=2)
                                P.recip(osm[:, 0:1], o1[:, qb % 2, 128:129], [tb[b1]], [t_osm])
                                P.recip(osm[:, 1:2], o2[:, qb % 2, 128:129], [tb[b2]], [t_osm])
                                P.ts("dve", oa, o2[:, qb % 2, 0:128], osm[:, 1:2], ALU.mult, [tb[b2], t_osm, t_lam], [t_oa], s2=neglam[:, 0:1], op1=ALU.mult)
                                P.stt(od, o1[:, qb % 2, 0:128], osm[:, 0:1], oa, ALU.mult, ALU.add, [tb[b1], t_osm, t_oa], [t_od])
                                P.act(oa, od, AF.Square, [t_od], [t_oa, t_osm], accum=osm[:, 2:3])
                                rstd_from_ss(osm[:, 2:3], 1, t_osm, 1.0 / 128)
                                P.stt(mixed_v[:, qb, h * 128:(h + 1) * 128], od, osm[:, 2:3], gsub, ALU.mult, ALU.mult, [t_od, t_osm, t_lam], [t_mixed[qb]])
                    if stage_lim < 3:
                        continue
                    for qb in range(4):
                        tl = c * 4 + qb
                        row0 = s * S + tl * 128
                        gt = row0 // 128
                        Tb = bankbf(6)
                        for k in range(4):
                            P.tr(Tb[:, k * 128:(k + 1) * 128], mixed_v[:, qb, k * 128:(k + 1) * 128], identb, [t_mixed[qb], t_cb], [tb[6]])
                        P.cp("act", mT, Tb[:, 0:512], [tb[6]], [t_mT])
                        xrb = xr[qb % 2]; txr = t_xr[qb % 2]
                        if a1:
                            P.ld("sp", xrb, x_d[row0:row0 + 128, :], txr)
                        else:
                            P.ld("sp", xrb, x1_d[row0:row0 + 128, :], txr, reads=[tx1[gt]])
                        for cc in range(2):
                            bk = 7 if cc == 0 else 6
                            bk = (7, 1)[cc]
                            for k in range(4):
                                P.mm(banks[bk][:, :], mT[:, k * 128:(k + 1) * 128],
                                     wout_b[:, k * 1024 + cc * 512:k * 1024 + (cc + 1) * 512],
                                     k == 0, k == 3, [t_mT, t_w], [tb[bk]])
                            P.tt("dve", ytmp[:, cc * 512:(cc + 1) * 512], banks[bk][:, :], GATE1[:, cc * 512:(cc + 1) * 512], ALU.mult, [tb[bk], t_mod], [t_ytmp])
                        P.tt("pool", xrb, xrb, ytmp, ALU.add, [txr, t_ytmp], [txr])
                        P.ld("sp", x1_d[row0:row0 + 128, :], xrb, txr, reads=[txr], writes=[tx1[gt]])

    if do_b:
        A.off = base_off
        P.barrier()
        NTT = TT // 128
        gB = A.f32(1024)
        t_gB = P.tok("gB")
        P.ld("sp", gB, gains_d[:, G_N2:G_N2 + 1024], t_gB)
        pwq_b = A.bf16(8 * 2048)
        keys_b = A.bf16(16 * 128)
        t_wB = P.tok("wB")
        wst = [A.f32(4096), A.f32(4096)]
        t_wst = P.toks("wstB", 2)
        stage = wst
        t_stage = t_wst
        ei = 0
        for k in range(8):
            load_cast_weight(pwq_b[:, k * 2048:(k + 1) * 2048], pwq_d[k * 128:(k + 1) * 128, :], 2048, stage, t_stage, t_wB, k, ei); ei += 1
        load_cast_weight(keys_b, keys_d.rearrange("p c n -> p (c n)"), 2048, stage, t_stage, t_wB, 0, ei); ei += 1
        G2 = A.f32(1024)
        SH2 = A.f32(1024)
        GATE2 = [A.f32(1024) for _ in range(NSEQ)]
        t_mod2 = P.tok("mod2")
        t_gate2 = P.toks("gate2", NSEQ)
        mtmp = [A.f32(512), A.f32(512)]
        t_mtmp = P.toks("mtmpB", 2)
        xt = [A.f32(1024), A.f32(1024)]
        t_xt = P.toks("xB", 2)
        junk = A.bf16(1024)
        t_junk = P.tok("junkB")
        sm = A.f32(64)
        t_sm = P.tok("smB")
        h2f = A.f32(1024)
        t_h2f = P.tok("h2f")
        h2b = [A.bf16(1024), A.bf16(1024)]
        t_h2b = P.toks("h2b", 2)
        h2T = A.bf16(1024)
        t_h2T = P.tok("h2T")
        qT = A.bf16(16 * 128)
        t_qTB = P.tok("qTB")
        sc = wst[0][:, 0:2048]
        t_sc = t_wst[0]
        sc2 = [A.f32(128) for _ in range(4)]
        t_sc2 = P.toks("sc2", 4)
        sv = A.f32(256)
        t_svc = P.toks("svc", 16)
        si = A.u32(256)
        t_sic = P.toks("sic", 16)
        sif = A.f32(256)
        t_sif = P.tok("sif")
        cand = wst[0][:, 2048:4096]
        t_cand = t_wst[0]
        cand2 = [A.f32(256) for _ in range(4)]
        t_cand2 = P.toks("cand2", 4)
        fv = A.f32(128)
        t_fvh = P.toks("fvh", 8)
        fpos = A.u32(128)
        t_fph = P.toks("fph", 8)
        k0f = A.f32(128)
        k1f = A.f32(128)
        t_k = P.tok("k01")
        oh = wst[1][:, 0:2048]
        t_oh = t_wst[1]
        e0 = A.f32(128)
        e1 = A.f32(128)
        t_e = P.tok("e01")
        eidx = [A.u32(128), A.u32(128)]
        t_eidx = P.toks("eidx", 2)
        gw = [A.f32(128), A.f32(128)]
        t_gw = P.toks("gw", 2)
        araw = A.f32(128)
        cg = A.f32(128)
        NR = 8
        t_ar = P.toks("araw", NR)
        t_cf = P.toks("cg", NR)
        NG = 10
        uvb = [A.bf16(2048) for _ in range(NG)]
        t_uvb = P.toks("uvb", NG)
        dg = [A.bf16(128) for _ in range(4)]
        t_dg = P.toks("dg", 4)
        ujunk = A.bf16(1024)
        t_ujunk = P.tok("ujunk")
        yout = A.f32(1024)
        t_yout = P.tok("yout")
        src = x1_d if (do_a1 or do_a2) else x_d

        def pre(gt):
            s, tl = divmod(gt, NT)
            par = gt % 2
            row0 = gt * 128
            if tl == 0:
                compute_mod(s, [6, 7, 8, 9, 10, 11],
                            [SH2[:, 0:512], SH2[:, 512:1024], mtmp[0], mtmp[1], GATE2[s][:, 0:512], GATE2[s][:, 512:1024]],
                            [t_mod2, t_mod2, t_mtmp[0], t_mtmp[1], t_gate2[s], t_gate2[s]], wst, t_wst, [7, 0])
                for i in range(2):
                    P.stt(G2[:, i * 512:(i + 1) * 512], mtmp[i], 1.0, gB[:, i * 512:(i + 1) * 512], ALU.add, ALU.mult, [t_mtmp[i], t_gB], [t_mod2])
                yield
            xb = xt[par]; txb = t_xt[par]
            hb = h2b[par]; thb = t_h2b[par]
            P.ld("sp", xb, src[row0:row0 + 128, :], txb, reads=[tx1[gt]])
            P.act(junk, xb, AF.Square, [txb], [t_junk, t_sm], accum=sm[:, 0:1])
            rstd_from_ss(sm[:, 0:1], 1, t_sm, 1.0 / 1024)
            P.stt(h2f, xb, sm[:, 0:1], G2, ALU.mult, ALU.mult, [txb, t_sm, t_mod2], [t_h2f])
            P.tt("dve", hb, h2f, SH2, ALU.add, [t_h2f, t_mod2], [thb])
            yield
            Tb = bankbf(6)
            for k in range(8):
                P.tr(Tb[:, k * 128:(k + 1) * 128], hb[:, k * 128:(k + 1) * 128], identb, [thb, t_cb], [tb[6]])
            P.cp("act", h2T, Tb, [tb[6]], [t_h2T])
            yield
            for g4 in range(4):
                bk = (7, 0)[g4 % 2]
                for cc in range(4):
                    c16 = g4 * 4 + cc
                    for k in range(8):
                        P.mm(banks[bk][:, cc * 128:(cc + 1) * 128], pwq_b[:, k * 2048 + c16 * 128:k * 2048 + (c16 + 1) * 128],
                             h2T[:, k * 128:(k + 1) * 128], k == 0 and cc == 0, k == 7, [t_wB, t_h2T], [tb[bk]])
                P.cp("act", qT[:, g4 * 512:(g4 + 1) * 512], banks[bk][:, :], [tb[bk]], [t_qTB])
                yield
            for g4 in range(4):
                bk = (1, 7)[g4 % 2]
                for cc in range(4):
                    c16 = g4 * 4 + cc
                    P.mm(banks[bk][:, cc * 128:(cc + 1) * 128], qT[:, c16 * 128:(c16 + 1) * 128], keys_b[:, c16 * 128:(c16 + 1) * 128],
                         cc == 0, True, [t_qTB, t_wB], [tb[bk]])
                P.cp("act", sc[:, g4 * 512:(g4 + 1) * 512], banks[bk][:, :], [tb[bk]], [t_sc])
                yield
            svv = sv.rearrange("p (c k) -> p c k", c=16)
            siv = si.rearrange("p (c k) -> p c k", c=16)
            f_max = lambda o, i_: (lambda e: e.max(out=o, in_=i_))
            f_mr = lambda o, r, i_: (lambda e: e.match_replace(out=o, in_to_replace=r, in_values=i_, imm_value=-1e30))
            f_mi = lambda o, m_, i_: (lambda e: e.max_index(out=o, in_max=m_, in_values=i_))
            for g in range(4):
                cs4 = range(4 * g, 4 * g + 4)
                for c16 in cs4:
                    P.op("dve", f_max(svv[:, c16, 0:8], sc[:, c16 * 128:(c16 + 1) * 128]), [t_sc], [t_svc[c16]])
                for c16 in cs4:
                    P.op("dve", f_mr(sc2[c16 % 4], svv[:, c16, 0:8], sc[:, c16 * 128:(c16 + 1) * 128]), [t_sc, t_svc[c16]], [t_sc2[c16 % 4]])
                for c16 in cs4:
                    P.op("dve", f_max(svv[:, c16, 8:16], sc2[c16 % 4]), [t_sc2[c16 % 4]], [t_svc[c16]])
                for c16 in cs4:
                    P.op("dve", f_mi(siv[:, c16, 0:8], svv[:, c16, 0:8], sc[:, c16 * 128:(c16 + 1) * 128]), [t_sc, t_svc[c16]], [t_sic[c16]])
                for c16 in cs4:
                    P.op("dve", f_mi(siv[:, c16, 8:16], svv[:, c16, 8:16], sc[:, c16 * 128:(c16 + 1) * 128]), [t_sc, t_svc[c16]], [t_sic[c16]])
                yield
            P.cp("dve", sif, si, t_sic, [t_sif])
            sv4 = sv.rearrange("p (h t k) -> p h t k", h=8, t=2)
            candv = cand.rearrange("p (h a b) -> p h a b", h=8, a=16)
            P.tt("dve", candv, sv4[:, :, 0, :].unsqueeze(3).to_broadcast([128, 8, 16, 16]),
                 sv4[:, :, 1, :].unsqueeze(2).to_broadcast([128, 8, 16, 16]), ALU.add, t_svc, [t_cand])
            yield
            fvv = fv.rearrange("p (h k) -> p h k", h=8)
            fpv = fpos.rearrange("p (h k) -> p h k", h=8)
            for g in range(2):
                hs4 = range(4 * g, 4 * g + 4)
                for h in hs4:
                    P.op("dve", f_max(fvv[:, h, 0:8], cand[:, h * 256:(h + 1) * 256]), [t_cand], [t_fvh[h]])
                for h in hs4:
                    P.op("dve", f_mr(cand2[h % 4], fvv[:, h, 0:8], cand[:, h * 256:(h + 1) * 256]), [t_cand, t_fvh[h]], [t_cand2[h % 4]])
                for h in hs4:
                    P.op("dve", f_max(fvv[:, h, 8:16], cand2[h % 4]), [t_cand2[h % 4]], [t_fvh[h]])
                for h in hs4:
                    P.op("dve", f_mi(fpv[:, h, 0:8], fvv[:, h, 0:8], cand[:, h * 256:(h + 1) * 256]), [t_cand, t_fvh[h]], [t_fph[h]])
                for h in hs4:
                    P.op("dve", f_mi(fpv[:, h, 8:16], fvv[:, h, 8:16], cand[:, h * 256:(h + 1) * 256]), [t_cand, t_fvh[h]], [t_fph[h]])
                yield
            P.cp("dve", k1f, fpos, t_fph, [t_k])
            P.tt("dve", oh.rearrange("p (a b) -> p a b", b=16), k1f.unsqueeze(2).to_broadcast([128, 128, 16]),
                 thr16.unsqueeze(1).to_broadcast([128, 128, 16]), ALU.is_ge, [t_k, t_consts], [t_oh])
            P.red(k0f, oh.rearrange("p (a b) -> p a b", b=16), [t_oh], [t_k])
            P.stt(k1f, k0f, -16.0, k1f, ALU.mult, ALU.add, [t_k], [t_k])
            yield
            sif4 = sif.rearrange("p (h t k) -> p h t k", h=8, t=2)
            for t_, (kf_, e_) in enumerate(((k0f, e0), (k1f, e1))):
                P.tt("dve", oh.rearrange("p (a b) -> p a b", b=16), kf_.unsqueeze(2).to_broadcast([128, 128, 16]),
                     iota16.unsqueeze(1).to_broadcast([128, 128, 16]), ALU.is_equal, [t_k, t_consts], [t_oh])
                P.tt("dve", oh.rearrange("p (h a b) -> p h a b", h=8, a=16), oh.rearrange("p (h a b) -> p h a b", h=8, a=16),
                     sif4[:, :, t_, :].unsqueeze(2).to_broadcast([128, 8, 16, 16]), ALU.mult, [t_oh, t_sif], [t_oh])
                P.red(e_, oh.rearrange("p (a b) -> p a b", b=16), [t_oh], [t_e])
                yield
            P.stt(e0, e0, 128.0, e1, ALU.mult, ALU.add, [t_e], [t_e])
            P.cp("dve", eidx[par], e0, [t_e], [t_eidx[par]])
            gwp = gw[par]; tgw = t_gw[par]
            P.tt("dve", gwp.rearrange("p (h k) -> p h k", h=8), fvv, fvv[:, :, 0:1].to_broadcast([128, 8, 16]), ALU.subtract, t_fvh, [tgw])
            P.act(gwp, gwp, AF.Exp, [tgw], [tgw])
            P.red(sm[:, 8:16], gwp.rearrange("p (h k) -> p h k", h=8), [tgw], [t_sm])
            P.recip(sm[:, 8:16], sm[:, 8:16], [t_sm], [t_sm])
            P.tt("dve", gwp.rearrange("p (h k) -> p h k", h=8), gwp.rearrange("p (h k) -> p h k", h=8),
                 sm[:, 8:16].unsqueeze(2).to_broadcast([128, 8, 16]), ALU.mult, [tgw, t_sm], [tgw])
            yield

        cnt = {"gi": 0, "di": 0}

        def slot_front(gt, sl):
            par = gt % 2
            gi = cnt["gi"]; cnt["gi"] += 1
            ub = uvb[gi % NG]; tub = t_uvb[gi % NG]
            P.dma("pool", (lambda o, ix: (lambda e: e.indirect_dma_start(out=o, out_offset=None, in_=uv_d[:, :],
                  in_offset=bass.IndirectOffsetOnAxis(ap=ix, axis=0))))(ub, eidx[par][:, sl:sl + 1]), tub, reads=[t_eidx[par], t_uv], writes=[tub])
            tar = t_ar[sl % NR]; tcf = t_cf[sl % NR]
            P.stt(ujunk, ub[:, 0:1024], 1.0, h2b[par], ALU.mult, ALU.mult, [tub, t_h2b[par]], [t_ujunk, tar], accum=araw[:, sl:sl + 1])
            P.act(cg[:, sl:sl + 1], araw[:, sl:sl + 1], AF.Gelu, [tar], [tcf])
            return (gt, sl, ub, tub)

        def slot_back(gt, sl, ub, tub):
            par = gt % 2
            yb = (2, 3) if par == 0 else (4, 5)
            tcf = t_cf[sl % NR]
            di = cnt["di"]; cnt["di"] += 1
            db = dg[di % 4]; tdb = t_dg[di % 4]
            P.ts("dve", db, identb, cg[:, sl:sl + 1], ALU.mult, [t_cb, tcf, t_gw[par]], [tdb], s2=gw[par][:, sl:sl + 1], op1=ALU.mult)
            for cc in range(2):
                bk = yb[cc]
                P.mm(banks[bk][:, :], db, ub[:, 1024 + cc * 512:1024 + (cc + 1) * 512], sl == 0, sl == 127, [tdb, tub], [tb[bk]])
            if sl == 127:
                fin(gt)

        def fin(gt):
            s, tl = divmod(gt, NT)
            par = gt % 2
            yb = (2, 3) if par == 0 else (4, 5)
            row0 = gt * 128
            for cc in range(2):
                bk = yb[cc]
                P.tt("dve", yout[:, cc * 512:(cc + 1) * 512], banks[bk][:, :], GATE2[s][:, cc * 512:(cc + 1) * 512], ALU.mult, [tb[bk], t_gate2[s]], [t_yout])
            P.tt("pool", yout, yout, xt[par], ALU.add, [t_yout, t_xt[par]], [t_yout])
            P.ld("sp", out_d[row0:row0 + 128, :], yout, t_yout, reads=[t_yout], writes=[], final=True)

        for _ in pre(0):
            pass
        SKEW = 3
        pend = []
        for gt in range(NTT):
            gen = pre(gt + 1) if gt + 1 < NTT else None
            for sl in range(128):
                pend.append(slot_front(gt, sl))
                if len(pend) > SKEW:
                    slot_back(*pend.pop(0))
                if gen is not None and sl % 2 == 1 and sl >= 8:
                    next(gen, None)
            if gen is not None:
                for _ in gen:
                    pass
        while pend:
            slot_back(*pend.pop(0))
    else:
        A.off = base_off
        P.barrier()
        cb = [A.f32(1024), A.f32(1024)]
        t_cbuf = P.toks("cpb", 2)
        for gt in range(TT // 128):
            P.ld("sp", cb[gt % 2], x1_d[gt * 128:(gt + 1) * 128, :], t_cbuf[gt % 2], reads=[tx1[gt]])
            P.ld("sp", out_d[gt * 128:(gt + 1) * 128, :], cb[gt % 2], t_cbuf[gt % 2], reads=[t_cbuf[gt % 2]], writes=[], final=True)

    P.arena_hw = A.hw
    P.emit()
    return nc, P


def rope_table(S):
    half = 32
    inv = (1.0 / (10000.0 ** (np.arange(half, dtype=np.float32) / np.float32(half)))).astype(np.float32)
    ang = np.arange(S, dtype=np.float32)[:, None] * inv[None, :]
    cs = np.concatenate([np.cos(ang), np.sin(ang)], axis=-1).astype(np.float32)
    return np.ascontiguousarray(cs.reshape(S // 128, 128, 64).transpose(1, 0, 2))


def make_consts():
    c = np.zeros((128, C_TOT), np.float32)
    c[:, C_ID:C_ID + 128] = np.eye(128, dtype=np.float32)
    k = np.arange(128)
    c[:, C_TRI:C_TRI + 128] = (k[:, None] <= k[None, :]).astype(np.float32)
    c[:, C_IOTA:C_IOTA + 16] = np.arange(16, dtype=np.float32)[None, :]
    c[:, C_ONES:C_ONES + 128] = 1.0
    c[:, C_THR:C_THR + 15] = 16.0 * np.arange(1, 16, dtype=np.float32)[None, :]
    c[:, C_THR + 15] = 1e9
    return c


def host_layout(inp, S, NSEQ, ncores):
    f = lambda a: np.ascontiguousarray(np.asarray(a, dtype=np.float32))
    x = f(inp["x"])
    c = f(inp["c"])
    rep = lambda v: np.broadcast_to(f(v).reshape(1, -1), (128, f(v).size))
    mqg = f(inp["mla_q_g"])[0]
    mkg = f(inp["mla_k_g"])[0]
    gains = np.concatenate([
        rep(inp["norm1_g"][0]), rep(inp["mla_q_lat_g"][0]), rep(inp["mla_kv_lat_g"][0]),
        rep(mkg[128:192]),
        rep(np.tile(f(inp["diff_q_g"])[0], 8)), rep(np.tile(f(inp["diff_k_g"])[0], 8)),
        rep(np.tile(mqg, 4)), rep(np.tile(mkg[:128], 4)),
        rep(inp["diff_subln_g"][0]), rep(inp["norm2_g"][0]),
        rep(inp["diff_lq1"][0]), rep(inp["diff_lk1"][0]), rep(inp["diff_lq2"][0]), rep(inp["diff_lk2"][0]),
    ], axis=1)
    gains = np.ascontiguousarray(gains, dtype=np.float32)
    assert gains.shape[1] == G_TOT
    keysT = np.ascontiguousarray(f(inp["peer_sub_keys"])[0].reshape(16, 128, 128).transpose(2, 0, 1))
    shared = {
        "ada_w": f(inp["ada_w"])[0], "ada_b": f(inp["ada_b"])[0].reshape(1, -1),
        "w_in": f(inp["w_in"])[0], "w_q_up": f(inp["mla_w_q_up"])[0], "w_kv_up": f(inp["mla_w_kv_up"])[0],
        "w_out": f(inp["w_out"])[0], "peer_w_q": f(inp["peer_w_q"])[0], "keysT": keysT,
        "peer_u": f(inp["peer_u"])[0], "peer_v": f(inp["peer_v"])[0],
        "gains": gains, "rope": rope_table(S), "consts": make_consts(),
    }
    maps = []
    for i in range(ncores):
        xs = np.ascontiguousarray(x[i * NSEQ:(i + 1) * NSEQ].reshape(NSEQ * S, D))
        cs = c[i * NSEQ:(i + 1) * NSEQ]
        cT = np.ascontiguousarray(cs.reshape(NSEQ, 8, 128).transpose(2, 0, 1))
        m = dict(shared)
        m["x"] = xs
        m["cT"] = cT
        maps.append(m)
    return maps


_CACHE = {}


def kernel(**inputs):
    B, S, _ = inputs["x"].shape
    ncores = 8
    NSEQ = B // ncores
    key = (S, NSEQ)
    if key not in _CACHE:
        _CACHE[key] = build(S, NSEQ)[0]
    nc = _CACHE[key]
    maps = host_layout(inputs, S, NSEQ, ncores)
    res = run_bass_kernel_spmd(nc, maps, core_ids=list(range(ncores)))
    out = np.stack([r["out"].reshape(NSEQ, S, D) for r in res.results], axis=0).reshape(B, S, D)
    return out.astype(np.float32)
```

```python
import math
import numpy as np
from contextlib import ExitStack
import concourse.bass as bass
import concourse.mybir as mybir
from concourse.bass_utils import run_bass_kernel_spmd

F32 = mybir.dt.float32
BF16 = mybir.dt.bfloat16
U32 = mybir.dt.uint32
I32 = mybir.dt.int32
ALU = mybir.AluOpType
AF = mybir.ActivationFunctionType
AX = mybir.AxisListType

ENGS = ("pe", "act", "dve", "pool", "sp")
CHUNK = 16000
EPS = 1e-6
D = 1024
NEXP = 16384


class Tok:
    __slots__ = ("name", "w", "r", "dsem", "dcount", "excl")

    def __init__(self, name):
        self.name = name
        self.excl = False
        self.w = None
        self.r = {}
        self.dsem = None
        self.dcount = 0


class Prog:
    def __init__(self, nc):
        self.nc = nc
        self.es = ExitStack()
        self.ops = {e: [] for e in ENGS}
        self.cnt = {e: 0 for e in ENGS}
        self.esems = {e: [] for e in ENGS}
        self.seen = {e: {} for e in ENGS}
        self.semobj = {}
        self.nsem = 0
        self.final_events = []
        self.dma_toks = []
        self.nins = 0

    def sbuf(self, name, shape, dt):
        return self.es.enter_context(self.nc.sbuf_tensor(name, list(shape), dt))

    def psum(self, name, shape, dt):
        return self.es.enter_context(self.nc.psum_tensor(name, list(shape), dt))

    def newsem(self, name):
        self.nsem += 1
        return self.es.enter_context(self.nc.semaphore(name))

    def tok(self, name):
        return Tok(name)

    def toks(self, name, n):
        return [Tok(f"{name}{i}") for i in range(n)]

    def _eng_event(self, eng):
        c = self.cnt[eng]
        ch, v = divmod(c, CHUNK)
        while len(self.esems[eng]) <= ch:
            s = self.newsem(f"e_{eng}_{len(self.esems[eng])}")
            self.esems[eng].append(s)
            self.semobj[(eng, len(self.esems[eng]) - 1)] = s
        self.cnt[eng] = c + 1
        return ((eng, ch), v + 1)

    def _collect(self, eng, reads, writes, same_eng_sync):
        deps = {}

        def add(ev):
            if ev is None:
                return
            k, v = ev
            if deps.get(k, 0) < v:
                deps[k] = v

        for b in reads:
            add(b.w)
        for b in writes:
            add(b.w)
            for k, v in b.r.items():
                add((k, v))
        waits = []
        seen = self.seen[eng]
        for k, v in deps.items():
            if (not same_eng_sync) and k[0] == eng:
                continue
            if seen.get(k, 0) >= v:
                continue
            seen[k] = v
            waits.append((k, v))
        return waits

    def _record(self, ev, reads, writes):
        k, v = ev
        for b in reads:
            if b.r.get(k, 0) < v:
                b.r[k] = v
        for b in writes:
            b.w = ev
            b.r = {}

    def op(self, eng, fn, reads=(), writes=(), sync_same=None):
        if sync_same is None:
            sync_same = eng != "pe"
        ex = [b for b in reads if b.excl]
        if ex:
            writes = list(writes) + ex
        waits = self._collect(eng, reads, writes, sync_same)
        ev = self._eng_event(eng)
        self._record(ev, reads, writes)
        self.ops[eng].append((waits, fn, ev, 1))
        self.nins += 1

    def dma(self, eng, fn, owner, reads=(), writes=(), final=False):
        waits = self._collect(eng, reads, writes, False)
        if owner.dsem is None:
            owner.dsem = ("dma", self.nsem)
            self.semobj[owner.dsem] = self.newsem(f"d{self.nsem}")
            self.dma_toks.append(owner)
        owner.dcount += 16
        ev = (owner.dsem, owner.dcount)
        self._record(ev, reads, writes)
        self.ops[eng].append((waits, fn, ev, 16))
        if final:
            self.final_events.append(ev)
        self.nins += 1

    def barrier(self):
        evs = {}
        for e in ENGS:
            c = self.cnt[e]
            if c == 0:
                continue
            ch, v = divmod(c - 1, CHUNK)
            evs[(e, ch)] = v + 1
        for t in self.dma_toks:
            evs[t.dsem] = t.dcount
        for e in ENGS:
            waits = []
            for k, v in evs.items():
                if k[0] == e:
                    continue
                if self.seen[e].get(k, 0) >= v:
                    continue
                self.seen[e][k] = v
                waits.append((k, v))
            self.ops[e].append((waits, None, None, 0))

    def emit(self):
        nc = self.nc
        fw = {}
        for k, v in self.final_events:
            fw[k] = max(fw.get(k, 0), v)
        self.ops["sp"].append((list(fw.items()), None, None, 0))
        semobj = self.semobj

        def run(engobj, lst):
            for waits, fn, ev, inc in lst:
                for k, v in waits:
                    engobj.wait_ge(semobj[k], v)
                if fn is None:
                    continue
                ins = fn(engobj)
                ins.then_inc(semobj[ev[0]], inc)

        with nc.Block() as block:
            @block.tensor
            def _(e):
                run(e, self.ops["pe"])

            @block.scalar
            def _(e):
                run(e, self.ops["act"])

            @block.vector
            def _(e):
                run(e, self.ops["dve"])

            @block.gpsimd
            def _(e):
                run(e, self.ops["pool"])

            @block.sync
            def _(e):
                run(e, self.ops["sp"])
        self.es.close()

    def mm(self, out, lhsT, rhs, start, stop, reads, writes):
        self.op("pe", lambda e: e.matmul(out, lhsT=lhsT, rhs=rhs, start=start, stop=stop), reads, writes)

    def tr(self, out, in_, ident, reads, writes):
        self.op("pe", lambda e: e.transpose(out=out, in_=in_, identity=ident), reads, writes)

    def act(self, out, in_, func, reads, writes, bias=None, scale=None, accum=None):
        kw = {}
        if bias is not None:
            kw["bias"] = bias
        if scale is not None:
            kw["scale"] = scale
        if accum is not None:
            kw["accum_out"] = accum
        self.op("act", lambda e: e.activation(out=out, in_=in_, func=func, **kw), reads, writes)

    def tt(self, eng, out, in0, in1, op, reads, writes):
        self.op(eng, lambda e: e.tensor_tensor(out=out, in0=in0, in1=in1, op=op), reads, writes)

    def ts(self, eng, out, in0, s1, op0, reads, writes, s2=None, op1=None):
        if op1 is None:
            self.op(eng, lambda e: e.tensor_scalar(out=out, in0=in0, scalar1=s1, scalar2=None, op0=op0), reads, writes)
        else:
            self.op(eng, lambda e: e.tensor_scalar(out=out, in0=in0, scalar1=s1, scalar2=s2, op0=op0, op1=op1), reads, writes)

    def stt(self, out, in0, scalar, in1, op0, op1, reads, writes, accum=None):
        if accum is None:
            self.op("dve", lambda e: e.scalar_tensor_tensor(out=out, in0=in0, scalar=scalar, in1=in1, op0=op0, op1=op1), reads, writes)
        else:
            self.op("dve", lambda e: e.scalar_tensor_tensor(out=out, in0=in0, scalar=scalar, in1=in1, op0=op0, op1=op1, accum_out=accum), reads, writes)

    def red(self, out, in_, reads, writes, op=ALU.add):
        self.op("dve", lambda e: e.tensor_reduce(out=out, in_=in_, axis=AX.X, op=op), reads, writes)

    def cp(self, eng, out, in_, reads, writes):
        if eng == "act":
            self.op("act", lambda e: e.copy(out=out, in_=in_), reads, writes)
        else:
            self.op(eng, lambda e: e.tensor_copy(out=out, in_=in_), reads, writes)

    def recip(self, out, in_, reads, writes):
        self.op("dve", lambda e: e.reciprocal(out=out, in_=in_), reads, writes)

    def memset(self, eng, ap, val, writes):
        self.op(eng, lambda e: e.memset(ap, val), (), writes)

    def ld(self, eng, out, in_, owner, reads=(), writes=None, final=False):
        if writes is None:
            writes = [owner]
        self.dma(eng, lambda e: e.dma_start(out=out, in_=in_), owner, reads, writes, final)


class Arena:
    def __init__(self, ap, words):
        self.ap = ap
        self.words = words
        self.off = 0

    def f32(self, n):
        n = (n + 1) // 2 * 2
        a = self.ap[:, self.off:self.off + n]
        self.off += n
        self.hw = max(getattr(self, "hw", 0), self.off)
        assert self.off <= self.words, f"arena overflow {self.off} > {self.words}"
        return a

    def bf16(self, n):
        w = (n + 1) // 2
        return self.f32(w).bitcast(BF16)[:, 0:n]

    def u32(self, n):
        return self.f32(n).bitcast(U32)


G_N1 = 0
G_QLAT = G_N1 + 1024
G_KVLAT = G_QLAT + 384
G_KPE = G_KVLAT + 256
G_DQK = G_KPE + 64
G_Q12 = G_DQK + 1024
G_K4 = G_Q12 + 768
G_SUB = G_K4 + 512
G_N2 = G_SUB + 128
G_LQ = G_N2 + 1024
G_TOT = G_LQ + 256

C_ID = 0
C_TRI = 128
C_IOTA = 256
C_ONES = 272
C_THR = 400
C_TOT = 416

LAMBDA_INIT = 0.8 - 0.6 * math.exp(-0.3 * 0)


def build(S, NSEQ, do_a1=True, do_a2=True, do_b=True, stage_lim=9):
    NT = S // 128
    NCH = S // 512
    TT = NSEQ * S
    nc = bass.Bass("TRN2", target_bir_lowering=False)
    P = Prog(nc)

    def din(name, shape, dt=F32):
        return nc.dram_tensor(name, list(shape), dt, kind="ExternalInput").ap()

    x_d = din("x", [TT, D])
    cT_d = din("cT", [128, NSEQ, 8])
    adaw_d = din("ada_w", [D, 6 * D])
    adab_d = din("ada_b", [1, 6 * D])
    win_d = din("w_in", [D, 2240])
    wq_d = din("w_q_up", [384, 768])
    wkv_d = din("w_kv_up", [256, 1024])
    wout_d = din("w_out", [D, D])
    pwq_d = din("peer_w_q", [D, 2048])
    keys_d = din("keysT", [128, 16, 128])
    u_d = din("peer_u", [NEXP, D])
    v_d = din("peer_v", [NEXP, D])
    gains_d = din("gains", [128, G_TOT])
    rope_d = din("rope", [128, NT, 64])
    consts_d = din("consts", [128, C_TOT])
    out_d = nc.dram_tensor("out", [TT, D], F32, kind="ExternalOutput").ap()
    x1_d = nc.dram_tensor("x1s", [TT, D], F32, kind="Internal").ap()
    tx1 = [P.tok(f"x1d{i}") for i in range(TT // 128)]

    AW = 53200
    arena_t = P.sbuf("arena", [128, AW], F32)
    A = Arena(arena_t, AW)
    banks = [P.psum(f"pb{i}", [128, 512], F32) for i in range(8)]
    tb = P.toks("bank", 8)
    for t_ in tb:
        t_.excl = True

    def bankbf(i):
        return banks[i][:, :].bitcast(BF16)

    consts = A.f32(C_TOT)
    t_consts = P.tok("consts")
    P.ld("sp", consts, consts_d[:, :], t_consts)
    ident_f = consts[:, C_ID:C_ID + 128]
    iota16 = consts[:, C_IOTA:C_IOTA + 16]
    ones_f = consts[:, C_ONES:C_ONES + 128]
    thr16 = consts[:, C_THR:C_THR + 16]
    identb = A.bf16(128)
    trib = A.bf16(128)
    t_cb = P.tok("constsb")
    P.cp("dve", identb, ident_f, [t_consts], [t_cb])
    P.cp("dve", trib, consts[:, C_TRI:C_TRI + 128], [t_consts], [t_cb])
    cT = A.f32(NSEQ * 8)
    t_cT = P.tok("cT")
    P.ld("sp", cT, cT_d.rearrange("p b k -> p (b k)"), t_cT)
    adab = [A.f32(512), A.f32(512)]
    t_adab = P.toks("adab", 2)
    base_off = A.off

    def small(n):
        return A.f32(n)

    def load_cast_weight(dst_bf, src_dram_rows, ncols, stage, t_stage, t_dst, k, eng_i):
        st = stage[eng_i % len(stage)]
        ts_ = t_stage[eng_i % len(stage)]
        P.ld("sp", st[:, 0:ncols], src_dram_rows, ts_)
        eng = ("act", "dve", "pool")[eng_i % 3]
        P.cp(eng, dst_bf, st[:, 0:ncols], [ts_], [t_dst])

    def compute_mod(b, chunks, dests, t_dests, wst, t_wst, mbank):
        sil = small(8)
        crep = A.f32(8 * 128)
        t_sil = P.tok("sil")
        t_crep = P.tok("crep")
        P.act(sil, cT[:, b * 8:(b + 1) * 8], AF.Silu, [t_cT], [t_sil])
        P.cp("dve", crep.rearrange("p (k m) -> p k m", k=8), sil.unsqueeze(2).to_broadcast([128, 8, 128]), [t_sil], [t_crep])
        adaw_v = adaw_d.rearrange("(k p) n -> p k n", p=128)
        for i, ch in enumerate(chunks):
            w = wst[i % 2]
            tw = t_wst[i % 2]
            P.ld("sp", w.rearrange("p (k n) -> p k n", k=8), adaw_v[:, :, ch * 512:(ch + 1) * 512], tw)
            bk = mbank[i % 2]
            P.ld("sp", adab[i % 2][0:1, :], adab_d[:, ch * 512:(ch + 1) * 512], t_adab[i % 2])
            for k in range(8):
                P.mm(banks[bk][:, :], crep[:, k * 128:(k + 1) * 128], w[:, k * 512:(k + 1) * 512], k == 0, False, [t_crep, tw], [tb[bk]])
            P.mm(banks[bk][:, :], ones_f[0:1, :], adab[i % 2][0:1, :], False, True, [t_consts, t_adab[i % 2]], [tb[bk]])
            P.cp("act", dests[i], banks[bk][:, :], [tb[bk]], [t_dests[i]])

    def rope(src, dst, G, cs, t_src, t_dst, t_cs, tmp, t_tmp):
        cosb = cs[:, 0:32].unsqueeze(1).to_broadcast([128, G, 32])
        sinb = cs[:, 32:64].unsqueeze(1).to_broadcast([128, G, 32])
        x1 = src[:, :, 0:32]
        x2 = src[:, :, 32:64]
        t1 = tmp[:, 0:G * 32].rearrange("p (g d) -> p g d", g=G)
        t2 = tmp[:, G * 32:2 * G * 32].rearrange("p (g d) -> p g d", g=G)
        t3 = tmp[:, 2 * G * 32:3 * G * 32].rearrange("p (g d) -> p g d", g=G)
        t4 = tmp[:, 3 * G * 32:4 * G * 32].rearrange("p (g d) -> p g d", g=G)
        P.tt("dve", t1, x1, cosb, ALU.mult, [t_src, t_cs], [t_tmp[0]])
        P.tt("dve", t2, x2, sinb, ALU.mult, [t_src, t_cs], [t_tmp[1]])
        P.tt("dve", dst[:, :, 0:32], t1, t2, ALU.subtract, [t_tmp[0], t_tmp[1]], [t_dst])
        P.tt("pool", t3, x2, cosb, ALU.mult, [t_src, t_cs], [t_tmp[2]])
        P.tt("pool", t4, x1, sinb, ALU.mult, [t_src, t_cs], [t_tmp[3]])
        P.tt("pool", dst[:, :, 32:64], t3, t4, ALU.add, [t_tmp[2], t_tmp[3]], [t_dst])

    def rstd_from_ss(ss, n, t_ss, scale):
        P.act(ss, ss, AF.Sqrt, [t_ss], [t_ss], bias=EPS, scale=scale)
        P.recip(ss, ss, [t_ss], [t_ss])

    uv_d = nc.dram_tensor("uvtab", [NEXP, 2048], BF16, kind="Internal").ap()
    t_uv = P.tok("uvtab")
    if do_b:
        A.off = base_off
        RB = 4
        uview = u_d.rearrange("(r p) d -> p r d", p=128)
        vview = v_d.rearrange("(r p) d -> p r d", p=128)
        uvview = uv_d.rearrange("(r p) d -> p r d", p=128)
        fu = [A.f32(RB * 1024) for _ in range(2)]
        fvv_ = [A.f32(RB * 1024) for _ in range(2)]
        fo = [A.bf16(RB * 2048) for _ in range(2)]
        t_fu = P.toks("fu", 2)
        t_fv_ = P.toks("fv_", 2)
        t_fo = P.toks("fo", 2)
        for it in range(NEXP // 128 // RB):
            b_ = it % 2
            P.ld("sp", fu[b_].rearrange("p (r d) -> p r d", r=RB), uview[:, it * RB:(it + 1) * RB, :], t_fu[b_])
            P.ld("sp", fvv_[b_].rearrange("p (r d) -> p r d", r=RB), vview[:, it * RB:(it + 1) * RB, :], t_fv_[b_])
            fov = fo[b_].rearrange("p (r d) -> p r d", r=RB)
            P.cp("act", fov[:, :, 0:1024], fu[b_].rearrange("p (r d) -> p r d", r=RB), [t_fu[b_]], [t_fo[b_]])
            P.cp("dve", fov[:, :, 1024:2048], fvv_[b_].rearrange("p (r d) -> p r d", r=RB), [t_fv_[b_]], [t_fo[b_]])
            P.ld("sp", uvview[:, it * RB:(it + 1) * RB, :], fov, t_fo[b_], reads=[t_fo[b_]], writes=[t_uv])
        P.barrier()

    if do_a1 or do_a2:
        A.off = base_off
        ropes = A.f32(NT * 64)
        t_rope = P.tok("rope")
        P.ld("sp", ropes.rearrange("p (t d) -> p t d", t=NT), rope_d[:, :, :], t_rope)
        lam2 = small(2)
        neglam = small(2)
        t_lam = P.tok("lam")
        gsub = A.f32(128)
        G1 = A.f32(1024)
        SH1 = A.f32(1024)
        GATE1 = A.f32(1024)
        t_mod = P.tok("mod1")
        kv_off = A.off
        lq = A.f32(256)
        t_lq = P.tok("lq")
        P.ld("sp", lq, gains_d[:, G_LQ:G_LQ + 256], t_lq)
        prod = A.f32(128)
        lq4 = lq.rearrange("p (a b d) -> p a b d", a=2, b=2)
        P.tt("dve", prod.rearrange("p (a d) -> p a d", a=2), lq4[:, :, 0, :], lq4[:, :, 1, :], ALU.mult, [t_lq], [t_lam])
        P.red(lam2[:, 0:2], prod.rearrange("p (a d) -> p a d", a=2), [t_lam], [t_lam])
        P.act(lam2[:, 0:2], lam2[:, 0:2], AF.Exp, [t_lam], [t_lam])
        P.tt("dve", neglam[:, 0:1], lam2[:, 1:2], lam2[:, 0:1], ALU.subtract, [t_lam], [t_lam])
        P.ts("dve", neglam[:, 0:1], neglam[:, 0:1], -LAMBDA_INIT, ALU.add, [t_lam], [t_lam])
        gs_t = A.f32(128)
        t_gs = P.tok("gs_t")
        P.ld("sp", gs_t, gains_d[:, G_SUB:G_SUB + 128], t_gs)
        P.ts("dve", gsub, gs_t, 1.0 - LAMBDA_INIT, ALU.mult, [t_gs], [t_lam])

        for s in range(NSEQ):
            A.off = kv_off
            P.barrier()
            wst = [A.f32(4096), A.f32(4096)]
            t_wst = P.toks("wst", 2)
            mtmp = [A.f32(512), A.f32(512)]
            t_mtmp = P.toks("mtmp", 2)
            gn1 = A.f32(1024)
            t_gn1 = P.tok("gn1")
            P.ld("sp", gn1, gains_d[:, G_N1:G_N1 + 1024], t_gn1)
            dests = [SH1[:, 0:512], SH1[:, 512:1024], mtmp[0], mtmp[1], GATE1[:, 0:512], GATE1[:, 512:1024]]
            t_dests = [t_mod, t_mod, t_mtmp[0], t_mtmp[1], t_mod, t_mod]
            compute_mod(s, [0, 1, 2, 3, 4, 5], dests, t_dests, wst, t_wst, [0, 1])
            for i in range(2):
                P.stt(G1[:, i * 512:(i + 1) * 512], mtmp[i], 1.0, gn1[:, i * 512:(i + 1) * 512], ALU.add, ALU.mult, [t_mtmp[i], t_gn1], [t_mod])

            for phase in ("a1", "a2"):
                if phase == "a1" and not do_a1:
                    continue
                if phase == "a2" and not do_a2:
                    continue
                A.off = kv_off
                P.barrier()
                a1 = phase == "a1"
                t_gA = P.tok("gA")
                t_w = P.tok("wA")
                gseg = {}

                def gload(name, col, n):
                    buf = A.f32(n)
                    P.ld("sp", buf, gains_d[:, col:col + n], t_gA)
                    gseg[name] = buf

                if a1:
                    gload("qlat", G_QLAT, 384); gload("kvlat", G_KVLAT, 256); gload("kpe", G_KPE, 64)
                    gload("q12", G_Q12, 768); gload("k4", G_K4, 512)
                    WCOL0, WN = 0, 704
                else:
                    gload("dqk", G_DQK, 1024)
                    WCOL0, WN = 704, 1536
                win_b = A.bf16(8 * WN)
                wout_b = A.bf16(4 * 1024)
                wofs = 0 if a1 else 512
                if a1:
                    wq_b = A.bf16(3 * 768)
                    wkv_b = A.bf16(2 * 1024)
                mark = A.off
                stage = [A.f32(1536), A.f32(1536)]
                t_stage = P.toks("stage", 2)
                ei = 0
                for k in range(8):
                    load_cast_weight(win_b[:, k * WN:(k + 1) * WN], win_d[k * 128:(k + 1) * 128, WCOL0:WCOL0 + WN], WN, stage, t_stage, t_w, k, ei); ei += 1
                for k in range(4):
                    load_cast_weight(wout_b[:, k * 1024:(k + 1) * 1024], wout_d[wofs + k * 128:wofs + (k + 1) * 128, :], 1024, stage, t_stage, t_w, k, ei); ei += 1
                if a1:
                    for k in range(3):
                        load_cast_weight(wq_b[:, k * 768:(k + 1) * 768], wq_d[k * 128:(k + 1) * 128, :], 768, stage, t_stage, t_w, k, ei); ei += 1
                    for k in range(2):
                        load_cast_weight(wkv_b[:, k * 1024:(k + 1) * 1024], wkv_d[k * 128:(k + 1) * 128, :], 1024, stage, t_stage, t_w, k, ei); ei += 1
                P.barrier()
                A.off = mark
                if a1:
                    kTn = A.bf16(4 * S)
                    kTp = A.bf16(S)
                    Vv = A.bf16(NT * 4 * 130)
                else:
                    kTn = A.bf16(4 * S)
                    kTp = None
                    Vv = A.bf16(NT * 4 * 130)
                kTn_v = kTn.rearrange("p (h s) -> p h s", h=4)
                Vv_v = Vv.rearrange("p (t h d) -> p t h d", t=NT, h=4)
                t_kv = P.toks("kv", NT)
                t_vinit = P.tok("vinit")
                P.memset("pool", Vv, 1.0, [t_vinit] + t_kv)
                xt = [A.f32(1024), A.f32(1024)]
                t_xt = P.toks("xt", 2)
                junk = A.bf16(1024)
                t_junk = P.tok("junk")
                sm = [A.f32(64), A.f32(64)]
                t_sm = P.toks("sm", 2)
                tmpf = A.f32(1024)
                t_tmpf = P.tok("tmpf")
                hb = A.bf16(1024)
                t_hb = P.tok("hb")
                hT = A.bf16(1024)
                t_hT = P.tok("hT")
                NPJ = 704 if a1 else 1536
                proj = A.f32(NPJ)
                t_proj = P.tok("proj")
                sq = A.f32(768 if a1 else 1024)
                t_sq = P.tok("sq")
                nrm = A.f32(64 if a1 else 1024)
                t_nrm = P.tok("nrm")
                rtmp = A.f32(4 * (4 if a1 else 16) * 32)
                t_rtmp = P.toks("rtmp", 4)
                rot = A.bf16(64 if a1 else 1024)
                t_rot = P.tok("rot")
                if a1:
                    qn = A.bf16(640)
                    t_qn = P.tok("qn")
                    qnT = A.bf16(640)
                    t_qnT = P.tok("qnT")
                    qf = A.f32(768)
                    t_qf = P.tok("qf")
                    qpe = A.f32(256)
                    t_qpe = P.tok("qpe")
                    qm = A.bf16(768)
                    t_qm = P.tok("qm")
                    kf = A.f32(512)
                    t_kf = P.tok("kf")
                    kn = A.bf16(512)
                    t_kn = P.tok("kn")
                    qTn = [A.bf16(4 * 512), A.bf16(4 * 512)]
                    qTp = [A.bf16(4 * 512), A.bf16(4 * 512)]
                    qTp_v = [q_.rearrange("p (h s) -> p h s", h=4) for q_ in qTp]
                else:
                    qTn = [A.bf16(4 * 512), A.bf16(4 * 512)]
                qTn_v = [q_.rearrange("p (h s) -> p h s", h=4) for q_ in qTn]
                t_qT = P.toks("qT", 2)
                NPT = 4
                PT = [A.bf16(512) for _ in range(NPT)]
                t_PT = P.toks("PT", NPT)
                mixed = [A.bf16(4 * 512), A.bf16(4 * 512)]
                mixed_v = [m_.rearrange("p (q c) -> p q c", q=4) for m_ in mixed]
                t_mixed = [P.toks("mixedA", 4), P.toks("mixedB", 4)]
                mT = A.bf16(512)
                t_mT = P.tok("mT")
                xr = xt
                t_xr = t_xt
                ytmp = A.f32(1024)
                t_ytmp = P.tok("ytmp")
                osm = A.f32(64)
                t_osm = P.tok("osm")
                oa = A.f32(128)
                t_oa = P.tok("oa")
                od = A.f32(128)
                t_od = P.tok("od")
                cntA = {"ti": 0, "pti": 0, "xi": 0}
                MB = (4, 5, 7) if a1 else (7, 7, 7)
                OB = (4, 5) if a1 else (7, 7)

                def tile_proc(c):
                    cp_ = c % 2
                    for j in range(4):
                        tl = c * 4 + j
                        row0 = s * S + tl * 128
                        ti = cntA["ti"]; cntA["ti"] += 1
                        xb = xt[ti % 2]; txb = t_xt[ti % 2]
                        smb = sm[ti % 2]; tsm = t_sm[ti % 2]
                        P.ld("sp", xb, x_d[row0:row0 + 128, :], txb)
                        P.act(junk, xb, AF.Square, [txb], [t_junk, tsm], accum=smb[:, 0:1])
                        rstd_from_ss(smb[:, 0:1], 1, tsm, 1.0 / 1024)
                        yield
                        P.stt(tmpf, xb, smb[:, 0:1], G1, ALU.mult, ALU.mult, [txb, tsm, t_mod], [t_tmpf])
                        P.tt("pool", hb, tmpf, SH1, ALU.add, [t_tmpf, t_mod], [t_hb])
                        yield
                        Tb = bankbf(6)
                        for k in range(8):
                            P.tr(Tb[:, k * 128:(k + 1) * 128], hb[:, k * 128:(k + 1) * 128], identb, [t_hb, t_cb], [tb[6]])
                        yield
                        P.cp("act", hT, Tb, [tb[6]], [t_hT])
                        yield
                        if a1:
                            chunks = [(0, 512), (512, 704)]
                        else:
                            chunks = [(0, 512), (512, 1024), (1024, 1536)]
                        for ci, (c0, c1) in enumerate(chunks):
                            bk = MB[ci]
                            for k in range(8):
                                P.mm(banks[bk][:, 0:c1 - c0], hT[:, k * 128:(k + 1) * 128],
                                     win_b[:, k * WN + c0:k * WN + c1], k == 0, k == 7, [t_hT, t_w], [tb[bk]])
                            yield
                            P.cp("act" if ci % 2 == 0 else "dve", proj[:, c0:c1], banks[bk][:, 0:c1 - c0], [tb[bk]], [t_proj])
                            yield
                        cs = ropes[:, tl * 64:(tl + 1) * 64]
                        if a1:
                            P.act(sq[:, 0:704], proj[:, 0:704], AF.Square, [t_proj], [t_sq])
                            ss11 = smb[:, 4:15]
                            P.red(ss11, sq[:, 0:704].rearrange("p (g d) -> p g d", d=64), [t_sq], [tsm])
                            yield
                            ss3 = smb[:, 16:19]
                            P.red(ss3[:, 0:1], ss11[:, 0:6], [tsm], [tsm])
                            P.red(ss3[:, 1:2], ss11[:, 6:10], [tsm], [tsm])
                            P.ts("dve", ss3[:, 0:1], ss3[:, 0:1], 1.0 / 384, ALU.mult, [tsm], [tsm])
                            P.ts("dve", ss3[:, 1:2], ss3[:, 1:2], 1.0 / 256, ALU.mult, [tsm], [tsm])
                            P.ts("dve", ss3[:, 2:3], ss11[:, 10:11], 1.0 / 64, ALU.mult, [tsm], [tsm])
                            yield
                            rstd_from_ss(ss3, 3, tsm, 1.0)
                            yield
                            P.stt(qn[:, 0:384], proj[:, 0:384], ss3[:, 0:1], gseg["qlat"], ALU.mult, ALU.mult, [t_proj, tsm, t_gA], [t_qn])
                            P.stt(qn[:, 384:640], proj[:, 384:640], ss3[:, 1:2], gseg["kvlat"], ALU.mult, ALU.mult, [t_proj, tsm, t_gA], [t_qn])
                            P.stt(nrm[:, 0:64], proj[:, 640:704], ss3[:, 2:3], gseg["kpe"], ALU.mult, ALU.mult, [t_proj, tsm, t_gA], [t_nrm])
                            yield
                            rope(nrm[:, 0:64].rearrange("p (g d) -> p g d", g=1), rot[:, 0:64].rearrange("p (g d) -> p g d", g=1), 1, cs,
                                 t_nrm, t_rot, t_rope, rtmp, t_rtmp)
                            Tb = bankbf(6)
                            for k in range(5):
                                P.tr(Tb[:, k * 128:(k + 1) * 128], qn[:, k * 128:(k + 1) * 128], identb, [t_qn, t_cb], [tb[6]])
                            yield
                            P.cp("act", qnT, Tb[:, 0:640], [tb[6]], [t_qnT])
                            yield
                            QB = (7, 4)
                            KB = (5, 7)
                            for cc in range(2):
                                for k in range(3):
                                    P.mm(banks[QB[cc]][:, 0:384], qnT[:, k * 128:(k + 1) * 128], wq_b[:, k * 768 + cc * 384:k * 768 + (cc + 1) * 384],
                                         k == 0, k == 2, [t_qnT, t_w], [tb[QB[cc]]])
                            yield
                            for cc in range(2):
                                P.act(sq[:, cc * 384:(cc + 1) * 384], banks[QB[cc]][:, 0:384], AF.Square, [tb[QB[cc]]], [t_sq])
                            yield
                            ss12 = smb[:, 20:32]
                            P.red(ss12, sq[:, 0:768].rearrange("p (g d) -> p g d", d=64), [t_sq], [tsm])
                            ss12v = ss12.rearrange("p (h t) -> p h t", t=3)
                            r12 = smb[:, 32:44]
                            r12v = r12.rearrange("p (h t) -> p h t", t=3)
                            P.tt("dve", r12v[:, :, 0], ss12v[:, :, 0], ss12v[:, :, 1], ALU.add, [tsm], [tsm])
                            yield
                            P.ts("dve", r12v[:, :, 0], r12v[:, :, 0], 1.0 / 128, ALU.mult, [tsm], [tsm])
                            P.ts("dve", r12v[:, :, 2], ss12v[:, :, 2], 1.0 / 64, ALU.mult, [tsm], [tsm])
                            yield
                            P.cp("dve", r12v[:, :, 1], r12v[:, :, 0], [tsm], [tsm])
                            yield
                            rstd_from_ss(r12, 12, tsm, 1.0)
                            yield
                            for cc in range(2):
                                P.tt("dve", qf[:, cc * 384:(cc + 1) * 384].rearrange("p (g d) -> p g d", d=64),
                                     banks[QB[cc]][:, 0:384].rearrange("p (g d) -> p g d", d=64),
                                     r12[:, cc * 6:(cc + 1) * 6].unsqueeze(2).to_broadcast([128, 6, 64]), ALU.mult, [tb[QB[cc]], tsm], [t_qf])
                            yield
                            for cc in range(2):
                                for k in range(2):
                                    P.mm(banks[KB[cc]][:, :], qnT[:, (3 + k) * 128:(4 + k) * 128], wkv_b[:, k * 1024 + cc * 512:k * 1024 + (cc + 1) * 512],
                                         k == 0, k == 1, [t_qnT, t_w], [tb[KB[cc]]])
                            qfv = qf.rearrange("p (h d) -> p h d", h=4)
                            gq = gseg["q12"].rearrange("p (h d) -> p h d", h=4)
                            qmv = qm.rearrange("p (h d) -> p h d", h=4)
                            P.tt("dve", qmv[:, :, 0:128], qfv[:, :, 0:128], gq[:, :, 0:128], ALU.mult, [t_qf, t_gA], [t_qm])
                            qpev = qpe.rearrange("p (h d) -> p h d", h=4)
                            P.tt("pool", qpev, qfv[:, :, 128:192], gq[:, :, 128:192], ALU.mult, [t_qf, t_gA], [t_qpe])
                            yield
                            rope(qpev, qmv[:, :, 128:192], 4, cs, t_qpe, t_qm, t_rope, rtmp, t_rtmp)
                            yield
                            for cc in range(2):
                                kvv = banks[KB[cc]][:, :].rearrange("p (h d) -> p h d", h=2)
                                P.act(sq[:, cc * 256:(cc + 1) * 256].rearrange("p (h d) -> p h d", h=2), kvv[:, :, 0:128], AF.Square, [tb[KB[cc]]], [t_sq])
                                P.cp("act", Vv_v[:, tl, 2 * cc:2 * cc + 2, 0:128], kvv[:, :, 128:256], [tb[KB[cc]], t_vinit], [t_kv[tl]])
                            yield
                            ssk = smb[:, 44:48]
                            P.red(ssk, sq[:, 0:512].rearrange("p (h d) -> p h d", h=4), [t_sq], [tsm])
                            yield
                            rstd_from_ss(ssk, 4, tsm, 1.0 / 128)
                            yield
                            for cc in range(2):
                                kvv = banks[KB[cc]][:, :].rearrange("p (h d) -> p h d", h=2)
                                P.tt("dve", kf[:, cc * 256:(cc + 1) * 256].rearrange("p (h d) -> p h d", h=2), kvv[:, :, 0:128],
                                     ssk[:, 2 * cc:2 * cc + 2].unsqueeze(2).to_broadcast([128, 2, 128]), ALU.mult, [tb[KB[cc]], tsm], [t_kf])
                            yield
                            P.tt("pool", kn, kf, gseg["k4"], ALU.mult, [t_kf, t_gA], [t_kn])
                            yield
                            Tb = bankbf(6)
                            for h in range(4):
                                P.tr(Tb[:, h * 128:(h + 1) * 128], qmv[:, h, 0:128], identb, [t_qm, t_cb], [tb[6]])
                                P.tr(Tb[:, 512 + h * 128:512 + (h + 1) * 128], kn[:, h * 128:(h + 1) * 128], identb, [t_kn, t_cb], [tb[6]])
                            yield
                            P.cp("act", qTn_v[cp_][:, :, j * 128:(j + 1) * 128], Tb[:, 0:512].rearrange("p (h t) -> p h t", h=4), [tb[6]], [t_qT[cp_]])
                            P.cp("dve", kTn_v[:, :, tl * 128:(tl + 1) * 128], Tb[:, 512:1024].rearrange("p (h t) -> p h t", h=4), [tb[6]], [t_kv[tl]])
                            yield
                            Tb = bankbf(6)
                            for h in range(4):
                                P.tr(Tb[0:64, h * 128:(h + 1) * 128], qmv[:, h, 128:192], identb, [t_qm, t_cb], [tb[6]])
                            P.tr(Tb[0:64, 512:640], rot[:, 0:64], identb, [t_rot, t_cb], [tb[6]])
                            yield
                            P.cp("act", qTp_v[cp_][0:64, :, j * 128:(j + 1) * 128], Tb[0:64, 0:512].rearrange("p (h t) -> p h t", h=4), [tb[6]], [t_qT[cp_]])
                            P.cp("dve", kTp[0:64, tl * 128:(tl + 1) * 128], Tb[0:64, 512:640], [tb[6]], [t_kv[tl]])
                            yield
                        else:
                            P.act(sq[:, 0:1024], proj[:, 0:1024], AF.Square, [t_proj], [t_sq])
                            ss16 = smb[:, 4:20]
                            P.red(ss16, sq[:, 0:1024].rearrange("p (g d) -> p g d", d=64), [t_sq], [tsm])
                            yield
                            rstd_from_ss(ss16, 16, tsm, 1.0 / 64)
                            yield
                            P.tt("dve", tmpf.rearrange("p (g d) -> p g d", d=64), proj[:, 0:1024].rearrange("p (g d) -> p g d", d=64),
                                 ss16.unsqueeze(2).to_broadcast([128, 16, 64]), ALU.mult, [t_proj, tsm], [t_tmpf])
                            yield
                            P.tt("pool", nrm, tmpf, gseg["dqk"], ALU.mult, [t_tmpf, t_gA], [t_nrm])
                            yield
                            rope(nrm.rearrange("p (g d) -> p g d", d=64), rot.rearrange("p (g d) -> p g d", d=64), 16, cs,
                                 t_nrm, t_rot, t_rope, rtmp, t_rtmp)
                            P.cp("act", Vv_v[:, tl, :, 0:128], proj[:, 1024:1536].rearrange("p (h d) -> p h d", h=4), [t_proj, t_vinit], [t_kv[tl]])
                            yield
                            Tb = bankbf(6)
                            for h in range(8):
                                P.tr(Tb[:, h * 128:(h + 1) * 128], rot[:, h * 128:(h + 1) * 128], identb, [t_rot, t_cb], [tb[6]])
                            yield
                            P.cp("act", qTn_v[cp_][:, :, j * 128:(j + 1) * 128], Tb[:, 0:512].rearrange("p (h t) -> p h t", h=4), [tb[6]], [t_qT[cp_]])
                            P.cp("dve", kTn_v[:, :, tl * 128:(tl + 1) * 128], Tb[:, 512:1024].rearrange("p (h t) -> p h t", h=4), [tb[6]], [t_kv[tl]])
                            yield

                def emit_ST(c, h, m, kb):
                    cp_ = c % 2
                    jd = kb - 4 * c
                    q0 = 0 if jd < 0 else jd * 128
                    sb = kb % 2
                    if a1:
                        P.mm(banks[sb][:, q0:512], kTn_v[:, h, kb * 128:(kb + 1) * 128], qTn_v[cp_][:, h, q0:512], True, False,
                             [t_kv[kb], t_qT[cp_]], [tb[sb]])
                        P.mm(banks[sb][:, q0:512], kTp[0:64, kb * 128:(kb + 1) * 128], qTp_v[cp_][0:64, h, q0:512], False, True,
                             [t_kv[kb], t_qT[cp_]], [tb[sb]])
                        scale = 192 ** -0.5
                    else:
                        P.mm(banks[sb][:, q0:512], kTn_v[m * 64:(m + 1) * 64, h, kb * 128:(kb + 1) * 128],
                             qTn_v[cp_][m * 64:(m + 1) * 64, h, q0:512], True, True, [t_kv[kb], t_qT[cp_]], [tb[sb]])
                        scale = 64 ** -0.5
                    pti = cntA["pti"]; cntA["pti"] += 1
                    pb = PT[pti % NPT]; tpb = t_PT[pti % NPT]
                    P.act(pb[:, q0:512], banks[sb][:, q0:512], AF.Exp, [tb[sb]], [tpb], scale=scale)
                    if jd >= 0:
                        P.tt("pool", pb[:, q0:q0 + 128], pb[:, q0:q0 + 128], trib, ALU.mult, [tpb, t_cb], [tpb])
                    return (c, h, m, kb, pb, tpb)

                def emit_PV(c, h, m, kb, pb, tpb):
                    cp_ = c % 2
                    jd = kb - 4 * c
                    nkb = 4 * c + 4
                    ob = (2, 3) if m == 0 else (4, 5)
                    for qb in range(max(jd, 0), 4):
                        bk = ob[qb // 2]
                        ov = banks[bk][:, 0:260].rearrange("p (a d) -> p a d", a=2)
                        P.mm(ov[:, qb % 2, 0:129], pb[:, qb * 128:(qb + 1) * 128], Vv_v[:, kb, h, 0:129],
                             kb == 0 and qb % 2 == 0, kb == 4 * c + qb, [tpb, t_kv[kb]], [tb[bk]])
                    nmap = 1 if a1 else 2
                    if kb == nkb - 1 and m == nmap - 1:
                        for qb in range(4):
                            if a1:
                                bk = (2, 3)[qb // 2]
                                ov = banks[bk][:, 0:260].rearrange("p (a d) -> p a d", a=2)
                                P.recip(osm[:, qb:qb + 1], ov[:, qb % 2, 128:129], [tb[bk]], [t_osm])
                                P.ts("dve", mixed_v[cp_][:, qb, h * 128:(h + 1) * 128], ov[:, qb % 2, 0:128], osm[:, qb:qb + 1], ALU.mult, [tb[bk], t_osm], [t_mixed[cp_][qb]])
                            else:
                                b1 = (2, 3)[qb // 2]
                                b2 = (4, 5)[qb // 2]
                                o1 = banks[b1][:, 0:260].rearrange("p (a d) -> p a d", a=2)
                                o2 = banks[b2][:, 0:260].rearrange("p (a d) -> p a d", a=2)
                                P.recip(osm[:, 0:1], o1[:, qb % 2, 128:129], [tb[b1]], [t_osm])
                                P.recip(osm[:, 1:2], o2[:, qb % 2, 128:129], [tb[b2]], [t_osm])
                                P.ts("dve", oa, o2[:, qb % 2, 0:128], osm[:, 1:2], ALU.mult, [tb[b2], t_osm, t_lam], [t_oa], s2=neglam[:, 0:1], op1=ALU.mult)
                                P.stt(od, o1[:, qb % 2, 0:128], osm[:, 0:1], oa, ALU.mult, ALU.add, [tb[b1], t_osm, t_oa], [t_od])
                                P.act(oa, od, AF.Square, [t_od], [t_oa, t_osm], accum=osm[:, 2:3])
                                rstd_from_ss(osm[:, 2:3], 1, t_osm, 1.0 / 128)
                                P.stt(mixed_v[cp_][:, qb, h * 128:(h + 1) * 128], od, osm[:, 2:3], gsub, ALU.mult, ALU.mult, [t_od, t_osm, t_lam], [t_mixed[cp_][qb]])

                def outproj(c):
                    cp_ = c % 2
                    for qb in range(4):
                        tl = c * 4 + qb
                        row0 = s * S + tl * 128
                        gt = row0 // 128
                        Tb = bankbf(6)
                        for k in range(4):
                            P.tr(Tb[:, k * 128:(k + 1) * 128], mixed_v[cp_][:, qb, k * 128:(k + 1) * 128], identb, [t_mixed[cp_][qb], t_cb], [tb[6]])
                        yield
                        P.cp("act", mT, Tb[:, 0:512], [tb[6]], [t_mT])
                        xi = cntA["ti"]; cntA["ti"] += 1
                        xrb = xr[xi % 2]; txr = t_xr[xi % 2]
                        if a1:
                            P.ld("sp", xrb, x_d[row0:row0 + 128, :], txr)
                        else:
                            P.ld("sp", xrb, x1_d[row0:row0 + 128, :], txr, reads=[tx1[gt]])
                        yield
                        for cc in range(2):
                            bk = OB[cc]
                            for k in range(4):
                                P.mm(banks[bk][:, :], mT[:, k * 128:(k + 1) * 128],
                                     wout_b[:, k * 1024 + cc * 512:k * 1024 + (cc + 1) * 512],
                                     k == 0, k == 3, [t_mT, t_w], [tb[bk]])
                            yield
                            P.tt("dve", ytmp[:, cc * 512:(cc + 1) * 512], banks[bk][:, :], GATE1[:, cc * 512:(cc + 1) * 512], ALU.mult, [tb[bk], t_mod], [t_ytmp])
                            yield
                        P.tt("pool", xrb, xrb, ytmp, ALU.add, [txr, t_ytmp], [txr])
                        P.ld("sp", x1_d[row0:row0 + 128, :], xrb, txr, reads=[txr], writes=[tx1[gt]])
                        yield

                def chain(*gens):
                    for g_ in gens:
                        if g_ is not None:
                            yield from g_

                if stage_lim >= 1:
                    for _ in tile_proc(0):
                        pass
                    nmap = 1 if a1 else 2
                    for c in range(NCH):
                        bg = chain(outproj(c - 1) if (c > 0 and stage_lim >= 3) else None,
                                   tile_proc(c + 1) if c + 1 < NCH else None)
                        if stage_lim >= 2:
                            nkb = 4 * c + 4
                            steps = [(h, m, kb) for h in range(4) for m in range(nmap) for kb in range(nkb)]
                            est = 24 + (140 if a1 else 80)
                            pulls = max(1, -(-est // len(steps)))
                            prev = None
                            for (h, m, kb) in steps:
                                cur = emit_ST(c, h, m, kb)
                                if prev is not None:
                                    emit_PV(*prev)
                                prev = cur
                                for _ in range(pulls):
                                    next(bg, None)
                            emit_PV(*prev)
                        for _ in bg:
                            pass
                    if stage_lim >= 3:
                        for _ in outproj(NCH - 1):
                            pass

    if do_b:
        A.off = base_off
        P.barrier()
        NTT = TT // 128
        gB = A.f32(1024)
        t_gB = P.tok("gB")
        P.ld("sp", gB, gains_d[:, G_N2:G_N2 + 1024], t_gB)
        pwq_b = A.bf16(8 * 2048)
        keys_b = A.bf16(16 * 128)
        t_wB = P.tok("wB")
        wst = [A.f32(4096), A.f32(4096)]
        t_wst = P.toks("wstB", 2)
        stage = wst
        t_stage = t_wst
        ei = 0
        for k in range(8):
            load_cast_weight(pwq_b[:, k * 2048:(k + 1) * 2048], pwq_d[k * 128:(k + 1) * 128, :], 2048, stage, t_stage, t_wB, k, ei); ei += 1
        load_cast_weight(keys_b, keys_d.rearrange("p c n -> p (c n)"), 2048, stage, t_stage, t_wB, 0, ei); ei += 1
        G2 = A.f32(1024)
        SH2 = A.f32(1024)
        GATE2 = [A.f32(1024) for _ in range(NSEQ)]
        t_mod2 = P.tok("mod2")
        t_gate2 = P.toks("gate2", NSEQ)
        mtmp = [A.f32(512), A.f32(512)]
        t_mtmp = P.toks("mtmpB", 2)
        xt = [A.f32(1024), A.f32(1024)]
        t_xt = P.toks("xB", 2)
        junk = A.bf16(1024)
        t_junk = P.tok("junkB")
        sm = A.f32(64)
        t_sm = P.tok("smB")
        h2f = A.f32(1024)
        t_h2f = P.tok("h2f")
        h2b = [A.bf16(1024), A.bf16(1024)]
        t_h2b = P.toks("h2b", 2)
        h2T = A.bf16(1024)
        t_h2T = P.tok("h2T")
        qT = A.bf16(16 * 128)
        t_qTB = P.tok("qTB")
        sc = wst[0][:, 0:2048]
        t_sc = t_wst[0]
        sc2 = [A.f32(128) for _ in range(4)]
        t_sc2 = P.toks("sc2", 4)
        sv = A.f32(256)
        t_svc = P.toks("svc", 16)
        si = A.u32(256)
        t_sic = P.toks("sic", 16)
        sif = A.f32(256)
        t_sif = P.tok("sif")
        cand = wst[0][:, 2048:4096]
        t_cand = t_wst[0]
        cand2 = [A.f32(256) for _ in range(4)]
        t_cand2 = P.toks("cand2", 4)
        fv = A.f32(128)
        t_fvh = P.toks("fvh", 8)
        fpos = A.u32(128)
        t_fph = P.toks("fph", 8)
        k0f = A.f32(128)
        k1f = A.f32(128)
        t_k = P.tok("k01")
        oh = wst[1][:, 0:2048]
        t_oh = t_wst[1]
        e0 = A.f32(128)
        e1 = A.f32(128)
        t_e = P.tok("e01")
        eidx = [A.u32(128), A.u32(128)]
        t_eidx = P.toks("eidx", 2)
        gw = [A.f32(128), A.f32(128)]
        t_gw = P.toks("gw", 2)
        araw = A.f32(128)
        cg = A.f32(128)
        NR = 8
        t_ar = P.toks("araw", NR)
        t_cf = P.toks("cg", NR)
        NG = 10
        uvb = [A.bf16(2048) for _ in range(NG)]
        t_uvb = P.toks("uvb", NG)
        dg = [A.bf16(128) for _ in range(4)]
        t_dg = P.toks("dg", 4)
        ujunk = A.bf16(1024)
        t_ujunk = P.tok("ujunk")
        prod = [A.bf16(1024) for _ in range(4)]
        t_prod = P.toks("prod", 4)
        yout = A.f32(1024)
        t_yout = P.tok("yout")
        src = x1_d if (do_a1 or do_a2) else x_d

        def pre(gt):
            s, tl = divmod(gt, NT)
            par = gt % 2
            row0 = gt * 128
            if tl == 0:
                compute_mod(s, [6, 7, 8, 9, 10, 11],
                            [SH2[:, 0:512], SH2[:, 512:1024], mtmp[0], mtmp[1], GATE2[s][:, 0:512], GATE2[s][:, 512:1024]],
                            [t_mod2, t_mod2, t_mtmp[0], t_mtmp[1], t_gate2[s], t_gate2[s]], wst, t_wst, [7, 0])
                for i in range(2):
                    P.stt(G2[:, i * 512:(i + 1) * 512], mtmp[i], 1.0, gB[:, i * 512:(i + 1) * 512], ALU.add, ALU.mult, [t_mtmp[i], t_gB], [t_mod2])
                yield
            xb = xt[par]; txb = t_xt[par]
            hb = h2b[par]; thb = t_h2b[par]
            P.ld("sp", xb, src[row0:row0 + 128, :], txb, reads=[tx1[gt]])
            P.act(junk, xb, AF.Square, [txb], [t_junk, t_sm], accum=sm[:, 0:1])
            rstd_from_ss(sm[:, 0:1], 1, t_sm, 1.0 / 1024)
            P.stt(h2f, xb, sm[:, 0:1], G2, ALU.mult, ALU.mult, [txb, t_sm, t_mod2], [t_h2f])
            P.tt("dve", hb, h2f, SH2, ALU.add, [t_h2f, t_mod2], [thb])
            yield
            Tb = bankbf(6)
            for k in range(8):
                P.tr(Tb[:, k * 128:(k + 1) * 128], hb[:, k * 128:(k + 1) * 128], identb, [thb, t_cb], [tb[6]])
            P.cp("act", h2T, Tb, [tb[6]], [t_h2T])
            yield
            for g4 in range(4):
                bk = (7, 0)[g4 % 2]
                for cc in range(4):
                    c16 = g4 * 4 + cc
                    for k in range(8):
                        P.mm(banks[bk][:, cc * 128:(cc + 1) * 128], pwq_b[:, k * 2048 + c16 * 128:k * 2048 + (c16 + 1) * 128],
                             h2T[:, k * 128:(k + 1) * 128], k == 0 and cc == 0, k == 7, [t_wB, t_h2T], [tb[bk]])
                P.cp("act", qT[:, g4 * 512:(g4 + 1) * 512], banks[bk][:, :], [tb[bk]], [t_qTB])
                yield
            for g4 in range(4):
                bk = (1, 7)[g4 % 2]
                for cc in range(4):
                    c16 = g4 * 4 + cc
                    P.mm(banks[bk][:, cc * 128:(cc + 1) * 128], qT[:, c16 * 128:(c16 + 1) * 128], keys_b[:, c16 * 128:(c16 + 1) * 128],
                         cc == 0, True, [t_qTB, t_wB], [tb[bk]])
                P.cp("act", sc[:, g4 * 512:(g4 + 1) * 512], banks[bk][:, :], [tb[bk]], [t_sc])
                yield
            svv = sv.rearrange("p (c k) -> p c k", c=16)
            siv = si.rearrange("p (c k) -> p c k", c=16)
            f_max = lambda o, i_: (lambda e: e.max(out=o, in_=i_))
            f_mr = lambda o, r, i_: (lambda e: e.match_replace(out=o, in_to_replace=r, in_values=i_, imm_value=-1e30))
            f_mi = lambda o, m_, i_: (lambda e: e.max_index(out=o, in_max=m_, in_values=i_))
            for g in range(4):
                cs4 = range(4 * g, 4 * g + 4)
                for c16 in cs4:
                    P.op("dve", f_max(svv[:, c16, 0:8], sc[:, c16 * 128:(c16 + 1) * 128]), [t_sc], [t_svc[c16]])
                for c16 in cs4:
                    P.op("dve", f_mr(sc2[c16 % 4], svv[:, c16, 0:8], sc[:, c16 * 128:(c16 + 1) * 128]), [t_sc, t_svc[c16]], [t_sc2[c16 % 4]])
                for c16 in cs4:
                    P.op("dve", f_max(svv[:, c16, 8:16], sc2[c16 % 4]), [t_sc2[c16 % 4]], [t_svc[c16]])
                for c16 in cs4:
                    P.op("dve", f_mi(siv[:, c16, 0:8], svv[:, c16, 0:8], sc[:, c16 * 128:(c16 + 1) * 128]), [t_sc, t_svc[c16]], [t_sic[c16]])
                for c16 in cs4:
                    P.op("dve", f_mi(siv[:, c16, 8:16], svv[:, c16, 8:16], sc[:, c16 * 128:(c16 + 1) * 128]), [t_sc, t_svc[c16]], [t_sic[c16]])
                yield
            P.cp("dve", sif, si, t_sic, [t_sif])
            sv4 = sv.rearrange("p (h t k) -> p h t k", h=8, t=2)
            candv = cand.rearrange("p (h a b) -> p h a b", h=8, a=16)
            P.tt("dve", candv, sv4[:, :, 0, :].unsqueeze(3).to_broadcast([128, 8, 16, 16]),
                 sv4[:, :, 1, :].unsqueeze(2).to_broadcast([128, 8, 16, 16]), ALU.add, t_svc, [t_cand])
            yield
            fvv = fv.rearrange("p (h k) -> p h k", h=8)
            fpv = fpos.rearrange("p (h k) -> p h k", h=8)
            for g in range(2):
                hs4 = range(4 * g, 4 * g + 4)
                for h in hs4:
                    P.op("dve", f_max(fvv[:, h, 0:8], cand[:, h * 256:(h + 1) * 256]), [t_cand], [t_fvh[h]])
                for h in hs4:
                    P.op("dve", f_mr(cand2[h % 4], fvv[:, h, 0:8], cand[:, h * 256:(h + 1) * 256]), [t_cand, t_fvh[h]], [t_cand2[h % 4]])
                for h in hs4:
                    P.op("dve", f_max(fvv[:, h, 8:16], cand2[h % 4]), [t_cand2[h % 4]], [t_fvh[h]])
                for h in hs4:
                    P.op("dve", f_mi(fpv[:, h, 0:8], fvv[:, h, 0:8], cand[:, h * 256:(h + 1) * 256]), [t_cand, t_fvh[h]], [t_fph[h]])
                for h in hs4:
                    P.op("dve", f_mi(fpv[:, h, 8:16], fvv[:, h, 8:16], cand[:, h * 256:(h + 1) * 256]), [t_cand, t_fvh[h]], [t_fph[h]])
                yield
            P.cp("dve", k1f, fpos, t_fph, [t_k])
            P.tt("dve", oh.rearrange("p (a b) -> p a b", b=16), k1f.unsqueeze(2).to_broadcast([128, 128, 16]),
                 thr16.unsqueeze(1).to_broadcast([128, 128, 16]), ALU.is_ge, [t_k, t_consts], [t_oh])
            P.red(k0f, oh.rearrange("p (a b) -> p a b", b=16), [t_oh], [t_k])
            P.stt(k1f, k0f, -16.0, k1f, ALU.mult, ALU.add, [t_k], [t_k])
            yield
            sif4 = sif.rearrange("p (h t k) -> p h t k", h=8, t=2)
            for t_, (kf_, e_) in enumerate(((k0f, e0), (k1f, e1))):
                P.tt("dve", oh.rearrange("p (a b) -> p a b", b=16), kf_.unsqueeze(2).to_broadcast([128, 128, 16]),
                     iota16.unsqueeze(1).to_broadcast([128, 128, 16]), ALU.is_equal, [t_k, t_consts], [t_oh])
                P.tt("dve", oh.rearrange("p (h a b) -> p h a b", h=8, a=16), oh.rearrange("p (h a b) -> p h a b", h=8, a=16),
                     sif4[:, :, t_, :].unsqueeze(2).to_broadcast([128, 8, 16, 16]), ALU.mult, [t_oh, t_sif], [t_oh])
                P.red(e_, oh.rearrange("p (a b) -> p a b", b=16), [t_oh], [t_e])
                yield
            P.stt(e0, e0, 128.0, e1, ALU.mult, ALU.add, [t_e], [t_e])
            P.cp("dve", eidx[par], e0, [t_e], [t_eidx[par]])
            gwp = gw[par]; tgw = t_gw[par]
            P.tt("dve", gwp.rearrange("p (h k) -> p h k", h=8), fvv, fvv[:, :, 0:1].to_broadcast([128, 8, 16]), ALU.subtract, t_fvh, [tgw])
            P.act(gwp, gwp, AF.Exp, [tgw], [tgw])
            P.red(sm[:, 8:16], gwp.rearrange("p (h k) -> p h k", h=8), [tgw], [t_sm])
            P.recip(sm[:, 8:16], sm[:, 8:16], [t_sm], [t_sm])
            P.tt("dve", gwp.rearrange("p (h k) -> p h k", h=8), gwp.rearrange("p (h k) -> p h k", h=8),
                 sm[:, 8:16].unsqueeze(2).to_broadcast([128, 8, 16]), ALU.mult, [tgw, t_sm], [tgw])
            yield

        cnt = {"gi": 0, "di": 0}

        def slot_front(gt, sl):
            par = gt % 2
            gi = cnt["gi"]; cnt["gi"] += 1
            ub = uvb[gi % NG]; tub = t_uvb[gi % NG]
            P.dma("pool", (lambda o, ix: (lambda e: e.indirect_dma_start(out=o, out_offset=None, in_=uv_d[:, :],
                  in_offset=bass.IndirectOffsetOnAxis(ap=ix, axis=0))))(ub, eidx[par][:, sl:sl + 1]), tub, reads=[t_eidx[par], t_uv], writes=[tub])
            tar = t_ar[sl % NR]; tcf = t_cf[sl % NR]
            pr = prod[gi % 4]; tpr = t_prod[gi % 4]
            P.tt("dve", pr, ub[:, 0:1024], h2b[par], ALU.mult, [tub, t_h2b[par]], [tpr])
            P.act(ujunk, pr, AF.Copy, [tpr], [t_ujunk, tar], accum=araw[:, sl:sl + 1])
            P.act(cg[:, sl:sl + 1], araw[:, sl:sl + 1], AF.Gelu, [tar], [tcf])
            return (gt, sl, ub, tub)

        def slot_back(gt, sl, ub, tub):
            par = gt % 2
            yb = (2, 3) if par == 0 else (4, 5)
            tcf = t_cf[sl % NR]
            di = cnt["di"]; cnt["di"] += 1
            db = dg[di % 4]; tdb = t_dg[di % 4]
            P.ts("dve", db, identb, cg[:, sl:sl + 1], ALU.mult, [t_cb, tcf, t_gw[par]], [tdb], s2=gw[par][:, sl:sl + 1], op1=ALU.mult)
            for cc in range(2):
                bk = yb[cc]
                P.mm(banks[bk][:, :], db, ub[:, 1024 + cc * 512:1024 + (cc + 1) * 512], sl == 0, sl == 127, [tdb, tub], [tb[bk]])
            if sl == 127:
                fin(gt)

        def fin(gt):
            s, tl = divmod(gt, NT)
            par = gt % 2
            yb = (2, 3) if par == 0 else (4, 5)
            row0 = gt * 128
            for cc in range(2):
                bk = yb[cc]
                P.tt("dve", yout[:, cc * 512:(cc + 1) * 512], banks[bk][:, :], GATE2[s][:, cc * 512:(cc + 1) * 512], ALU.mult, [tb[bk], t_gate2[s]], [t_yout])
            P.tt("pool", yout, yout, xt[par], ALU.add, [t_yout, t_xt[par]], [t_yout])
            P.ld("sp", out_d[row0:row0 + 128, :], yout, t_yout, reads=[t_yout], writes=[], final=True)

        for _ in pre(0):
            pass
        SKEW = 3
        pend = []
        for gt in range(NTT):
            gen = pre(gt + 1) if gt + 1 < NTT else None
            for sl in range(128):
                pend.append(slot_front(gt, sl))
                if len(pend) > SKEW:
                    slot_back(*pend.pop(0))
                if gen is not None and sl % 2 == 1 and sl >= 8:
                    next(gen, None)
            if gen is not None:
                for _ in gen:
                    pass
        while pend:
            slot_back(*pend.pop(0))
    else:
        A.off = base_off
        P.barrier()
        cb = [A.f32(1024), A.f32(1024)]
        t_cbuf = P.toks("cpb", 2)
        for gt in range(TT // 128):
            P.ld("sp", cb[gt % 2], x1_d[gt * 128:(gt + 1) * 128, :], t_cbuf[gt % 2], reads=[tx1[gt]])
            P.ld("sp", out_d[gt * 128:(gt + 1) * 128, :], cb[gt % 2], t_cbuf[gt % 2], reads=[t_cbuf[gt % 2]], writes=[], final=True)

    P.arena_hw = A.hw
    P.emit()
    return nc, P


def rope_table(S):
    half = 32
    inv = (1.0 / (10000.0 ** (np.arange(half, dtype=np.float32) / np.float32(half)))).astype(np.float32)
    ang = np.arange(S, dtype=np.float32)[:, None] * inv[None, :]
    cs = np.concatenate([np.cos(ang), np.sin(ang)], axis=-1).astype(np.float32)
    return np.ascontiguousarray(cs.reshape(S // 128, 128, 64).transpose(1, 0, 2))


def make_consts():
    c = np.zeros((128, C_TOT), np.float32)
    c[:, C_ID:C_ID + 128] = np.eye(128, dtype=np.float32)
    k = np.arange(128)
    c[:, C_TRI:C_TRI + 128] = (k[:, None] <= k[None, :]).astype(np.float32)
    c[:, C_IOTA:C_IOTA + 16] = np.arange(16, dtype=np.float32)[None, :]
    c[:, C_ONES:C_ONES + 128] = 1.0
    c[:, C_THR:C_THR + 15] = 16.0 * np.arange(1, 16, dtype=np.float32)[None, :]
    c[:, C_THR + 15] = 1e9
    return c


def host_layout(inp, S, NSEQ, ncores):
    f = lambda a: np.ascontiguousarray(np.asarray(a, dtype=np.float32))
    x = f(inp["x"])
    c = f(inp["c"])
    rep = lambda v: np.broadcast_to(f(v).reshape(1, -1), (128, f(v).size))
    mqg = f(inp["mla_q_g"])[0]
    mkg = f(inp["mla_k_g"])[0]
    gains = np.concatenate([
        rep(inp["norm1_g"][0]), rep(inp["mla_q_lat_g"][0]), rep(inp["mla_kv_lat_g"][0]),
        rep(mkg[128:192]),
        rep(np.tile(f(inp["diff_q_g"])[0], 8)), rep(np.tile(f(inp["diff_k_g"])[0], 8)),
        rep(np.tile(mqg, 4)), rep(np.tile(mkg[:128], 4)),
        rep(inp["diff_subln_g"][0]), rep(inp["norm2_g"][0]),
        rep(inp["diff_lq1"][0]), rep(inp["diff_lk1"][0]), rep(inp["diff_lq2"][0]), rep(inp["diff_lk2"][0]),
    ], axis=1)
    gains = np.ascontiguousarray(gains, dtype=np.float32)
    assert gains.shape[1] == G_TOT
    keysT = np.ascontiguousarray(f(inp["peer_sub_keys"])[0].reshape(16, 128, 128).transpose(2, 0, 1))
    shared = {
        "ada_w": f(inp["ada_w"])[0], "ada_b": f(inp["ada_b"])[0].reshape(1, -1),
        "w_in": f(inp["w_in"])[0], "w_q_up": f(inp["mla_w_q_up"])[0], "w_kv_up": f(inp["mla_w_kv_up"])[0],
        "w_out": f(inp["w_out"])[0], "peer_w_q": f(inp["peer_w_q"])[0], "keysT": keysT,
        "peer_u": f(inp["peer_u"])[0], "peer_v": f(inp["peer_v"])[0],
        "gains": gains, "rope": rope_table(S), "consts": make_consts(),
    }
    maps = []
    for i in range(ncores):
        xs = np.ascontiguousarray(x[i * NSEQ:(i + 1) * NSEQ].reshape(NSEQ * S, D))
        cs = c[i * NSEQ:(i + 1) * NSEQ]
        cT = np.ascontiguousarray(cs.reshape(NSEQ, 8, 128).transpose(2, 0, 1))
        m = dict(shared)
        m["x"] = xs
        m["cT"] = cT
        maps.append(m)
    return maps


_CACHE = {}


def kernel(**inputs):
    B, S, _ = inputs["x"].shape
    ncores = 8
    NSEQ = B // ncores
    key = (S, NSEQ)
    if key not in _CACHE:
        _CACHE[key] = build(S, NSEQ)[0]
    nc = _CACHE[key]
    maps = host_layout(inputs, S, NSEQ, ncores)
    res = run_bass_kernel_spmd(nc, maps, core_ids=list(range(ncores)))
    out = np.stack([r["out"].reshape(NSEQ, S, D) for r in res.results], axis=0).reshape(B, S, D)
    return out.astype(np.float32)
```

```python
import math
import numpy as np
from contextlib import ExitStack
import concourse.bass as bass
import concourse.mybir as mybir
from concourse.bass_utils import run_bass_kernel_spmd

F32 = mybir.dt.float32
BF16 = mybir.dt.bfloat16
U32 = mybir.dt.uint32
I32 = mybir.dt.int32
ALU = mybir.AluOpType
AF = mybir.ActivationFunctionType
AX = mybir.AxisListType

ENGS = ("pe", "act", "dve", "pool", "sp")
CHUNK = 16000
EPS = 1e-6
D = 1024
NEXP = 16384


class Tok:
    __slots__ = ("name", "w", "r", "dsem", "dcount", "excl")

    def __init__(self, name):
        self.name = name
        self.excl = False
        self.w = None
        self.r = {}
        self.dsem = None
        self.dcount = 0


class Prog:
    def __init__(self, nc):
        self.nc = nc
        self.es = ExitStack()
        self.ops = {e: [] for e in ENGS}
        self.cnt = {e: 0 for e in ENGS}
        self.esems = {e: [] for e in ENGS}
        self.seen = {e: {} for e in ENGS}
        self.semobj = {}
        self.nsem = 0
        self.final_events = []
        self.dma_toks = []
        self.nins = 0

    def sbuf(self, name, shape, dt):
        return self.es.enter_context(self.nc.sbuf_tensor(name, list(shape), dt))

    def psum(self, name, shape, dt):
        return self.es.enter_context(self.nc.psum_tensor(name, list(shape), dt))

    def newsem(self, name):
        self.nsem += 1
        return self.es.enter_context(self.nc.semaphore(name))

    def tok(self, name):
        return Tok(name)

    def toks(self, name, n):
        return [Tok(f"{name}{i}") for i in range(n)]

    def _eng_event(self, eng):
        c = self.cnt[eng]
        ch, v = divmod(c, CHUNK)
        while len(self.esems[eng]) <= ch:
            s = self.newsem(f"e_{eng}_{len(self.esems[eng])}")
            self.esems[eng].append(s)
            self.semobj[(eng, len(self.esems[eng]) - 1)] = s
        self.cnt[eng] = c + 1
        return ((eng, ch), v + 1)

    def _collect(self, eng, reads, writes, same_eng_sync):
        deps = {}

        def add(ev):
            if ev is None:
                return
            k, v = ev
            if deps.get(k, 0) < v:
                deps[k] = v

        for b in reads:
            add(b.w)
        for b in writes:
            add(b.w)
            for k, v in b.r.items():
                add((k, v))
        waits = []
        seen = self.seen[eng]
        for k, v in deps.items():
            if (not same_eng_sync) and k[0] == eng:
                continue
            if seen.get(k, 0) >= v:
                continue
            seen[k] = v
            waits.append((k, v))
        return waits

    def _record(self, ev, reads, writes):
        k, v = ev
        for b in reads:
            if b.r.get(k, 0) < v:
                b.r[k] = v
        for b in writes:
            b.w = ev
            b.r = {}

    def op(self, eng, fn, reads=(), writes=(), sync_same=None):
        if sync_same is None:
            sync_same = eng != "pe"
        ex = [b for b in reads if b.excl]
        if ex:
            writes = list(writes) + ex
        waits = self._collect(eng, reads, writes, sync_same)
        ev = self._eng_event(eng)
        self._record(ev, reads, writes)
        self.ops[eng].append((waits, fn, ev, 1))
        self.nins += 1

    def dma(self, eng, fn, owner, reads=(), writes=(), final=False):
        waits = self._collect(eng, reads, writes, False)
        if owner.dsem is None:
            owner.dsem = ("dma", self.nsem)
            self.semobj[owner.dsem] = self.newsem(f"d{self.nsem}")
            self.dma_toks.append(owner)
        owner.dcount += 16
        ev = (owner.dsem, owner.dcount)
        self._record(ev, reads, writes)
        self.ops[eng].append((waits, fn, ev, 16))
        if final:
            self.final_events.append(ev)
        self.nins += 1

    def barrier(self):
        evs = {}
        for e in ENGS:
            c = self.cnt[e]
            if c == 0:
                continue
            ch, v = divmod(c - 1, CHUNK)
            evs[(e, ch)] = v + 1
        for t in self.dma_toks:
            evs[t.dsem] = t.dcount
        for e in ENGS:
            waits = []
            for k, v in evs.items():
                if k[0] == e:
                    continue
                if self.seen[e].get(k, 0) >= v:
                    continue
                self.seen[e][k] = v
                waits.append((k, v))
            self.ops[e].append((waits, None, None, 0))

    def emit(self):
        nc = self.nc
        fw = {}
        for k, v in self.final_events:
            fw[k] = max(fw.get(k, 0), v)
        self.ops["sp"].append((list(fw.items()), None, None, 0))
        semobj = self.semobj

        def run(engobj, lst):
            for waits, fn, ev, inc in lst:
                for k, v in waits:
                    engobj.wait_ge(semobj[k], v)
                if fn is None:
                    continue
                ins = fn(engobj)
                ins.then_inc(semobj[ev[0]], inc)

        with nc.Block() as block:
            @block.tensor
            def _(e):
                run(e, self.ops["pe"])

            @block.scalar
            def _(e):
                run(e, self.ops["act"])

            @block.vector
            def _(e):
                run(e, self.ops["dve"])

            @block.gpsimd
            def _(e):
                run(e, self.ops["pool"])

            @block.sync
            def _(e):
                run(e, self.ops["sp"])
        self.es.close()

    def mm(self, out, lhsT, rhs, start, stop, reads, writes):
        self.op("pe", lambda e: e.matmul(out, lhsT=lhsT, rhs=rhs, start=start, stop=stop), reads, writes)

    def tr(self, out, in_, ident, reads, writes):
        self.op("pe", lambda e: e.transpose(out=out, in_=in_, identity=ident), reads, writes)

    def act(self, out, in_, func, reads, writes, bias=None, scale=None, accum=None):
        kw = {}
        if bias is not None:
            kw["bias"] = bias
        if scale is not None:
            kw["scale"] = scale
        if accum is not None:
            kw["accum_out"] = accum
        self.op("act", lambda e: e.activation(out=out, in_=in_, func=func, **kw), reads, writes)

    def tt(self, eng, out, in0, in1, op, reads, writes):
        self.op(eng, lambda e: e.tensor_tensor(out=out, in0=in0, in1=in1, op=op), reads, writes)

    def ts(self, eng, out, in0, s1, op0, reads, writes, s2=None, op1=None):
        if op1 is None:
            self.op(eng, lambda e: e.tensor_scalar(out=out, in0=in0, scalar1=s1, scalar2=None, op0=op0), reads, writes)
        else:
            self.op(eng, lambda e: e.tensor_scalar(out=out, in0=in0, scalar1=s1, scalar2=s2, op0=op0, op1=op1), reads, writes)

    def stt(self, out, in0, scalar, in1, op0, op1, reads, writes, accum=None):
        if accum is None:
            self.op("dve", lambda e: e.scalar_tensor_tensor(out=out, in0=in0, scalar=scalar, in1=in1, op0=op0, op1=op1), reads, writes)
        else:
            self.op("dve", lambda e: e.scalar_tensor_tensor(out=out, in0=in0, scalar=scalar, in1=in1, op0=op0, op1=op1, accum_out=accum), reads, writes)

    def red(self, out, in_, reads, writes, op=ALU.add):
        self.op("dve", lambda e: e.tensor_reduce(out=out, in_=in_, axis=AX.X, op=op), reads, writes)

    def cp(self, eng, out, in_, reads, writes):
        if eng == "act":
            self.op("act", lambda e: e.copy(out=out, in_=in_), reads, writes)
        else:
            self.op(eng, lambda e: e.tensor_copy(out=out, in_=in_), reads, writes)

    def recip(self, out, in_, reads, writes):
        self.op("dve", lambda e: e.reciprocal(out=out, in_=in_), reads, writes)

    def memset(self, eng, ap, val, writes):
        self.op(eng, lambda e: e.memset(ap, val), (), writes)

    def ld(self, eng, out, in_, owner, reads=(), writes=None, final=False):
        if writes is None:
            writes = [owner]
        self.dma(eng, lambda e: e.dma_start(out=out, in_=in_), owner, reads, writes, final)


class Arena:
    def __init__(self, ap, words):
        self.ap = ap
        self.words = words
        self.off = 0

    def f32(self, n):
        n = (n + 1) // 2 * 2
        a = self.ap[:, self.off:self.off + n]
        self.off += n
        self.hw = max(getattr(self, "hw", 0), self.off)
        assert self.off <= self.words, f"arena overflow {self.off} > {self.words}"
        return a

    def bf16(self, n):
        w = (n + 1) // 2
        return self.f32(w).bitcast(BF16)[:, 0:n]

    def u32(self, n):
        return self.f32(n).bitcast(U32)


G_N1 = 0
G_QLAT = G_N1 + 1024
G_KVLAT = G_QLAT + 384
G_KPE = G_KVLAT + 256
G_DQK = G_KPE + 64
G_Q12 = G_DQK + 1024
G_K4 = G_Q12 + 768
G_SUB = G_K4 + 512
G_N2 = G_SUB + 128
G_LQ = G_N2 + 1024
G_TOT = G_LQ + 256

C_ID = 0
C_TRI = 128
C_IOTA = 256
C_ONES = 272
C_THR = 400
C_TOT = 416

LAMBDA_INIT = 0.8 - 0.6 * math.exp(-0.3 * 0)


def build(S, NSEQ, do_a1=True, do_a2=True, do_b=True, stage_lim=9):
    NT = S // 128
    NCH = S // 512
    TT = NSEQ * S
    nc = bass.Bass("TRN2", target_bir_lowering=False)
    P = Prog(nc)

    def din(name, shape, dt=F32):
        return nc.dram_tensor(name, list(shape), dt, kind="ExternalInput").ap()

    x_d = din("x", [TT, D])
    cT_d = din("cT", [128, NSEQ, 8])
    adaw_d = din("ada_w", [D, 6 * D])
    adab_d = din("ada_b", [1, 6 * D])
    win_d = din("w_in", [D, 2240])
    wq_d = din("w_q_up", [384, 768])
    wkv_d = din("w_kv_up", [256, 1024])
    wout_d = din("w_out", [D, D])
    pwq_d = din("peer_w_q", [D, 2048])
    keys_d = din("keysT", [128, 16, 128])
    u_d = din("peer_u", [NEXP, D])
    v_d = din("peer_v", [NEXP, D])
    gains_d = din("gains", [128, G_TOT])
    rope_d = din("rope", [128, NT, 64])
    consts_d = din("consts", [128, C_TOT])
    out_d = nc.dram_tensor("out", [TT, D], F32, kind="ExternalOutput").ap()
    x1_d = nc.dram_tensor("x1s", [TT, D], F32, kind="Internal").ap()
    tx1 = [P.tok(f"x1d{i}") for i in range(TT // 128)]

    AW = 53200
    arena_t = P.sbuf("arena", [128, AW], F32)
    A = Arena(arena_t, AW)
    banks = [P.psum(f"pb{i}", [128, 512], F32) for i in range(8)]
    tb = P.toks("bank", 8)
    for t_ in tb:
        t_.excl = True

    def bankbf(i):
        return banks[i][:, :].bitcast(BF16)

    consts = A.f32(C_TOT)
    t_consts = P.tok("consts")
    P.ld("sp", consts, consts_d[:, :], t_consts)
    ident_f = consts[:, C_ID:C_ID + 128]
    iota16 = consts[:, C_IOTA:C_IOTA + 16]
    ones_f = consts[:, C_ONES:C_ONES + 128]
    thr16 = consts[:, C_THR:C_THR + 16]
    identb = A.bf16(128)
    trib = A.bf16(128)
    t_cb = P.tok("constsb")
    P.cp("dve", identb, ident_f, [t_consts], [t_cb])
    P.cp("dve", trib, consts[:, C_TRI:C_TRI + 128], [t_consts], [t_cb])
    cT = A.f32(NSEQ * 8)
    t_cT = P.tok("cT")
    P.ld("sp", cT, cT_d.rearrange("p b k -> p (b k)"), t_cT)
    adab = [A.f32(512), A.f32(512)]
    t_adab = P.toks("adab", 2)
    base_off = A.off

    def small(n):
        return A.f32(n)

    def load_cast_weight(dst_bf, src_dram_rows, ncols, stage, t_stage, t_dst, k, eng_i):
        st = stage[eng_i % len(stage)]
        ts_ = t_stage[eng_i % len(stage)]
        P.ld("sp", st[:, 0:ncols], src_dram_rows, ts_)
        eng = ("act", "dve", "pool")[eng_i % 3]
        P.cp(eng, dst_bf, st[:, 0:ncols], [ts_], [t_dst])

    def compute_mod(b, chunks, dests, t_dests, wst, t_wst, mbank):
        sil = small(8)
        crep = A.f32(8 * 128)
        t_sil = P.tok("sil")
        t_crep = P.tok("crep")
        P.act(sil, cT[:, b * 8:(b + 1) * 8], AF.Silu, [t_cT], [t_sil])
        P.cp("dve", crep.rearrange("p (k m) -> p k m", k=8), sil.unsqueeze(2).to_broadcast([128, 8, 128]), [t_sil], [t_crep])
        adaw_v = adaw_d.rearrange("(k p) n -> p k n", p=128)
        for i, ch in enumerate(chunks):
            w = wst[i % 2]
            tw = t_wst[i % 2]
            P.ld("sp", w.rearrange("p (k n) -> p k n", k=8), adaw_v[:, :, ch * 512:(ch + 1) * 512], tw)
            bk = mbank[i % 2]
            P.ld("sp", adab[i % 2][0:1, :], adab_d[:, ch * 512:(ch + 1) * 512], t_adab[i % 2])
            for k in range(8):
                P.mm(banks[bk][:, :], crep[:, k * 128:(k + 1) * 128], w[:, k * 512:(k + 1) * 512], k == 0, False, [t_crep, tw], [tb[bk]])
            P.mm(banks[bk][:, :], ones_f[0:1, :], adab[i % 2][0:1, :], False, True, [t_consts, t_adab[i % 2]], [tb[bk]])
            P.cp("act", dests[i], banks[bk][:, :], [tb[bk]], [t_dests[i]])

    def rope(src, dst, G, cs, t_src, t_dst, t_cs, tmp, t_tmp):
        cosb = cs[:, 0:32].unsqueeze(1).to_broadcast([128, G, 32])
        sinb = cs[:, 32:64].unsqueeze(1).to_broadcast([128, G, 32])
        x1 = src[:, :, 0:32]
        x2 = src[:, :, 32:64]
        t1 = tmp[:, 0:G * 32].rearrange("p (g d) -> p g d", g=G)
        t2 = tmp[:, G * 32:2 * G * 32].rearrange("p (g d) -> p g d", g=G)
        t3 = tmp[:, 2 * G * 32:3 * G * 32].rearrange("p (g d) -> p g d", g=G)
        t4 = tmp[:, 3 * G * 32:4 * G * 32].rearrange("p (g d) -> p g d", g=G)
        P.tt("dve", t1, x1, cosb, ALU.mult, [t_src, t_cs], [t_tmp[0]])
        P.tt("dve", t2, x2, sinb, ALU.mult, [t_src, t_cs], [t_tmp[1]])
        P.tt("dve", dst[:, :, 0:32], t1, t2, ALU.subtract, [t_tmp[0], t_tmp[1]], [t_dst])
        P.tt("pool", t3, x2, cosb, ALU.mult, [t_src, t_cs], [t_tmp[2]])
        P.tt("pool", t4, x1, sinb, ALU.mult, [t_src, t_cs], [t_tmp[3]])
        P.tt("pool", dst[:, :, 32:64], t3, t4, ALU.add, [t_tmp[2], t_tmp[3]], [t_dst])

    def rstd_from_ss(ss, n, t_ss, scale):
        P.act(ss, ss, AF.Ln, [t_ss], [t_ss], bias=EPS, scale=scale)
        P.act(ss, ss, AF.Exp, [t_ss], [t_ss], scale=-0.5)

    uv_d = nc.dram_tensor("uvtab", [NEXP, 2048], BF16, kind="Internal").ap()
    t_uv = P.tok("uvtab")
    if do_b:
        A.off = base_off
        RB = 4
        uview = u_d.rearrange("(r p) d -> p r d", p=128)
        vview = v_d.rearrange("(r p) d -> p r d", p=128)
        uvview = uv_d.rearrange("(r p) d -> p r d", p=128)
        fu = [A.f32(RB * 1024) for _ in range(2)]
        fvv_ = [A.f32(RB * 1024) for _ in range(2)]
        fo = [A.bf16(RB * 2048) for _ in range(2)]
        t_fu = P.toks("fu", 2)
        t_fv_ = P.toks("fv_", 2)
        t_fo = P.toks("fo", 2)
        for it in range(NEXP // 128 // RB):
            b_ = it % 2
            P.ld("sp", fu[b_].rearrange("p (r d) -> p r d", r=RB), uview[:, it * RB:(it + 1) * RB, :], t_fu[b_])
            P.ld("sp", fvv_[b_].rearrange("p (r d) -> p r d", r=RB), vview[:, it * RB:(it + 1) * RB, :], t_fv_[b_])
            fov = fo[b_].rearrange("p (r d) -> p r d", r=RB)
            P.cp("act", fov[:, :, 0:1024], fu[b_].rearrange("p (r d) -> p r d", r=RB), [t_fu[b_]], [t_fo[b_]])
            P.cp("dve", fov[:, :, 1024:2048], fvv_[b_].rearrange("p (r d) -> p r d", r=RB), [t_fv_[b_]], [t_fo[b_]])
            P.ld("sp", uvview[:, it * RB:(it + 1) * RB, :], fov, t_fo[b_], reads=[t_fo[b_]], writes=[t_uv])
        P.barrier()

    if do_a1 or do_a2:
        A.off = base_off
        ropes = A.f32(NT * 64)
        t_rope = P.tok("rope")
        P.ld("sp", ropes.rearrange("p (t d) -> p t d", t=NT), rope_d[:, :, :], t_rope)
        lam2 = small(2)
        neglam = small(2)
        t_lam = P.tok("lam")
        gsub = A.f32(128)
        G1 = A.f32(1024)
        SH1 = A.f32(1024)
        GATE1 = A.f32(1024)
        t_mod = P.tok("mod1")
        kv_off = A.off
        lq = A.f32(256)
        t_lq = P.tok("lq")
        P.ld("sp", lq, gains_d[:, G_LQ:G_LQ + 256], t_lq)
        prod = A.f32(128)
        lq4 = lq.rearrange("p (a b d) -> p a b d", a=2, b=2)
        P.tt("dve", prod.rearrange("p (a d) -> p a d", a=2), lq4[:, :, 0, :], lq4[:, :, 1, :], ALU.mult, [t_lq], [t_lam])
        P.red(lam2[:, 0:2], prod.rearrange("p (a d) -> p a d", a=2), [t_lam], [t_lam])
        P.act(lam2[:, 0:2], lam2[:, 0:2], AF.Exp, [t_lam], [t_lam])
        P.tt("dve", neglam[:, 0:1], lam2[:, 1:2], lam2[:, 0:1], ALU.subtract, [t_lam], [t_lam])
        P.ts("dve", neglam[:, 0:1], neglam[:, 0:1], -LAMBDA_INIT, ALU.add, [t_lam], [t_lam])
        gs_t = A.f32(128)
        t_gs = P.tok("gs_t")
        P.ld("sp", gs_t, gains_d[:, G_SUB:G_SUB + 128], t_gs)
        P.ts("dve", gsub, gs_t, 1.0 - LAMBDA_INIT, ALU.mult, [t_gs], [t_lam])

        for s in range(NSEQ):
            A.off = kv_off
            P.barrier()
            wst = [A.f32(4096), A.f32(4096)]
            t_wst = P.toks("wst", 2)
            mtmp = [A.f32(512), A.f32(512)]
            t_mtmp = P.toks("mtmp", 2)
            gn1 = A.f32(1024)
            t_gn1 = P.tok("gn1")
            P.ld("sp", gn1, gains_d[:, G_N1:G_N1 + 1024], t_gn1)
            dests = [SH1[:, 0:512], SH1[:, 512:1024], mtmp[0], mtmp[1], GATE1[:, 0:512], GATE1[:, 512:1024]]
            t_dests = [t_mod, t_mod, t_mtmp[0], t_mtmp[1], t_mod, t_mod]
            compute_mod(s, [0, 1, 2, 3, 4, 5], dests, t_dests, wst, t_wst, [0, 1])
            for i in range(2):
                P.stt(G1[:, i * 512:(i + 1) * 512], mtmp[i], 1.0, gn1[:, i * 512:(i + 1) * 512], ALU.add, ALU.mult, [t_mtmp[i], t_gn1], [t_mod])

            for phase in ("a1", "a2"):
                if phase == "a1" and not do_a1:
                    continue
                if phase == "a2" and not do_a2:
                    continue
                A.off = kv_off
                P.barrier()
                a1 = phase == "a1"
                t_gA = P.tok("gA")
                t_w = P.tok("wA")
                gseg = {}

                def gload(name, col, n):
                    buf = A.f32(n)
                    P.ld("sp", buf, gains_d[:, col:col + n], t_gA)
                    gseg[name] = buf

                if a1:
                    gload("qlat", G_QLAT, 384); gload("kvlat", G_KVLAT, 256); gload("kpe", G_KPE, 64)
                    gload("q12", G_Q12, 768); gload("k4", G_K4, 512)
                    WCOL0, WN = 0, 704
                else:
                    gload("dqk", G_DQK, 1024)
                    WCOL0, WN = 704, 1536
                win_b = A.bf16(8 * WN)
                wout_b = A.bf16(4 * 1024)
                wofs = 0 if a1 else 512
                if a1:
                    wq_b = A.bf16(3 * 768)
                    wkv_b = A.bf16(2 * 1024)
                mark = A.off
                stage = [A.f32(1536), A.f32(1536)]
                t_stage = P.toks("stage", 2)
                ei = 0
                for k in range(8):
                    load_cast_weight(win_b[:, k * WN:(k + 1) * WN], win_d[k * 128:(k + 1) * 128, WCOL0:WCOL0 + WN], WN, stage, t_stage, t_w, k, ei); ei += 1
                for k in range(4):
                    load_cast_weight(wout_b[:, k * 1024:(k + 1) * 1024], wout_d[wofs + k * 128:wofs + (k + 1) * 128, :], 1024, stage, t_stage, t_w, k, ei); ei += 1
                if a1:
                    for k in range(3):
                        load_cast_weight(wq_b[:, k * 768:(k + 1) * 768], wq_d[k * 128:(k + 1) * 128, :], 768, stage, t_stage, t_w, k, ei); ei += 1
                    for k in range(2):
                        load_cast_weight(wkv_b[:, k * 1024:(k + 1) * 1024], wkv_d[k * 128:(k + 1) * 128, :], 1024, stage, t_stage, t_w, k, ei); ei += 1
                P.barrier()
                A.off = mark
                if a1:
                    kTn = A.bf16(4 * S)
                    kTp = A.bf16(S)
                    Vv = A.bf16(NT * 4 * 130)
                else:
                    kTn = A.bf16(4 * S)
                    kTp = None
                    Vv = A.bf16(NT * 4 * 130)
                kTn_v = kTn.rearrange("p (h s) -> p h s", h=4)
                Vv_v = Vv.rearrange("p (t h d) -> p t h d", t=NT, h=4)
                t_kv = P.toks("kv", NT)
                t_vinit = P.tok("vinit")
                P.memset("pool", Vv, 1.0, [t_vinit] + t_kv)
                xt = [A.f32(1024), A.f32(1024)]
                t_xt = P.toks("xt", 2)
                junk = A.bf16(1024)
                t_junk = P.tok("junk")
                sm = [A.f32(64), A.f32(64)]
                t_sm = P.toks("sm", 2)
                tmpf = A.f32(1024)
                t_tmpf = P.tok("tmpf")
                hb = A.bf16(1024)
                t_hb = P.tok("hb")
                hT = A.bf16(1024)
                t_hT = P.tok("hT")
                NPJ = 704 if a1 else 1536
                proj = A.f32(NPJ)
                t_proj = P.tok("proj")
                sq = A.f32(768 if a1 else 1024)
                t_sq = P.tok("sq")
                nrm = A.f32(64 if a1 else 1024)
                t_nrm = P.tok("nrm")
                rtmp = A.f32(4 * (4 if a1 else 16) * 32)
                t_rtmp = P.toks("rtmp", 4)
                rot = A.bf16(64 if a1 else 1024)
                t_rot = P.tok("rot")
                if a1:
                    qn = A.bf16(640)
                    t_qn = P.tok("qn")
                    qnT = A.bf16(640)
                    t_qnT = P.tok("qnT")
                    qf = A.f32(768)
                    t_qf = P.tok("qf")
                    qpe = A.f32(256)
                    t_qpe = P.tok("qpe")
                    qm = A.bf16(768)
                    t_qm = P.tok("qm")
                    kf = A.f32(512)
                    t_kf = P.tok("kf")
                    kn = A.bf16(512)
                    t_kn = P.tok("kn")
                    qTn = [A.bf16(4 * 512), A.bf16(4 * 512)]
                    qTp = [A.bf16(4 * 512), A.bf16(4 * 512)]
                    qTp_v = [q_.rearrange("p (h s) -> p h s", h=4) for q_ in qTp]
                else:
                    qTn = [A.bf16(4 * 512), A.bf16(4 * 512)]
                qTn_v = [q_.rearrange("p (h s) -> p h s", h=4) for q_ in qTn]
                t_qT = P.toks("qT", 2)
                NPT = 4
                PT = [A.bf16(512) for _ in range(NPT)]
                t_PT = P.toks("PT", NPT)
                mixed = [A.bf16(4 * 512), A.bf16(4 * 512)]
                mixed_v = [m_.rearrange("p (q c) -> p q c", q=4) for m_ in mixed]
                t_mixed = [P.toks("mixedA", 4), P.toks("mixedB", 4)]
                mT = A.bf16(512)
                t_mT = P.tok("mT")
                xr = xt
                t_xr = t_xt
                ytmp = A.f32(1024)
                t_ytmp = P.tok("ytmp")
                osm = A.f32(64)
                t_osm = P.tok("osm")
                oa = A.f32(128)
                t_oa = P.tok("oa")
                od = A.f32(128)
                t_od = P.tok("od")
                cntA = {"ti": 0, "pti": 0, "xi": 0}
                MB = (4, 5, 7) if a1 else (7, 7, 7)
                OB = (4, 5) if a1 else (7, 7)

                def tile_proc(c):
                    cp_ = c % 2
                    for j in range(4):
                        tl = c * 4 + j
                        row0 = s * S + tl * 128
                        ti = cntA["ti"]; cntA["ti"] += 1
                        xb = xt[ti % 2]; txb = t_xt[ti % 2]
                        smb = sm[ti % 2]; tsm = t_sm[ti % 2]
                        P.ld("sp", xb, x_d[row0:row0 + 128, :], txb)
                        P.act(junk, xb, AF.Square, [txb], [t_junk, tsm], accum=smb[:, 0:1])
                        rstd_from_ss(smb[:, 0:1], 1, tsm, 1.0 / 1024)
                        yield
                        P.stt(tmpf, xb, smb[:, 0:1], G1, ALU.mult, ALU.mult, [txb, tsm, t_mod], [t_tmpf])
                        P.tt("pool", hb, tmpf, SH1, ALU.add, [t_tmpf, t_mod], [t_hb])
                        yield
                        Tb = bankbf(6)
                        for k in range(8):
                            P.tr(Tb[:, k * 128:(k + 1) * 128], hb[:, k * 128:(k + 1) * 128], identb, [t_hb, t_cb], [tb[6]])
                        yield
                        P.cp("act", hT, Tb, [tb[6]], [t_hT])
                        yield
                        if a1:
                            chunks = [(0, 512), (512, 704)]
                        else:
                            chunks = [(0, 512), (512, 1024), (1024, 1536)]
                        for ci, (c0, c1) in enumerate(chunks):
                            bk = MB[ci]
                            for k in range(8):
                                P.mm(banks[bk][:, 0:c1 - c0], hT[:, k * 128:(k + 1) * 128],
                                     win_b[:, k * WN + c0:k * WN + c1], k == 0, k == 7, [t_hT, t_w], [tb[bk]])
                            yield
                            P.cp("act" if ci % 2 == 0 else "dve", proj[:, c0:c1], banks[bk][:, 0:c1 - c0], [tb[bk]], [t_proj])
                            yield
                        cs = ropes[:, tl * 64:(tl + 1) * 64]
                        if a1:
                            P.act(sq[:, 0:704], proj[:, 0:704], AF.Square, [t_proj], [t_sq])
                            ss11 = smb[:, 4:15]
                            P.red(ss11, sq[:, 0:704].rearrange("p (g d) -> p g d", d=64), [t_sq], [tsm])
                            yield
                            ss3 = smb[:, 16:19]
                            P.red(ss3[:, 0:1], ss11[:, 0:6], [tsm], [tsm])
                            P.red(ss3[:, 1:2], ss11[:, 6:10], [tsm], [tsm])
                            P.ts("dve", ss3[:, 0:1], ss3[:, 0:1], 1.0 / 384, ALU.mult, [tsm], [tsm])
                            P.ts("dve", ss3[:, 1:2], ss3[:, 1:2], 1.0 / 256, ALU.mult, [tsm], [tsm])
                            P.ts("dve", ss3[:, 2:3], ss11[:, 10:11], 1.0 / 64, ALU.mult, [tsm], [tsm])
                            yield
                            rstd_from_ss(ss3, 3, tsm, 1.0)
                            yield
                            P.stt(qn[:, 0:384], proj[:, 0:384], ss3[:, 0:1], gseg["qlat"], ALU.mult, ALU.mult, [t_proj, tsm, t_gA], [t_qn])
                            P.stt(qn[:, 384:640], proj[:, 384:640], ss3[:, 1:2], gseg["kvlat"], ALU.mult, ALU.mult, [t_proj, tsm, t_gA], [t_qn])
                            P.stt(nrm[:, 0:64], proj[:, 640:704], ss3[:, 2:3], gseg["kpe"], ALU.mult, ALU.mult, [t_proj, tsm, t_gA], [t_nrm])
                            yield
                            rope(nrm[:, 0:64].rearrange("p (g d) -> p g d", g=1), rot[:, 0:64].rearrange("p (g d) -> p g d", g=1), 1, cs,
                                 t_nrm, t_rot, t_rope, rtmp, t_rtmp)
                            Tb = bankbf(6)
                            for k in range(5):
                                P.tr(Tb[:, k * 128:(k + 1) * 128], qn[:, k * 128:(k + 1) * 128], identb, [t_qn, t_cb], [tb[6]])
                            yield
                            P.cp("act", qnT, Tb[:, 0:640], [tb[6]], [t_qnT])
                            yield
                            QB = (7, 4)
                            KB = (5, 7)
                            for cc in range(2):
                                for k in range(3):
                                    P.mm(banks[QB[cc]][:, 0:384], qnT[:, k * 128:(k + 1) * 128], wq_b[:, k * 768 + cc * 384:k * 768 + (cc + 1) * 384],
                                         k == 0, k == 2, [t_qnT, t_w], [tb[QB[cc]]])
                            yield
                            for cc in range(2):
                                P.act(sq[:, cc * 384:(cc + 1) * 384], banks[QB[cc]][:, 0:384], AF.Square, [tb[QB[cc]]], [t_sq])
                            yield
                            ss12 = smb[:, 20:32]
                            P.red(ss12, sq[:, 0:768].rearrange("p (g d) -> p g d", d=64), [t_sq], [tsm])
                            ss12v = ss12.rearrange("p (h t) -> p h t", t=3)
                            r12 = smb[:, 32:44]
                            r12v = r12.rearrange("p (h t) -> p h t", t=3)
                            P.tt("dve", r12v[:, :, 0], ss12v[:, :, 0], ss12v[:, :, 1], ALU.add, [tsm], [tsm])
                            yield
                            P.ts("dve", r12v[:, :, 0], r12v[:, :, 0], 1.0 / 128, ALU.mult, [tsm], [tsm])
                            P.ts("dve", r12v[:, :, 2], ss12v[:, :, 2], 1.0 / 64, ALU.mult, [tsm], [tsm])
                            yield
                            P.cp("dve", r12v[:, :, 1], r12v[:, :, 0], [tsm], [tsm])
                            yield
                            rstd_from_ss(r12, 12, tsm, 1.0)
                            yield
                            for cc in range(2):
                                P.tt("dve", qf[:, cc * 384:(cc + 1) * 384].rearrange("p (g d) -> p g d", d=64),
                                     banks[QB[cc]][:, 0:384].rearrange("p (g d) -> p g d", d=64),
                                     r12[:, cc * 6:(cc + 1) * 6].unsqueeze(2).to_broadcast([128, 6, 64]), ALU.mult, [tb[QB[cc]], tsm], [t_qf])
                            yield
                            for cc in range(2):
                                for k in range(2):
                                    P.mm(banks[KB[cc]][:, :], qnT[:, (3 + k) * 128:(4 + k) * 128], wkv_b[:, k * 1024 + cc * 512:k * 1024 + (cc + 1) * 512],
                                         k == 0, k == 1, [t_qnT, t_w], [tb[KB[cc]]])
                            qfv = qf.rearrange("p (h d) -> p h d", h=4)
                            gq = gseg["q12"].rearrange("p (h d) -> p h d", h=4)
                            qmv = qm.rearrange("p (h d) -> p h d", h=4)
                            P.tt("dve", qmv[:, :, 0:128], qfv[:, :, 0:128], gq[:, :, 0:128], ALU.mult, [t_qf, t_gA], [t_qm])
                            qpev = qpe.rearrange("p (h d) -> p h d", h=4)
                            P.tt("pool", qpev, qfv[:, :, 128:192], gq[:, :, 128:192], ALU.mult, [t_qf, t_gA], [t_qpe])
                            yield
                            rope(qpev, qmv[:, :, 128:192], 4, cs, t_qpe, t_qm, t_rope, rtmp, t_rtmp)
                            yield
                            for cc in range(2):
                                kvv = banks[KB[cc]][:, :].rearrange("p (h d) -> p h d", h=2)
                                P.act(sq[:, cc * 256:(cc + 1) * 256].rearrange("p (h d) -> p h d", h=2), kvv[:, :, 0:128], AF.Square, [tb[KB[cc]]], [t_sq])
                                P.cp("act", Vv_v[:, tl, 2 * cc:2 * cc + 2, 0:128], kvv[:, :, 128:256], [tb[KB[cc]], t_vinit], [t_kv[tl]])
                            yield
                            ssk = smb[:, 44:48]
                            P.red(ssk, sq[:, 0:512].rearrange("p (h d) -> p h d", h=4), [t_sq], [tsm])
                            yield
                            rstd_from_ss(ssk, 4, tsm, 1.0 / 128)
                            yield
                            for cc in range(2):
                                kvv = banks[KB[cc]][:, :].rearrange("p (h d) -> p h d", h=2)
                                P.tt("dve", kf[:, cc * 256:(cc + 1) * 256].rearrange("p (h d) -> p h d", h=2), kvv[:, :, 0:128],
                                     ssk[:, 2 * cc:2 * cc + 2].unsqueeze(2).to_broadcast([128, 2, 128]), ALU.mult, [tb[KB[cc]], tsm], [t_kf])
                            yield
                            P.tt("pool", kn, kf, gseg["k4"], ALU.mult, [t_kf, t_gA], [t_kn])
                            yield
                            Tb = bankbf(6)
                            for h in range(4):
                                P.tr(Tb[:, h * 128:(h + 1) * 128], qmv[:, h, 0:128], identb, [t_qm, t_cb], [tb[6]])
                                P.tr(Tb[:, 512 + h * 128:512 + (h + 1) * 128], kn[:, h * 128:(h + 1) * 128], identb, [t_kn, t_cb], [tb[6]])
                            yield
                            P.cp("act", qTn_v[cp_][:, :, j * 128:(j + 1) * 128], Tb[:, 0:512].rearrange("p (h t) -> p h t", h=4), [tb[6]], [t_qT[cp_]])
                            P.cp("dve", kTn_v[:, :, tl * 128:(tl + 1) * 128], Tb[:, 512:1024].rearrange("p (h t) -> p h t", h=4), [tb[6]], [t_kv[tl]])
                            yield
                            Tb = bankbf(6)
                            for h in range(4):
                                P.tr(Tb[0:64, h * 128:(h + 1) * 128], qmv[:, h, 128:192], identb, [t_qm, t_cb], [tb[6]])
                            P.tr(Tb[0:64, 512:640], rot[:, 0:64], identb, [t_rot, t_cb], [tb[6]])
                            yield
                            P.cp("act", qTp_v[cp_][0:64, :, j * 128:(j + 1) * 128], Tb[0:64, 0:512].rearrange("p (h t) -> p h t", h=4), [tb[6]], [t_qT[cp_]])
                            P.cp("dve", kTp[0:64, tl * 128:(tl + 1) * 128], Tb[0:64, 512:640], [tb[6]], [t_kv[tl]])
                            yield
                        else:
                            P.act(sq[:, 0:1024], proj[:, 0:1024], AF.Square, [t_proj], [t_sq])
                            ss16 = smb[:, 4:20]
                            P.red(ss16, sq[:, 0:1024].rearrange("p (g d) -> p g d", d=64), [t_sq], [tsm])
                            yield
                            rstd_from_ss(ss16, 16, tsm, 1.0 / 64)
                            yield
                            P.tt("dve", tmpf.rearrange("p (g d) -> p g d", d=64), proj[:, 0:1024].rearrange("p (g d) -> p g d", d=64),
                                 ss16.unsqueeze(2).to_broadcast([128, 16, 64]), ALU.mult, [t_proj, tsm], [t_tmpf])
                            yield
                            P.tt("pool", nrm, tmpf, gseg["dqk"], ALU.mult, [t_tmpf, t_gA], [t_nrm])
                            yield
                            rope(nrm.rearrange("p (g d) -> p g d", d=64), rot.rearrange("p (g d) -> p g d", d=64), 16, cs,
                                 t_nrm, t_rot, t_rope, rtmp, t_rtmp)
                            P.cp("act", Vv_v[:, tl, :, 0:128], proj[:, 1024:1536].rearrange("p (h d) -> p h d", h=4), [t_proj, t_vinit], [t_kv[tl]])
                            yield
                            Tb = bankbf(6)
                            for h in range(8):
                                P.tr(Tb[:, h * 128:(h + 1) * 128], rot[:, h * 128:(h + 1) * 128], identb, [t_rot, t_cb], [tb[6]])
                            yield
                            P.cp("act", qTn_v[cp_][:, :, j * 128:(j + 1) * 128], Tb[:, 0:512].rearrange("p (h t) -> p h t", h=4), [tb[6]], [t_qT[cp_]])
                            P.cp("dve", kTn_v[:, :, tl * 128:(tl + 1) * 128], Tb[:, 512:1024].rearrange("p (h t) -> p h t", h=4), [tb[6]], [t_kv[tl]])
                            yield

                def emit_ST(c, h, m, kb):
                    cp_ = c % 2
                    jd = kb - 4 * c
                    q0 = 0 if jd < 0 else jd * 128
                    sb = kb % 2
                    if a1:
                        P.mm(banks[sb][:, q0:512], kTn_v[:, h, kb * 128:(kb + 1) * 128], qTn_v[cp_][:, h, q0:512], True, False,
                             [t_kv[kb], t_qT[cp_]], [tb[sb]])
                        P.mm(banks[sb][:, q0:512], kTp[0:64, kb * 128:(kb + 1) * 128], qTp_v[cp_][0:64, h, q0:512], False, True,
                             [t_kv[kb], t_qT[cp_]], [tb[sb]])
                        scale = 192 ** -0.5
                    else:
                        P.mm(banks[sb][:, q0:512], kTn_v[m * 64:(m + 1) * 64, h, kb * 128:(kb + 1) * 128],
                             qTn_v[cp_][m * 64:(m + 1) * 64, h, q0:512], True, True, [t_kv[kb], t_qT[cp_]], [tb[sb]])
                        scale = 64 ** -0.5
                    pti = cntA["pti"]; cntA["pti"] += 1
                    pb = PT[pti % NPT]; tpb = t_PT[pti % NPT]
                    P.act(pb[:, q0:512], banks[sb][:, q0:512], AF.Exp, [tb[sb]], [tpb], scale=scale)
                    if jd >= 0:
                        P.tt("pool", pb[:, q0:q0 + 128], pb[:, q0:q0 + 128], trib, ALU.mult, [tpb, t_cb], [tpb])
                    return (c, h, m, kb, pb, tpb)

                def emit_PV(c, h, m, kb, pb, tpb):
                    cp_ = c % 2
                    jd = kb - 4 * c
                    nkb = 4 * c + 4
                    ob = (2, 3) if m == 0 else (4, 5)
                    for qb in range(max(jd, 0), 4):
                        bk = ob[qb // 2]
                        ov = banks[bk][:, 0:260].rearrange("p (a d) -> p a d", a=2)
                        P.mm(ov[:, qb % 2, 0:129], pb[:, qb * 128:(qb + 1) * 128], Vv_v[:, kb, h, 0:129],
                             kb == 0 and qb % 2 == 0, kb == 4 * c + qb, [tpb, t_kv[kb]], [tb[bk]])
                    nmap = 1 if a1 else 2
                    if kb == nkb - 1 and m == nmap - 1:
                        for qb in range(4):
                            if a1:
                                bk = (2, 3)[qb // 2]
                                ov = banks[bk][:, 0:260].rearrange("p (a d) -> p a d", a=2)
                                P.recip(osm[:, qb:qb + 1], ov[:, qb % 2, 128:129], [tb[bk]], [t_osm])
                                P.ts("dve", mixed_v[cp_][:, qb, h * 128:(h + 1) * 128], ov[:, qb % 2, 0:128], osm[:, qb:qb + 1], ALU.mult, [tb[bk], t_osm], [t_mixed[cp_][qb]])
                            else:
                                b1 = (2, 3)[qb // 2]
                                b2 = (4, 5)[qb // 2]
                                o1 = banks[b1][:, 0:260].rearrange("p (a d) -> p a d", a=2)
                                o2 = banks[b2][:, 0:260].rearrange("p (a d) -> p a d", a=2)
                                P.recip(osm[:, 0:1], o1[:, qb % 2, 128:129], [tb[b1]], [t_osm])
                                P.recip(osm[:, 1:2], o2[:, qb % 2, 128:129], [tb[b2]], [t_osm])
                                P.ts("dve", oa, o2[:, qb % 2, 0:128], osm[:, 1:2], ALU.mult, [tb[b2], t_osm, t_lam], [t_oa], s2=neglam[:, 0:1], op1=ALU.mult)
                                P.stt(od, o1[:, qb % 2, 0:128], osm[:, 0:1], oa, ALU.mult, ALU.add, [tb[b1], t_osm, t_oa], [t_od])
                                P.act(oa, od, AF.Square, [t_od], [t_oa, t_osm], accum=osm[:, 2:3])
                                rstd_from_ss(osm[:, 2:3], 1, t_osm, 1.0 / 128)
                                P.stt(mixed_v[cp_][:, qb, h * 128:(h + 1) * 128], od, osm[:, 2:3], gsub, ALU.mult, ALU.mult, [t_od, t_osm, t_lam], [t_mixed[cp_][qb]])

                def outproj(c):
                    cp_ = c % 2
                    for qb in range(4):
                        tl = c * 4 + qb
                        row0 = s * S + tl * 128
                        gt = row0 // 128
                        Tb = bankbf(6)
                        for k in range(4):
                            P.tr(Tb[:, k * 128:(k + 1) * 128], mixed_v[cp_][:, qb, k * 128:(k + 1) * 128], identb, [t_mixed[cp_][qb], t_cb], [tb[6]])
                        yield
                        P.cp("act", mT, Tb[:, 0:512], [tb[6]], [t_mT])
                        xi = cntA["ti"]; cntA["ti"] += 1
                        xrb = xr[xi % 2]; txr = t_xr[xi % 2]
                        if a1:
                            P.ld("sp", xrb, x_d[row0:row0 + 128, :], txr)
                        else:
                            P.ld("sp", xrb, x1_d[row0:row0 + 128, :], txr, reads=[tx1[gt]])
                        yield
                        for cc in range(2):
                            bk = OB[cc]
                            for k in range(4):
                                P.mm(banks[bk][:, :], mT[:, k * 128:(k + 1) * 128],
                                     wout_b[:, k * 1024 + cc * 512:k * 1024 + (cc + 1) * 512],
                                     k == 0, k == 3, [t_mT, t_w], [tb[bk]])
                            yield
                            P.tt("dve", ytmp[:, cc * 512:(cc + 1) * 512], banks[bk][:, :], GATE1[:, cc * 512:(cc + 1) * 512], ALU.mult, [tb[bk], t_mod], [t_ytmp])
                            yield
                        P.tt("pool", xrb, xrb, ytmp, ALU.add, [txr, t_ytmp], [txr])
                        P.ld("sp", x1_d[row0:row0 + 128, :], xrb, txr, reads=[txr], writes=[tx1[gt]])
                        yield

                def chain(*gens):
                    for g_ in gens:
                        if g_ is not None:
                            yield from g_

                if stage_lim >= 1:
                    for _ in tile_proc(0):
                        pass
                    nmap = 1 if a1 else 2
                    for c in range(NCH):
                        bg = chain(outproj(c - 1) if (c > 0 and stage_lim >= 3) else None,
                                   tile_proc(c + 1) if c + 1 < NCH else None)
                        if stage_lim >= 2:
                            nkb = 4 * c + 4
                            steps = [(h, m, kb) for h in range(4) for m in range(nmap) for kb in range(nkb)]
                            est = 24 + (140 if a1 else 80)
                            pulls = max(1, -(-est // len(steps)))
                            pendA = []
                            for (h, m, kb) in steps:
                                pendA.append(emit_ST(c, h, m, kb))
                                if len(pendA) > 2:
                                    emit_PV(*pendA.pop(0))
                                for _ in range(pulls):
                                    next(bg, None)
                            while pendA:
                                emit_PV(*pendA.pop(0))
                        for _ in bg:
                            pass
                    if stage_lim >= 3:
                        for _ in outproj(NCH - 1):
                            pass

    if do_b:
        A.off = base_off
        P.barrier()
        NTT = TT // 128
        gB = A.f32(1024)
        t_gB = P.tok("gB")
        P.ld("sp", gB, gains_d[:, G_N2:G_N2 + 1024], t_gB)
        pwq_b = A.bf16(8 * 2048)
        keys_b = A.bf16(16 * 128)
        t_wB = P.tok("wB")
        wst = [A.f32(4096), A.f32(4096)]
        t_wst = P.toks("wstB", 2)
        stage = wst
        t_stage = t_wst
        ei = 0
        for k in range(8):
            load_cast_weight(pwq_b[:, k * 2048:(k + 1) * 2048], pwq_d[k * 128:(k + 1) * 128, :], 2048, stage, t_stage, t_wB, k, ei); ei += 1
        load_cast_weight(keys_b, keys_d.rearrange("p c n -> p (c n)"), 2048, stage, t_stage, t_wB, 0, ei); ei += 1
        G2 = A.f32(1024)
        SH2 = A.f32(1024)
        GATE2 = [A.f32(1024) for _ in range(NSEQ)]
        t_mod2 = P.tok("mod2")
        t_gate2 = P.toks("gate2", NSEQ)
        mtmp = [A.f32(512), A.f32(512)]
        t_mtmp = P.toks("mtmpB", 2)
        xt = [A.f32(1024), A.f32(1024)]
        t_xt = P.toks("xB", 2)
        junk = A.bf16(1024)
        t_junk = P.tok("junkB")
        sm = A.f32(64)
        t_sm = P.tok("smB")
        h2f = A.f32(1024)
        t_h2f = P.tok("h2f")
        h2b = [A.bf16(1024), A.bf16(1024)]
        t_h2b = P.toks("h2b", 2)
        h2T = A.bf16(1024)
        t_h2T = P.tok("h2T")
        qT = A.bf16(16 * 128)
        t_qTB = P.tok("qTB")
        sc = wst[0][:, 0:2048]
        t_sc = t_wst[0]
        sc2 = [A.f32(128) for _ in range(4)]
        t_sc2 = P.toks("sc2", 4)
        sv = A.f32(256)
        t_svc = P.toks("svc", 16)
        si = A.u32(256)
        t_sic = P.toks("sic", 16)
        sif = A.f32(256)
        t_sif = P.tok("sif")
        cand = wst[0][:, 2048:4096]
        t_cand = t_wst[0]
        cand2 = [A.f32(256) for _ in range(4)]
        t_cand2 = P.toks("cand2", 4)
        fv = A.f32(128)
        t_fvh = P.toks("fvh", 8)
        fpos = A.u32(128)
        t_fph = P.toks("fph", 8)
        k0f = A.f32(128)
        k1f = A.f32(128)
        t_k = P.tok("k01")
        oh = wst[1][:, 0:2048]
        t_oh = t_wst[1]
        e0 = A.f32(128)
        e1 = A.f32(128)
        t_e = P.tok("e01")
        eidx = [A.u32(128), A.u32(128)]
        t_eidx = P.toks("eidx", 2)
        gw = [A.f32(128), A.f32(128)]
        t_gw = P.toks("gw", 2)
        araw = A.f32(128)
        cg = A.f32(128)
        NR = 8
        t_ar = P.toks("araw", NR)
        t_cf = P.toks("cg", NR)
        NG = 10
        uvb = [A.bf16(2048) for _ in range(NG)]
        t_uvb = P.toks("uvb", NG)
        dg = [A.bf16(128) for _ in range(4)]
        t_dg = P.toks("dg", 4)
        ujunk = A.bf16(1024)
        t_ujunk = P.tok("ujunk")
        prod = [A.bf16(1024) for _ in range(4)]
        t_prod = P.toks("prod", 4)
        yout = A.f32(1024)
        t_yout = P.tok("yout")
        src = x1_d if (do_a1 or do_a2) else x_d

        def pre(gt):
            s, tl = divmod(gt, NT)
            par = gt % 2
            row0 = gt * 128
            if tl == 0:
                compute_mod(s, [6, 7, 8, 9, 10, 11],
                            [SH2[:, 0:512], SH2[:, 512:1024], mtmp[0], mtmp[1], GATE2[s][:, 0:512], GATE2[s][:, 512:1024]],
                            [t_mod2, t_mod2, t_mtmp[0], t_mtmp[1], t_gate2[s], t_gate2[s]], wst, t_wst, [7, 0])
                for i in range(2):
                    P.stt(G2[:, i * 512:(i + 1) * 512], mtmp[i], 1.0, gB[:, i * 512:(i + 1) * 512], ALU.add, ALU.mult, [t_mtmp[i], t_gB], [t_mod2])
                yield
            xb = xt[par]; txb = t_xt[par]
            hb = h2b[par]; thb = t_h2b[par]
            P.ld("sp", xb, src[row0:row0 + 128, :], txb, reads=[tx1[gt]])
            P.act(junk, xb, AF.Square, [txb], [t_junk, t_sm], accum=sm[:, 0:1])
            rstd_from_ss(sm[:, 0:1], 1, t_sm, 1.0 / 1024)
            P.stt(h2f, xb, sm[:, 0:1], G2, ALU.mult, ALU.mult, [txb, t_sm, t_mod2], [t_h2f])
            P.tt("dve", hb, h2f, SH2, ALU.add, [t_h2f, t_mod2], [thb])
            yield
            Tb = bankbf(6)
            for k in range(8):
                P.tr(Tb[:, k * 128:(k + 1) * 128], hb[:, k * 128:(k + 1) * 128], identb, [thb, t_cb], [tb[6]])
            P.cp("act", h2T, Tb, [tb[6]], [t_h2T])
            yield
            for g4 in range(4):
                bk = (7, 0)[g4 % 2]
                for cc in range(4):
                    c16 = g4 * 4 + cc
                    for k in range(8):
                        P.mm(banks[bk][:, cc * 128:(cc + 1) * 128], pwq_b[:, k * 2048 + c16 * 128:k * 2048 + (c16 + 1) * 128],
                             h2T[:, k * 128:(k + 1) * 128], k == 0 and cc == 0, k == 7, [t_wB, t_h2T], [tb[bk]])
                P.cp("act", qT[:, g4 * 512:(g4 + 1) * 512], banks[bk][:, :], [tb[bk]], [t_qTB])
                yield
            for g4 in range(4):
                bk = (1, 7)[g4 % 2]
                for cc in range(4):
                    c16 = g4 * 4 + cc
                    P.mm(banks[bk][:, cc * 128:(cc + 1) * 128], qT[:, c16 * 128:(c16 + 1) * 128], keys_b[:, c16 * 128:(c16 + 1) * 128],
                         cc == 0, True, [t_qTB, t_wB], [tb[bk]])
                P.cp("act", sc[:, g4 * 512:(g4 + 1) * 512], banks[bk][:, :], [tb[bk]], [t_sc])
                yield
            svv = sv.rearrange("p (c k) -> p c k", c=16)
            siv = si.rearrange("p (c k) -> p c k", c=16)
            f_max = lambda o, i_: (lambda e: e.max(out=o, in_=i_))
            f_mr = lambda o, r, i_: (lambda e: e.match_replace(out=o, in_to_replace=r, in_values=i_, imm_value=-1e30))
            f_mi = lambda o, m_, i_: (lambda e: e.max_index(out=o, in_max=m_, in_values=i_))
            for g in range(4):
                cs4 = range(4 * g, 4 * g + 4)
                for c16 in cs4:
                    P.op("dve", f_max(svv[:, c16, 0:8], sc[:, c16 * 128:(c16 + 1) * 128]), [t_sc], [t_svc[c16]])
                for c16 in cs4:
                    P.op("dve", f_mr(sc2[c16 % 4], svv[:, c16, 0:8], sc[:, c16 * 128:(c16 + 1) * 128]), [t_sc, t_svc[c16]], [t_sc2[c16 % 4]])
                for c16 in cs4:
                    P.op("dve", f_max(svv[:, c16, 8:16], sc2[c16 % 4]), [t_sc2[c16 % 4]], [t_svc[c16]])
                for c16 in cs4:
                    P.op("dve", f_mi(siv[:, c16, 0:8], svv[:, c16, 0:8], sc[:, c16 * 128:(c16 + 1) * 128]), [t_sc, t_svc[c16]], [t_sic[c16]])
                for c16 in cs4:
                    P.op("dve", f_mi(siv[:, c16, 8:16], svv[:, c16, 8:16], sc[:, c16 * 128:(c16 + 1) * 128]), [t_sc, t_svc[c16]], [t_sic[c16]])
                yield
            P.cp("dve", sif, si, t_sic, [t_sif])
            sv4 = sv.rearrange("p (h t k) -> p h t k", h=8, t=2)
            candv = cand.rearrange("p (h a b) -> p h a b", h=8, a=16)
            P.tt("dve", candv, sv4[:, :, 0, :].unsqueeze(3).to_broadcast([128, 8, 16, 16]),
                 sv4[:, :, 1, :].unsqueeze(2).to_broadcast([128, 8, 16, 16]), ALU.add, t_svc, [t_cand])
            yield
            fvv = fv.rearrange("p (h k) -> p h k", h=8)
            fpv = fpos.rearrange("p (h k) -> p h k", h=8)
            for g in range(2):
                hs4 = range(4 * g, 4 * g + 4)
                for h in hs4:
                    P.op("dve", f_max(fvv[:, h, 0:8], cand[:, h * 256:(h + 1) * 256]), [t_cand], [t_fvh[h]])
                for h in hs4:
                    P.op("dve", f_mr(cand2[h % 4], fvv[:, h, 0:8], cand[:, h * 256:(h + 1) * 256]), [t_cand, t_fvh[h]], [t_cand2[h % 4]])
                for h in hs4:
                    P.op("dve", f_max(fvv[:, h, 8:16], cand2[h % 4]), [t_cand2[h % 4]], [t_fvh[h]])
                for h in hs4:
                    P.op("dve", f_mi(fpv[:, h, 0:8], fvv[:, h, 0:8], cand[:, h * 256:(h + 1) * 256]), [t_cand, t_fvh[h]], [t_fph[h]])
                for h in hs4:
                    P.op("dve", f_mi(fpv[:, h, 8:16], fvv[:, h, 8:16], cand[:, h * 256:(h + 1) * 256]), [t_cand, t_fvh[h]], [t_fph[h]])
                yield
            P.cp("dve", k1f, fpos, t_fph, [t_k])
            P.tt("dve", oh.rearrange("p (a b) -> p a b", b=16), k1f.unsqueeze(2).to_broadcast([128, 128, 16]),
                 thr16.unsqueeze(1).to_broadcast([128, 128, 16]), ALU.is_ge, [t_k, t_consts], [t_oh])
            P.red(k0f, oh.rearrange("p (a b) -> p a b", b=16), [t_oh], [t_k])
            P.stt(k1f, k0f, -16.0, k1f, ALU.mult, ALU.add, [t_k], [t_k])
            yield
            sif4 = sif.rearrange("p (h t k) -> p h t k", h=8, t=2)
            for t_, (kf_, e_) in enumerate(((k0f, e0), (k1f, e1))):
                P.tt("dve", oh.rearrange("p (a b) -> p a b", b=16), kf_.unsqueeze(2).to_broadcast([128, 128, 16]),
                     iota16.unsqueeze(1).to_broadcast([128, 128, 16]), ALU.is_equal, [t_k, t_consts], [t_oh])
                P.tt("dve", oh.rearrange("p (h a b) -> p h a b", h=8, a=16), oh.rearrange("p (h a b) -> p h a b", h=8, a=16),
                     sif4[:, :, t_, :].unsqueeze(2).to_broadcast([128, 8, 16, 16]), ALU.mult, [t_oh, t_sif], [t_oh])
                P.red(e_, oh.rearrange("p (a b) -> p a b", b=16), [t_oh], [t_e])
                yield
            P.stt(e0, e0, 128.0, e1, ALU.mult, ALU.add, [t_e], [t_e])
            P.cp("dve", eidx[par], e0, [t_e], [t_eidx[par]])
            gwp = gw[par]; tgw = t_gw[par]
            P.tt("dve", gwp.rearrange("p (h k) -> p h k", h=8), fvv, fvv[:, :, 0:1].to_broadcast([128, 8, 16]), ALU.subtract, t_fvh, [tgw])
            P.act(gwp, gwp, AF.Exp, [tgw], [tgw])
            P.red(sm[:, 8:16], gwp.rearrange("p (h k) -> p h k", h=8), [tgw], [t_sm])
            P.recip(sm[:, 8:16], sm[:, 8:16], [t_sm], [t_sm])
            P.tt("dve", gwp.rearrange("p (h k) -> p h k", h=8), gwp.rearrange("p (h k) -> p h k", h=8),
                 sm[:, 8:16].unsqueeze(2).to_broadcast([128, 8, 16]), ALU.mult, [tgw, t_sm], [tgw])
            yield

        cnt = {"gi": 0, "di": 0}

        def slot_front(gt, sl):
            par = gt % 2
            gi = cnt["gi"]; cnt["gi"] += 1
            ub = uvb[gi % NG]; tub = t_uvb[gi % NG]
            P.dma("pool", (lambda o, ix: (lambda e: e.indirect_dma_start(out=o, out_offset=None, in_=uv_d[:, :],
                  in_offset=bass.IndirectOffsetOnAxis(ap=ix, axis=0))))(ub, eidx[par][:, sl:sl + 1]), tub, reads=[t_eidx[par], t_uv], writes=[tub])
            tar = t_ar[sl % NR]; tcf = t_cf[sl % NR]
            pr = prod[gi % 4]; tpr = t_prod[gi % 4]
            P.tt("dve", pr, ub[:, 0:1024], h2b[par], ALU.mult, [tub, t_h2b[par]], [tpr])
            P.act(ujunk, pr, AF.Copy, [tpr], [t_ujunk, tar], accum=araw[:, sl:sl + 1])
            P.act(cg[:, sl:sl + 1], araw[:, sl:sl + 1], AF.Gelu, [tar], [tcf])
            return (gt, sl, ub, tub)

        def slot_back(gt, sl, ub, tub):
            par = gt % 2
            yb = (2, 3) if par == 0 else (4, 5)
            tcf = t_cf[sl % NR]
            di = cnt["di"]; cnt["di"] += 1
            db = dg[di % 4]; tdb = t_dg[di % 4]
            P.ts("dve", db, identb, cg[:, sl:sl + 1], ALU.mult, [t_cb, tcf, t_gw[par]], [tdb], s2=gw[par][:, sl:sl + 1], op1=ALU.mult)
            for cc in range(2):
                bk = yb[cc]
                P.mm(banks[bk][:, :], db, ub[:, 1024 + cc * 512:1024 + (cc + 1) * 512], sl == 0, sl == 127, [tdb, tub], [tb[bk]])
            if sl == 127:
                fin(gt)

        def fin(gt):
            s, tl = divmod(gt, NT)
            par = gt % 2
            yb = (2, 3) if par == 0 else (4, 5)
            row0 = gt * 128
            for cc in range(2):
                bk = yb[cc]
                P.tt("dve", yout[:, cc * 512:(cc + 1) * 512], banks[bk][:, :], GATE2[s][:, cc * 512:(cc + 1) * 512], ALU.mult, [tb[bk], t_gate2[s]], [t_yout])
            P.tt("pool", yout, yout, xt[par], ALU.add, [t_yout, t_xt[par]], [t_yout])
            P.ld("sp", out_d[row0:row0 + 128, :], yout, t_yout, reads=[t_yout], writes=[], final=True)

        for _ in pre(0):
            pass
        SKEW = 3
        pend = []
        for gt in range(NTT):
            gen = pre(gt + 1) if gt + 1 < NTT else None
            for sl in range(128):
                pend.append(slot_front(gt, sl))
                if len(pend) > SKEW:
                    slot_back(*pend.pop(0))
                if gen is not None and sl % 2 == 1 and sl >= 8:
                    next(gen, None)
            if gen is not None:
                for _ in gen:
                    pass
        while pend:
            slot_back(*pend.pop(0))
    else:
        A.off = base_off
        P.barrier()
        cb = [A.f32(1024), A.f32(1024)]
        t_cbuf = P.toks("cpb", 2)
        for gt in range(TT // 128):
            P.ld("sp", cb[gt % 2], x1_d[gt * 128:(gt + 1) * 128, :], t_cbuf[gt % 2], reads=[tx1[gt]])
            P.ld("sp", out_d[gt * 128:(gt + 1) * 128, :], cb[gt % 2], t_cbuf[gt % 2], reads=[t_cbuf[gt % 2]], writes=[], final=True)

    P.arena_hw = A.hw
    P.emit()
    return nc, P


def rope_table(S):
    half = 32
    inv = (1.0 / (10000.0 ** (np.arange(half, dtype=np.float32) / np.float32(half)))).astype(np.float32)
    ang = np.arange(S, dtype=np.float32)[:, None] * inv[None, :]
    cs = np.concatenate([np.cos(ang), np.sin(ang)], axis=-1).astype(np.float32)
    return np.ascontiguousarray(cs.reshape(S // 128, 128, 64).transpose(1, 0, 2))


def make_consts():
    c = np.zeros((128, C_TOT), np.float32)
    c[:, C_ID:C_ID + 128] = np.eye(128, dtype=np.float32)
    k = np.arange(128)
    c[:, C_TRI:C_TRI + 128] = (k[:, None] <= k[None, :]).astype(np.float32)
    c[:, C_IOTA:C_IOTA + 16] = np.arange(16, dtype=np.float32)[None, :]
    c[:, C_ONES:C_ONES + 128] = 1.0
    c[:, C_THR:C_THR + 15] = 16.0 * np.arange(1, 16, dtype=np.float32)[None, :]
    c[:, C_THR + 15] = 1e9
    return c


def host_layout(inp, S, NSEQ, ncores):
    f = lambda a: np.ascontiguousarray(np.asarray(a, dtype=np.float32))
    x = f(inp["x"])
    c = f(inp["c"])
    rep = lambda v: np.broadcast_to(f(v).reshape(1, -1), (128, f(v).size))
    mqg = f(inp["mla_q_g"])[0]
    mkg = f(inp["mla_k_g"])[0]
    gains = np.concatenate([
        rep(inp["norm1_g"][0]), rep(inp["mla_q_lat_g"][0]), rep(inp["mla_kv_lat_g"][0]),
        rep(mkg[128:192]),
        rep(np.tile(f(inp["diff_q_g"])[0], 8)), rep(np.tile(f(inp["diff_k_g"])[0], 8)),
        rep(np.tile(mqg, 4)), rep(np.tile(mkg[:128], 4)),
        rep(inp["diff_subln_g"][0]), rep(inp["norm2_g"][0]),
        rep(inp["diff_lq1"][0]), rep(inp["diff_lk1"][0]), rep(inp["diff_lq2"][0]), rep(inp["diff_lk2"][0]),
    ], axis=1)
    gains = np.ascontiguousarray(gains, dtype=np.float32)
    assert gains.shape[1] == G_TOT
    keysT = np.ascontiguousarray(f(inp["peer_sub_keys"])[0].reshape(16, 128, 128).transpose(2, 0, 1))
    shared = {
        "ada_w": f(inp["ada_w"])[0], "ada_b": f(inp["ada_b"])[0].reshape(1, -1),
        "w_in": f(inp["w_in"])[0], "w_q_up": f(inp["mla_w_q_up"])[0], "w_kv_up": f(inp["mla_w_kv_up"])[0],
        "w_out": f(inp["w_out"])[0], "peer_w_q": f(inp["peer_w_q"])[0], "keysT": keysT,
        "peer_u": f(inp["peer_u"])[0], "peer_v": f(inp["peer_v"])[0],
        "gains": gains, "rope": rope_table(S), "consts": make_consts(),
    }
    maps = []
    for i in range(ncores):
        xs = np.ascontiguousarray(x[i * NSEQ:(i + 1) * NSEQ].reshape(NSEQ * S, D))
        cs = c[i * NSEQ:(i + 1) * NSEQ]
        cT = np.ascontiguousarray(cs.reshape(NSEQ, 8, 128).transpose(2, 0, 1))
        m = dict(shared)
        m["x"] = xs
        m["cT"] = cT
        maps.append(m)
    return maps


_CACHE = {}


def kernel(**inputs):
    B, S, _ = inputs["x"].shape
    ncores = 8
    NSEQ = B // ncores
    key = (S, NSEQ)
    if key not in _CACHE:
        _CACHE[key] = build(S, NSEQ)[0]
    nc = _CACHE[key]
    maps = host_layout(inputs, S, NSEQ, ncores)
    res = run_bass_kernel_spmd(nc, maps, core_ids=list(range(ncores)))
    out = np.stack([r["out"].reshape(NSEQ, S, D) for r in res.results], axis=0).reshape(B, S, D)
    return out.astype(np.float32)
```

```python
import math
import numpy as np
from contextlib import ExitStack
import concourse.bass as bass
import concourse.mybir as mybir
from concourse.bass_utils import run_bass_kernel_spmd

F32 = mybir.dt.float32
BF16 = mybir.dt.bfloat16
U32 = mybir.dt.uint32
I32 = mybir.dt.int32
ALU = mybir.AluOpType
AF = mybir.ActivationFunctionType
AX = mybir.AxisListType

ENGS = ("pe", "act", "dve", "pool", "sp")
CHUNK = 16000
EPS = 1e-6
D = 1024
NEXP = 16384


class Tok:
    __slots__ = ("name", "w", "r", "dsem", "dcount", "excl")

    def __init__(self, name):
        self.name = name
        self.excl = False
        self.w = None
        self.r = {}
        self.dsem = None
        self.dcount = 0


class Prog:
    def __init__(self, nc):
        self.nc = nc
        self.es = ExitStack()
        self.ops = {e: [] for e in ENGS}
        self.cnt = {e: 0 for e in ENGS}
        self.esems = {e: [] for e in ENGS}
        self.seen = {e: {} for e in ENGS}
        self.semobj = {}
        self.nsem = 0
        self.final_events = []
        self.dma_toks = []
        self.nins = 0

    def sbuf(self, name, shape, dt):
        return self.es.enter_context(self.nc.sbuf_tensor(name, list(shape), dt))

    def psum(self, name, shape, dt):
        return self.es.enter_context(self.nc.psum_tensor(name, list(shape), dt))

    def newsem(self, name):
        self.nsem += 1
        return self.es.enter_context(self.nc.semaphore(name))

    def tok(self, name):
        return Tok(name)

    def toks(self, name, n):
        return [Tok(f"{name}{i}") for i in range(n)]

    def _eng_event(self, eng):
        c = self.cnt[eng]
        ch, v = divmod(c, CHUNK)
        while len(self.esems[eng]) <= ch:
            s = self.newsem(f"e_{eng}_{len(self.esems[eng])}")
            self.esems[eng].append(s)
            self.semobj[(eng, len(self.esems[eng]) - 1)] = s
        self.cnt[eng] = c + 1
        return ((eng, ch), v + 1)

    def _collect(self, eng, reads, writes, same_eng_sync):
        deps = {}

        def add(ev):
            if ev is None:
                return
            k, v = ev
            if deps.get(k, 0) < v:
                deps[k] = v

        for b in reads:
            add(b.w)
        for b in writes:
            add(b.w)
            for k, v in b.r.items():
                add((k, v))
        waits = []
        seen = self.seen[eng]
        for k, v in deps.items():
            if (not same_eng_sync) and k[0] == eng:
                continue
            if seen.get(k, 0) >= v:
                continue
            seen[k] = v
            waits.append((k, v))
        return waits

    def _record(self, ev, reads, writes):
        k, v = ev
        for b in reads:
            if b.r.get(k, 0) < v:
                b.r[k] = v
        for b in writes:
            b.w = ev
            b.r = {}

    def op(self, eng, fn, reads=(), writes=(), sync_same=None):
        if sync_same is None:
            sync_same = eng != "pe"
        ex = [b for b in reads if b.excl]
        if ex:
            writes = list(writes) + ex
        waits = self._collect(eng, reads, writes, sync_same)
        ev = self._eng_event(eng)
        self._record(ev, reads, writes)
        self.ops[eng].append((waits, fn, ev, 1))
        self.nins += 1

    def dma(self, eng, fn, owner, reads=(), writes=(), final=False):
        waits = self._collect(eng, reads, writes, False)
        if owner.dsem is None:
            owner.dsem = ("dma", self.nsem)
            self.semobj[owner.dsem] = self.newsem(f"d{self.nsem}")
            self.dma_toks.append(owner)
        owner.dcount += 16
        ev = (owner.dsem, owner.dcount)
        self._record(ev, reads, writes)
        self.ops[eng].append((waits, fn, ev, 16))
        if final:
            self.final_events.append(ev)
        self.nins += 1

    def barrier(self):
        evs = {}
        for e in ENGS:
            c = self.cnt[e]
            if c == 0:
                continue
            ch, v = divmod(c - 1, CHUNK)
            evs[(e, ch)] = v + 1
        for t in self.dma_toks:
            evs[t.dsem] = t.dcount
        for e in ENGS:
            waits = []
            for k, v in evs.items():
                if k[0] == e:
                    continue
                if self.seen[e].get(k, 0) >= v:
                    continue
                self.seen[e][k] = v
                waits.append((k, v))
            self.ops[e].append((waits, None, None, 0))

    def emit(self):
        nc = self.nc
        fw = {}
        for k, v in self.final_events:
            fw[k] = max(fw.get(k, 0), v)
        self.ops["sp"].append((list(fw.items()), None, None, 0))
        semobj = self.semobj

        def run(engobj, lst):
            for waits, fn, ev, inc in lst:
                for k, v in waits:
                    engobj.wait_ge(semobj[k], v)
                if fn is None:
                    continue
                ins = fn(engobj)
                ins.then_inc(semobj[ev[0]], inc)

        with nc.Block() as block:
            @block.tensor
            def _(e):
                run(e, self.ops["pe"])

            @block.scalar
            def _(e):
                run(e, self.ops["act"])

            @block.vector
            def _(e):
                run(e, self.ops["dve"])

            @block.gpsimd
            def _(e):
                run(e, self.ops["pool"])

            @block.sync
            def _(e):
                run(e, self.ops["sp"])
        self.es.close()

    def mm(self, out, lhsT, rhs, start, stop, reads, writes):
        self.op("pe", lambda e: e.matmul(out, lhsT=lhsT, rhs=rhs, start=start, stop=stop), reads, writes)

    def tr(self, out, in_, ident, reads, writes):
        self.op("pe", lambda e: e.transpose(out=out, in_=in_, identity=ident), reads, writes)

    def act(self, out, in_, func, reads, writes, bias=None, scale=None, accum=None):
        kw = {}
        if bias is not None:
            kw["bias"] = bias
        if scale is not None:
            kw["scale"] = scale
        if accum is not None:
            kw["accum_out"] = accum
        self.op("act", lambda e: e.activation(out=out, in_=in_, func=func, **kw), reads, writes)

    def tt(self, eng, out, in0, in1, op, reads, writes):
        self.op(eng, lambda e: e.tensor_tensor(out=out, in0=in0, in1=in1, op=op), reads, writes)

    def ts(self, eng, out, in0, s1, op0, reads, writes, s2=None, op1=None):
        if op1 is None:
            self.op(eng, lambda e: e.tensor_scalar(out=out, in0=in0, scalar1=s1, scalar2=None, op0=op0), reads, writes)
        else:
            self.op(eng, lambda e: e.tensor_scalar(out=out, in0=in0, scalar1=s1, scalar2=s2, op0=op0, op1=op1), reads, writes)

    def stt(self, out, in0, scalar, in1, op0, op1, reads, writes, accum=None):
        if accum is None:
            self.op("dve", lambda e: e.scalar_tensor_tensor(out=out, in0=in0, scalar=scalar, in1=in1, op0=op0, op1=op1), reads, writes)
        else:
            self.op("dve", lambda e: e.scalar_tensor_tensor(out=out, in0=in0, scalar=scalar, in1=in1, op0=op0, op1=op1, accum_out=accum), reads, writes)

    def red(self, out, in_, reads, writes, op=ALU.add):
        self.op("dve", lambda e: e.tensor_reduce(out=out, in_=in_, axis=AX.X, op=op), reads, writes)

    def cp(self, eng, out, in_, reads, writes):
        if eng == "act":
            self.op("act", lambda e: e.copy(out=out, in_=in_), reads, writes)
        else:
            self.op(eng, lambda e: e.tensor_copy(out=out, in_=in_), reads, writes)

    def recip(self, out, in_, reads, writes):
        self.op("dve", lambda e: e.reciprocal(out=out, in_=in_), reads, writes)

    def memset(self, eng, ap, val, writes):
        self.op(eng, lambda e: e.memset(ap, val), (), writes)

    def ld(self, eng, out, in_, owner, reads=(), writes=None, final=False):
        if writes is None:
            writes = [owner]
        self.dma(eng, lambda e: e.dma_start(out=out, in_=in_), owner, reads, writes, final)


class Arena:
    def __init__(self, ap, words):
        self.ap = ap
        self.words = words
        self.off = 0

    def f32(self, n):
        n = (n + 1) // 2 * 2
        a = self.ap[:, self.off:self.off + n]
        self.off += n
        self.hw = max(getattr(self, "hw", 0), self.off)
        assert self.off <= self.words, f"arena overflow {self.off} > {self.words}"
        return a

    def bf16(self, n):
        w = (n + 1) // 2
        return self.f32(w).bitcast(BF16)[:, 0:n]

    def u32(self, n):
        return self.f32(n).bitcast(U32)


G_N1 = 0
G_QLAT = G_N1 + 1024
G_KVLAT = G_QLAT + 384
G_KPE = G_KVLAT + 256
G_DQK = G_KPE + 64
G_Q12 = G_DQK + 1024
G_K4 = G_Q12 + 768
G_SUB = G_K4 + 512
G_N2 = G_SUB + 128
G_LQ = G_N2 + 1024
G_TOT = G_LQ + 256

C_ID = 0
C_TRI = 128
C_IOTA = 256
C_ONES = 272
C_THR = 400
C_TOT = 416

LAMBDA_INIT = 0.8 - 0.6 * math.exp(-0.3 * 0)


def build(S, NSEQ, do_a1=True, do_a2=True, do_b=True, stage_lim=9):
    NT = S // 128
    NCH = S // 512
    TT = NSEQ * S
    nc = bass.Bass("TRN2", target_bir_lowering=False)
    P = Prog(nc)

    def din(name, shape, dt=F32):
        return nc.dram_tensor(name, list(shape), dt, kind="ExternalInput").ap()

    x_d = din("x", [TT, D])
    cT_d = din("cT", [128, NSEQ, 8])
    adaw_d = din("ada_w", [D, 6 * D])
    adab_d = din("ada_b", [1, 6 * D])
    win_d = din("w_in", [D, 2240])
    wq_d = din("w_q_up", [384, 768])
    wkv_d = din("w_kv_up", [256, 1024])
    wout_d = din("w_out", [D, D])
    pwq_d = din("peer_w_q", [D, 2048])
    keys_d = din("keysT", [128, 16, 128])
    u_d = din("peer_u", [NEXP, D])
    v_d = din("peer_v", [NEXP, D])
    gains_d = din("gains", [128, G_TOT])
    rope_d = din("rope", [128, NT, 64])
    consts_d = din("consts", [128, C_TOT])
    out_d = nc.dram_tensor("out", [TT, D], F32, kind="ExternalOutput").ap()
    x1_d = nc.dram_tensor("x1s", [TT, D], F32, kind="Internal").ap()
    tx1 = [P.tok(f"x1d{i}") for i in range(TT // 128)]

    AW = 53200
    arena_t = P.sbuf("arena", [128, AW], F32)
    A = Arena(arena_t, AW)
    banks = [P.psum(f"pb{i}", [128, 512], F32) for i in range(8)]
    tb = P.toks("bank", 8)
    for t_ in tb:
        t_.excl = True

    def bankbf(i):
        return banks[i][:, :].bitcast(BF16)

    consts = A.f32(C_TOT)
    t_consts = P.tok("consts")
    P.ld("sp", consts, consts_d[:, :], t_consts)
    ident_f = consts[:, C_ID:C_ID + 128]
    iota16 = consts[:, C_IOTA:C_IOTA + 16]
    ones_f = consts[:, C_ONES:C_ONES + 128]
    thr16 = consts[:, C_THR:C_THR + 16]
    identb = A.bf16(128)
    trib = A.bf16(128)
    t_cb = P.tok("constsb")
    P.cp("dve", identb, ident_f, [t_consts], [t_cb])
    P.cp("dve", trib, consts[:, C_TRI:C_TRI + 128], [t_consts], [t_cb])
    cT = A.f32(NSEQ * 8)
    t_cT = P.tok("cT")
    P.ld("sp", cT, cT_d.rearrange("p b k -> p (b k)"), t_cT)
    adab = [A.f32(512), A.f32(512)]
    t_adab = P.toks("adab", 2)
    base_off = A.off

    def small(n):
        return A.f32(n)

    def load_cast_weight(dst_bf, src_dram_rows, ncols, stage, t_stage, t_dst, k, eng_i):
        st = stage[eng_i % len(stage)]
        ts_ = t_stage[eng_i % len(stage)]
        P.ld("sp", st[:, 0:ncols], src_dram_rows, ts_)
        eng = ("act", "dve", "pool")[eng_i % 3]
        P.cp(eng, dst_bf, st[:, 0:ncols], [ts_], [t_dst])

    def compute_mod(b, chunks, dests, t_dests, wst, t_wst, mbank, bufs=None):
        if bufs is None:
            bufs = (small(8), A.f32(8 * 128), P.tok("sil"), P.tok("crep"))
        sil, crep, t_sil, t_crep = bufs
        P.act(sil, cT[:, b * 8:(b + 1) * 8], AF.Silu, [t_cT], [t_sil])
        P.cp("dve", crep.rearrange("p (k m) -> p k m", k=8), sil.unsqueeze(2).to_broadcast([128, 8, 128]), [t_sil], [t_crep])
        adaw_v = adaw_d.rearrange("(k p) n -> p k n", p=128)
        for i, ch in enumerate(chunks):
            w = wst[i % 2]
            tw = t_wst[i % 2]
            P.ld("sp", w.rearrange("p (k n) -> p k n", k=8), adaw_v[:, :, ch * 512:(ch + 1) * 512], tw)
            bk = mbank[i % 2]
            P.ld("sp", adab[i % 2][0:1, :], adab_d[:, ch * 512:(ch + 1) * 512], t_adab[i % 2])
            for k in range(8):
                P.mm(banks[bk][:, :], crep[:, k * 128:(k + 1) * 128], w[:, k * 512:(k + 1) * 512], k == 0, False, [t_crep, tw], [tb[bk]])
            P.mm(banks[bk][:, :], ones_f[0:1, :], adab[i % 2][0:1, :], False, True, [t_consts, t_adab[i % 2]], [tb[bk]])
            P.cp("act", dests[i], banks[bk][:, :], [tb[bk]], [t_dests[i]])

    def rope(src, dst, G, cs, t_src, t_dst, t_cs, tmp, t_tmp):
        cosb = cs[:, 0:32].unsqueeze(1).to_broadcast([128, G, 32])
        sinb = cs[:, 32:64].unsqueeze(1).to_broadcast([128, G, 32])
        x1 = src[:, :, 0:32]
        x2 = src[:, :, 32:64]
        t1 = tmp[:, 0:G * 32].rearrange("p (g d) -> p g d", g=G)
        t2 = tmp[:, G * 32:2 * G * 32].rearrange("p (g d) -> p g d", g=G)
        t3 = tmp[:, 2 * G * 32:3 * G * 32].rearrange("p (g d) -> p g d", g=G)
        t4 = tmp[:, 3 * G * 32:4 * G * 32].rearrange("p (g d) -> p g d", g=G)
        P.tt("dve", t1, x1, cosb, ALU.mult, [t_src, t_cs], [t_tmp[0]])
        P.tt("dve", t2, x2, sinb, ALU.mult, [t_src, t_cs], [t_tmp[1]])
        P.tt("dve", dst[:, :, 0:32], t1, t2, ALU.subtract, [t_tmp[0], t_tmp[1]], [t_dst])
        P.tt("pool", t3, x2, cosb, ALU.mult, [t_src, t_cs], [t_tmp[2]])
        P.tt("pool", t4, x1, sinb, ALU.mult, [t_src, t_cs], [t_tmp[3]])
        P.tt("pool", dst[:, :, 32:64], t3, t4, ALU.add, [t_tmp[2], t_tmp[3]], [t_dst])

    def rstd_from_ss(ss, n, t_ss, scale):
        P.act(ss, ss, AF.Ln, [t_ss], [t_ss], bias=EPS, scale=scale)
        P.act(ss, ss, AF.Exp, [t_ss], [t_ss], scale=-0.5)

    uv_d = nc.dram_tensor("uvtab", [NEXP, 2048], BF16, kind="Internal").ap()
    t_uv = P.tok("uvtab")
    NCAST = 128
    RPC = NEXP // (NCAST // 2)
    cast_state = {"i": 0}

    def emit_cast(n=1):
        for _ in range(n):
            i = cast_state["i"]
            if i >= NCAST or not do_b:
                return
            cast_state["i"] = i + 1
            q, r = i % 2, i // 2
            src_ = (u_d if q == 0 else v_d)[r * RPC:(r + 1) * RPC, :]
            dst_ = uv_d[r * RPC:(r + 1) * RPC, q * 1024:(q + 1) * 1024]
            P.dma("pool", (lambda s_, d_: (lambda e: e.dma_start(out=d_, in_=s_)))(src_, dst_), t_uv, writes=[t_uv])

    if do_a1 or do_a2:
        A.off = base_off
        ropes = A.f32(NT * 64)
        t_rope = P.tok("rope")
        P.ld("sp", ropes.rearrange("p (t d) -> p t d", t=NT), rope_d[:, :, :], t_rope)
        lam2 = small(2)
        neglam = small(2)
        t_lam = P.tok("lam")
        gsub = A.f32(128)
        G1 = A.f32(1024)
        SH1 = A.f32(1024)
        GATE1 = A.f32(1024)
        t_mod = P.tok("mod1")
        kv_off = A.off
        lq = A.f32(256)
        t_lq = P.tok("lq")
        P.ld("sp", lq, gains_d[:, G_LQ:G_LQ + 256], t_lq)
        prod = A.f32(128)
        lq4 = lq.rearrange("p (a b d) -> p a b d", a=2, b=2)
        P.tt("dve", prod.rearrange("p (a d) -> p a d", a=2), lq4[:, :, 0, :], lq4[:, :, 1, :], ALU.mult, [t_lq], [t_lam])
        P.red(lam2[:, 0:2], prod.rearrange("p (a d) -> p a d", a=2), [t_lam], [t_lam])
        P.act(lam2[:, 0:2], lam2[:, 0:2], AF.Exp, [t_lam], [t_lam])
        P.tt("dve", neglam[:, 0:1], lam2[:, 1:2], lam2[:, 0:1], ALU.subtract, [t_lam], [t_lam])
        P.ts("dve", neglam[:, 0:1], neglam[:, 0:1], -LAMBDA_INIT, ALU.add, [t_lam], [t_lam])
        gs_t = A.f32(128)
        t_gs = P.tok("gs_t")
        P.ld("sp", gs_t, gains_d[:, G_SUB:G_SUB + 128], t_gs)
        P.ts("dve", gsub, gs_t, 1.0 - LAMBDA_INIT, ALU.mult, [t_gs], [t_lam])

        for s in range(NSEQ):
            A.off = kv_off
            P.barrier()
            wst = [A.f32(4096), A.f32(4096)]
            t_wst = P.toks("wst", 2)
            mtmp = [A.f32(512), A.f32(512)]
            t_mtmp = P.toks("mtmp", 2)
            gn1 = A.f32(1024)
            t_gn1 = P.tok("gn1")
            P.ld("sp", gn1, gains_d[:, G_N1:G_N1 + 1024], t_gn1)
            dests = [SH1[:, 0:512], SH1[:, 512:1024], mtmp[0], mtmp[1], GATE1[:, 0:512], GATE1[:, 512:1024]]
            t_dests = [t_mod, t_mod, t_mtmp[0], t_mtmp[1], t_mod, t_mod]
            compute_mod(s, [0, 1, 2, 3, 4, 5], dests, t_dests, wst, t_wst, [0, 1])
            for i in range(2):
                P.stt(G1[:, i * 512:(i + 1) * 512], mtmp[i], 1.0, gn1[:, i * 512:(i + 1) * 512], ALU.add, ALU.mult, [t_mtmp[i], t_gn1], [t_mod])

            for phase in ("a1", "a2"):
                if phase == "a1" and not do_a1:
                    continue
                if phase == "a2" and not do_a2:
                    continue
                A.off = kv_off
                P.barrier()
                a1 = phase == "a1"
                t_gA = P.tok("gA")
                t_w = P.tok("wA")
                gseg = {}

                def gload(name, col, n):
                    buf = A.f32(n)
                    P.ld("sp", buf, gains_d[:, col:col + n], t_gA)
                    gseg[name] = buf

                if a1:
                    gload("qlat", G_QLAT, 384); gload("kvlat", G_KVLAT, 256); gload("kpe", G_KPE, 64)
                    gload("q12", G_Q12, 768); gload("k4", G_K4, 512)
                    WCOL0, WN = 0, 704
                else:
                    gload("dqk", G_DQK, 1024)
                    WCOL0, WN = 704, 1536
                win_b = A.bf16(8 * WN)
                wout_b = A.bf16(4 * 1024)
                wofs = 0 if a1 else 512
                if a1:
                    wq_b = A.bf16(3 * 768)
                    wkv_b = A.bf16(2 * 1024)
                mark = A.off
                stage = [A.f32(1536), A.f32(1536)]
                t_stage = P.toks("stage", 2)
                ei = 0
                for k in range(8):
                    load_cast_weight(win_b[:, k * WN:(k + 1) * WN], win_d[k * 128:(k + 1) * 128, WCOL0:WCOL0 + WN], WN, stage, t_stage, t_w, k, ei); ei += 1
                for k in range(4):
                    load_cast_weight(wout_b[:, k * 1024:(k + 1) * 1024], wout_d[wofs + k * 128:wofs + (k + 1) * 128, :], 1024, stage, t_stage, t_w, k, ei); ei += 1
                if a1:
                    for k in range(3):
                        load_cast_weight(wq_b[:, k * 768:(k + 1) * 768], wq_d[k * 128:(k + 1) * 128, :], 768, stage, t_stage, t_w, k, ei); ei += 1
                    for k in range(2):
                        load_cast_weight(wkv_b[:, k * 1024:(k + 1) * 1024], wkv_d[k * 128:(k + 1) * 128, :], 1024, stage, t_stage, t_w, k, ei); ei += 1
                P.barrier()
                A.off = mark
                if a1:
                    kTn = A.bf16(4 * S)
                    kTp = A.bf16(S)
                    Vv = A.bf16(NT * 4 * 130)
                else:
                    kTn = A.bf16(4 * S)
                    kTp = None
                    Vv = A.bf16(NT * 4 * 130)
                kTn_v = kTn.rearrange("p (h s) -> p h s", h=4)
                Vv_v = Vv.rearrange("p (t h d) -> p t h d", t=NT, h=4)
                t_kv = P.toks("kv", NT)
                t_vinit = P.tok("vinit")
                P.memset("pool", Vv, 1.0, [t_vinit] + t_kv)
                xt = [A.f32(1024), A.f32(1024)]
                t_xt = P.toks("xt", 2)
                junk = A.bf16(1024)
                t_junk = P.tok("junk")
                sm = [A.f32(64), A.f32(64)]
                t_sm = P.toks("sm", 2)
                tmpf = A.f32(1024)
                t_tmpf = P.tok("tmpf")
                hb = A.bf16(1024)
                t_hb = P.tok("hb")
                hT = A.bf16(1024)
                t_hT = P.tok("hT")
                NPJ = 704 if a1 else 1536
                proj = A.f32(NPJ)
                t_proj = P.tok("proj")
                sq = A.f32(768 if a1 else 1024)
                t_sq = P.tok("sq")
                nrm = A.f32(64 if a1 else 1024)
                t_nrm = P.tok("nrm")
                rtmp = A.f32(4 * (4 if a1 else 16) * 32)
                t_rtmp = P.toks("rtmp", 4)
                rot = A.bf16(64 if a1 else 1024)
                t_rot = P.tok("rot")
                if a1:
                    qn = A.bf16(640)
                    t_qn = P.tok("qn")
                    qnT = A.bf16(640)
                    t_qnT = P.tok("qnT")
                    qf = A.f32(768)
                    t_qf = P.tok("qf")
                    qpe = A.f32(256)
                    t_qpe = P.tok("qpe")
                    qm = A.bf16(768)
                    t_qm = P.tok("qm")
                    kf = A.f32(512)
                    t_kf = P.tok("kf")
                    kn = A.bf16(512)
                    t_kn = P.tok("kn")
                    qTn = [A.bf16(4 * 512), A.bf16(4 * 512)]
                    qTp = [A.bf16(4 * 512), A.bf16(4 * 512)]
                    qTp_v = [q_.rearrange("p (h s) -> p h s", h=4) for q_ in qTp]
                else:
                    qTn = [A.bf16(4 * 512), A.bf16(4 * 512)]
                qTn_v = [q_.rearrange("p (h s) -> p h s", h=4) for q_ in qTn]
                t_qT = P.toks("qT", 2)
                NPT = 4
                PT = [A.bf16(512) for _ in range(NPT)]
                t_PT = P.toks("PT", NPT)
                mixed = [A.bf16(4 * 512), A.bf16(4 * 512)]
                mixed_v = [m_.rearrange("p (q c) -> p q c", q=4) for m_ in mixed]
                t_mixed = [P.toks("mixedA", 4), P.toks("mixedB", 4)]
                mT = A.bf16(512)
                t_mT = P.tok("mT")
                xr = xt
                t_xr = t_xt
                ytmp = A.f32(1024)
                t_ytmp = P.tok("ytmp")
                osm = A.f32(64)
                t_osm = P.tok("osm")
                oa = A.f32(128)
                t_oa = P.tok("oa")
                od = A.f32(128)
                t_od = P.tok("od")
                cntA = {"ti": 0, "pti": 0, "xi": 0}
                MB = (4, 5, 7) if a1 else (7, 7, 7)
                OB = (4, 5) if a1 else (7, 7)

                def tile_proc(c):
                    cp_ = c % 2
                    for j in range(4):
                        tl = c * 4 + j
                        row0 = s * S + tl * 128
                        ti = cntA["ti"]; cntA["ti"] += 1
                        xb = xt[ti % 2]; txb = t_xt[ti % 2]
                        smb = sm[ti % 2]; tsm = t_sm[ti % 2]
                        emit_cast(2 if NT * NSEQ * 2 < NCAST else 1)
                        P.ld("sp", xb, x_d[row0:row0 + 128, :], txb)
                        P.act(junk, xb, AF.Square, [txb], [t_junk, tsm], accum=smb[:, 0:1])
                        rstd_from_ss(smb[:, 0:1], 1, tsm, 1.0 / 1024)
                        yield
                        P.stt(tmpf, xb, smb[:, 0:1], G1, ALU.mult, ALU.mult, [txb, tsm, t_mod], [t_tmpf])
                        P.tt("pool", hb, tmpf, SH1, ALU.add, [t_tmpf, t_mod], [t_hb])
                        yield
                        Tb = bankbf(6)
                        for k in range(8):
                            P.tr(Tb[:, k * 128:(k + 1) * 128], hb[:, k * 128:(k + 1) * 128], identb, [t_hb, t_cb], [tb[6]])
                        yield
                        P.cp("act", hT, Tb, [tb[6]], [t_hT])
                        yield
                        if a1:
                            chunks = [(0, 512), (512, 704)]
                        else:
                            chunks = [(0, 512), (512, 1024), (1024, 1536)]
                        for ci, (c0, c1) in enumerate(chunks):
                            bk = MB[ci]
                            for k in range(8):
                                P.mm(banks[bk][:, 0:c1 - c0], hT[:, k * 128:(k + 1) * 128],
                                     win_b[:, k * WN + c0:k * WN + c1], k == 0, k == 7, [t_hT, t_w], [tb[bk]])
                            yield
                            P.cp("act" if ci % 2 == 0 else "dve", proj[:, c0:c1], banks[bk][:, 0:c1 - c0], [tb[bk]], [t_proj])
                            yield
                        cs = ropes[:, tl * 64:(tl + 1) * 64]
                        if a1:
                            P.act(sq[:, 0:704], proj[:, 0:704], AF.Square, [t_proj], [t_sq])
                            ss11 = smb[:, 4:15]
                            P.red(ss11, sq[:, 0:704].rearrange("p (g d) -> p g d", d=64), [t_sq], [tsm])
                            yield
                            ss3 = smb[:, 16:19]
                            P.red(ss3[:, 0:1], ss11[:, 0:6], [tsm], [tsm])
                            P.red(ss3[:, 1:2], ss11[:, 6:10], [tsm], [tsm])
                            P.ts("dve", ss3[:, 0:1], ss3[:, 0:1], 1.0 / 384, ALU.mult, [tsm], [tsm])
                            P.ts("dve", ss3[:, 1:2], ss3[:, 1:2], 1.0 / 256, ALU.mult, [tsm], [tsm])
                            P.ts("dve", ss3[:, 2:3], ss11[:, 10:11], 1.0 / 64, ALU.mult, [tsm], [tsm])
                            yield
                            rstd_from_ss(ss3, 3, tsm, 1.0)
                            yield
                            P.stt(qn[:, 0:384], proj[:, 0:384], ss3[:, 0:1], gseg["qlat"], ALU.mult, ALU.mult, [t_proj, tsm, t_gA], [t_qn])
                            P.stt(qn[:, 384:640], proj[:, 384:640], ss3[:, 1:2], gseg["kvlat"], ALU.mult, ALU.mult, [t_proj, tsm, t_gA], [t_qn])
                            P.stt(nrm[:, 0:64], proj[:, 640:704], ss3[:, 2:3], gseg["kpe"], ALU.mult, ALU.mult, [t_proj, tsm, t_gA], [t_nrm])
                            yield
                            rope(nrm[:, 0:64].rearrange("p (g d) -> p g d", g=1), rot[:, 0:64].rearrange("p (g d) -> p g d", g=1), 1, cs,
                                 t_nrm, t_rot, t_rope, rtmp, t_rtmp)
                            Tb = bankbf(6)
                            for k in range(5):
                                P.tr(Tb[:, k * 128:(k + 1) * 128], qn[:, k * 128:(k + 1) * 128], identb, [t_qn, t_cb], [tb[6]])
                            yield
                            P.cp("act", qnT, Tb[:, 0:640], [tb[6]], [t_qnT])
                            yield
                            QB = (7, 4)
                            KB = (5, 7)
                            for cc in range(2):
                                for k in range(3):
                                    P.mm(banks[QB[cc]][:, 0:384], qnT[:, k * 128:(k + 1) * 128], wq_b[:, k * 768 + cc * 384:k * 768 + (cc + 1) * 384],
                                         k == 0, k == 2, [t_qnT, t_w], [tb[QB[cc]]])
                            yield
                            for cc in range(2):
                                P.act(sq[:, cc * 384:(cc + 1) * 384], banks[QB[cc]][:, 0:384], AF.Square, [tb[QB[cc]]], [t_sq])
                            yield
                            ss12 = smb[:, 20:32]
                            P.red(ss12, sq[:, 0:768].rearrange("p (g d) -> p g d", d=64), [t_sq], [tsm])
                            ss12v = ss12.rearrange("p (h t) -> p h t", t=3)
                            r12 = smb[:, 32:44]
                            r12v = r12.rearrange("p (h t) -> p h t", t=3)
                            P.tt("dve", r12v[:, :, 0], ss12v[:, :, 0], ss12v[:, :, 1], ALU.add, [tsm], [tsm])
                            yield
                            P.ts("dve", r12v[:, :, 0], r12v[:, :, 0], 1.0 / 128, ALU.mult, [tsm], [tsm])
                            P.ts("dve", r12v[:, :, 2], ss12v[:, :, 2], 1.0 / 64, ALU.mult, [tsm], [tsm])
                            yield
                            P.cp("dve", r12v[:, :, 1], r12v[:, :, 0], [tsm], [tsm])
                            yield
                            rstd_from_ss(r12, 12, tsm, 1.0)
                            yield
                            for cc in range(2):
                                P.tt("dve", qf[:, cc * 384:(cc + 1) * 384].rearrange("p (g d) -> p g d", d=64),
                                     banks[QB[cc]][:, 0:384].rearrange("p (g d) -> p g d", d=64),
                                     r12[:, cc * 6:(cc + 1) * 6].unsqueeze(2).to_broadcast([128, 6, 64]), ALU.mult, [tb[QB[cc]], tsm], [t_qf])
                            yield
                            for cc in range(2):
                                for k in range(2):
                                    P.mm(banks[KB[cc]][:, :], qnT[:, (3 + k) * 128:(4 + k) * 128], wkv_b[:, k * 1024 + cc * 512:k * 1024 + (cc + 1) * 512],
                                         k == 0, k == 1, [t_qnT, t_w], [tb[KB[cc]]])
                            qfv = qf.rearrange("p (h d) -> p h d", h=4)
                            gq = gseg["q12"].rearrange("p (h d) -> p h d", h=4)
                            qmv = qm.rearrange("p (h d) -> p h d", h=4)
                            P.tt("dve", qmv[:, :, 0:128], qfv[:, :, 0:128], gq[:, :, 0:128], ALU.mult, [t_qf, t_gA], [t_qm])
                            qpev = qpe.rearrange("p (h d) -> p h d", h=4)
                            P.tt("pool", qpev, qfv[:, :, 128:192], gq[:, :, 128:192], ALU.mult, [t_qf, t_gA], [t_qpe])
                            yield
                            rope(qpev, qmv[:, :, 128:192], 4, cs, t_qpe, t_qm, t_rope, rtmp, t_rtmp)
                            yield
                            for cc in range(2):
                                kvv = banks[KB[cc]][:, :].rearrange("p (h d) -> p h d", h=2)
                                P.act(sq[:, cc * 256:(cc + 1) * 256].rearrange("p (h d) -> p h d", h=2), kvv[:, :, 0:128], AF.Square, [tb[KB[cc]]], [t_sq])
                                P.cp("act", Vv_v[:, tl, 2 * cc:2 * cc + 2, 0:128], kvv[:, :, 128:256], [tb[KB[cc]], t_vinit], [t_kv[tl]])
                            yield
                            ssk = smb[:, 44:48]
                            P.red(ssk, sq[:, 0:512].rearrange("p (h d) -> p h d", h=4), [t_sq], [tsm])
                            yield
                            rstd_from_ss(ssk, 4, tsm, 1.0 / 128)
                            yield
                            for cc in range(2):
                                kvv = banks[KB[cc]][:, :].rearrange("p (h d) -> p h d", h=2)
                                P.tt("dve", kf[:, cc * 256:(cc + 1) * 256].rearrange("p (h d) -> p h d", h=2), kvv[:, :, 0:128],
                                     ssk[:, 2 * cc:2 * cc + 2].unsqueeze(2).to_broadcast([128, 2, 128]), ALU.mult, [tb[KB[cc]], tsm], [t_kf])
                            yield
                            P.tt("pool", kn, kf, gseg["k4"], ALU.mult, [t_kf, t_gA], [t_kn])
                            yield
                            Tb = bankbf(6)
                            for h in range(4):
                                P.tr(Tb[:, h * 128:(h + 1) * 128], qmv[:, h, 0:128], identb, [t_qm, t_cb], [tb[6]])
                                P.tr(Tb[:, 512 + h * 128:512 + (h + 1) * 128], kn[:, h * 128:(h + 1) * 128], identb, [t_kn, t_cb], [tb[6]])
                            yield
                            P.cp("act", qTn_v[cp_][:, :, j * 128:(j + 1) * 128], Tb[:, 0:512].rearrange("p (h t) -> p h t", h=4), [tb[6]], [t_qT[cp_]])
                            P.cp("dve", kTn_v[:, :, tl * 128:(tl + 1) * 128], Tb[:, 512:1024].rearrange("p (h t) -> p h t", h=4), [tb[6]], [t_kv[tl]])
                            yield
                            Tb = bankbf(6)
                            for h in range(4):
                                P.tr(Tb[0:64, h * 128:(h + 1) * 128], qmv[:, h, 128:192], identb, [t_qm, t_cb], [tb[6]])
                            P.tr(Tb[0:64, 512:640], rot[:, 0:64], identb, [t_rot, t_cb], [tb[6]])
                            yield
                            P.cp("act", qTp_v[cp_][0:64, :, j * 128:(j + 1) * 128], Tb[0:64, 0:512].rearrange("p (h t) -> p h t", h=4), [tb[6]], [t_qT[cp_]])
                            P.cp("dve", kTp[0:64, tl * 128:(tl + 1) * 128], Tb[0:64, 512:640], [tb[6]], [t_kv[tl]])
                            yield
                        else:
                            P.act(sq[:, 0:1024], proj[:, 0:1024], AF.Square, [t_proj], [t_sq])
                            ss16 = smb[:, 4:20]
                            P.red(ss16, sq[:, 0:1024].rearrange("p (g d) -> p g d", d=64), [t_sq], [tsm])
                            yield
                            rstd_from_ss(ss16, 16, tsm, 1.0 / 64)
                            yield
                            P.tt("dve", tmpf.rearrange("p (g d) -> p g d", d=64), proj[:, 0:1024].rearrange("p (g d) -> p g d", d=64),
                                 ss16.unsqueeze(2).to_broadcast([128, 16, 64]), ALU.mult, [t_proj, tsm], [t_tmpf])
                            yield
                            P.tt("pool", nrm, tmpf, gseg["dqk"], ALU.mult, [t_tmpf, t_gA], [t_nrm])
                            yield
                            rope(nrm.rearrange("p (g d) -> p g d", d=64), rot.rearrange("p (g d) -> p g d", d=64), 16, cs,
                                 t_nrm, t_rot, t_rope, rtmp, t_rtmp)
                            P.cp("act", Vv_v[:, tl, :, 0:128], proj[:, 1024:1536].rearrange("p (h d) -> p h d", h=4), [t_proj, t_vinit], [t_kv[tl]])
                            yield
                            Tb = bankbf(6)
                            for h in range(8):
                                P.tr(Tb[:, h * 128:(h + 1) * 128], rot[:, h * 128:(h + 1) * 128], identb, [t_rot, t_cb], [tb[6]])
                            yield
                            P.cp("act", qTn_v[cp_][:, :, j * 128:(j + 1) * 128], Tb[:, 0:512].rearrange("p (h t) -> p h t", h=4), [tb[6]], [t_qT[cp_]])
                            P.cp("dve", kTn_v[:, :, tl * 128:(tl + 1) * 128], Tb[:, 512:1024].rearrange("p (h t) -> p h t", h=4), [tb[6]], [t_kv[tl]])
                            yield

                def emit_ST(c, h, m, kb):
                    cp_ = c % 2
                    jd = kb - 4 * c
                    q0 = 0 if jd < 0 else jd * 128
                    sb = kb % 2
                    if a1:
                        P.mm(banks[sb][:, q0:512], kTn_v[:, h, kb * 128:(kb + 1) * 128], qTn_v[cp_][:, h, q0:512], True, False,
                             [t_kv[kb], t_qT[cp_]], [tb[sb]])
                        P.mm(banks[sb][:, q0:512], kTp[0:64, kb * 128:(kb + 1) * 128], qTp_v[cp_][0:64, h, q0:512], False, True,
                             [t_kv[kb], t_qT[cp_]], [tb[sb]])
                        scale = 192 ** -0.5
                    else:
                        P.mm(banks[sb][:, q0:512], kTn_v[m * 64:(m + 1) * 64, h, kb * 128:(kb + 1) * 128],
                             qTn_v[cp_][m * 64:(m + 1) * 64, h, q0:512], True, True, [t_kv[kb], t_qT[cp_]], [tb[sb]])
                        scale = 64 ** -0.5
                    pti = cntA["pti"]; cntA["pti"] += 1
                    pb = PT[pti % NPT]; tpb = t_PT[pti % NPT]
                    P.act(pb[:, q0:512], banks[sb][:, q0:512], AF.Exp, [tb[sb]], [tpb], scale=scale)
                    if jd >= 0:
                        P.tt("pool", pb[:, q0:q0 + 128], pb[:, q0:q0 + 128], trib, ALU.mult, [tpb, t_cb], [tpb])
                    return (c, h, m, kb, pb, tpb)

                def emit_PV(c, h, m, kb, pb, tpb):
                    cp_ = c % 2
                    jd = kb - 4 * c
                    nkb = 4 * c + 4
                    ob = (2, 3) if m == 0 else (4, 5)
                    for qb in range(max(jd, 0), 4):
                        bk = ob[qb // 2]
                        ov = banks[bk][:, 0:260].rearrange("p (a d) -> p a d", a=2)
                        P.mm(ov[:, qb % 2, 0:129], pb[:, qb * 128:(qb + 1) * 128], Vv_v[:, kb, h, 0:129],
                             kb == 0 and qb % 2 == 0, kb == 4 * c + qb, [tpb, t_kv[kb]], [tb[bk]])
                    nmap = 1 if a1 else 2
                    if kb == nkb - 1 and m == nmap - 1:
                        for qb in range(4):
                            if a1:
                                bk = (2, 3)[qb // 2]
                                ov = banks[bk][:, 0:260].rearrange("p (a d) -> p a d", a=2)
                                P.recip(osm[:, qb:qb + 1], ov[:, qb % 2, 128:129], [tb[bk]], [t_osm])
                                P.ts("dve", mixed_v[cp_][:, qb, h * 128:(h + 1) * 128], ov[:, qb % 2, 0:128], osm[:, qb:qb + 1], ALU.mult, [tb[bk], t_osm], [t_mixed[cp_][qb]])
                            else:
                                b1 = (2, 3)[qb // 2]
                                b2 = (4, 5)[qb // 2]
                                o1 = banks[b1][:, 0:260].rearrange("p (a d) -> p a d", a=2)
                                o2 = banks[b2][:, 0:260].rearrange("p (a d) -> p a d", a=2)
                                P.recip(osm[:, 0:1], o1[:, qb % 2, 128:129], [tb[b1]], [t_osm])
                                P.recip(osm[:, 1:2], o2[:, qb % 2, 128:129], [tb[b2]], [t_osm])
                                P.ts("dve", oa, o2[:, qb % 2, 0:128], osm[:, 1:2], ALU.mult, [tb[b2], t_osm, t_lam], [t_oa], s2=neglam[:, 0:1], op1=ALU.mult)
                                P.stt(od, o1[:, qb % 2, 0:128], osm[:, 0:1], oa, ALU.mult, ALU.add, [tb[b1], t_osm, t_oa], [t_od])
                                P.act(oa, od, AF.Square, [t_od], [t_oa, t_osm], accum=osm[:, 2:3])
                                rstd_from_ss(osm[:, 2:3], 1, t_osm, 1.0 / 128)
                                P.stt(mixed_v[cp_][:, qb, h * 128:(h + 1) * 128], od, osm[:, 2:3], gsub, ALU.mult, ALU.mult, [t_od, t_osm, t_lam], [t_mixed[cp_][qb]])

                def outproj(c):
                    cp_ = c % 2
                    for qb in range(4):
                        tl = c * 4 + qb
                        row0 = s * S + tl * 128
                        gt = row0 // 128
                        Tb = bankbf(6)
                        for k in range(4):
                            P.tr(Tb[:, k * 128:(k + 1) * 128], mixed_v[cp_][:, qb, k * 128:(k + 1) * 128], identb, [t_mixed[cp_][qb], t_cb], [tb[6]])
                        yield
                        P.cp("act", mT, Tb[:, 0:512], [tb[6]], [t_mT])
                        xi = cntA["ti"]; cntA["ti"] += 1
                        xrb = xr[xi % 2]; txr = t_xr[xi % 2]
                        if a1:
                            P.ld("sp", xrb, x_d[row0:row0 + 128, :], txr)
                        else:
                            P.ld("sp", xrb, x1_d[row0:row0 + 128, :], txr, reads=[tx1[gt]])
                        yield
                        for cc in range(2):
                            bk = OB[cc]
                            for k in range(4):
                                P.mm(banks[bk][:, :], mT[:, k * 128:(k + 1) * 128],
                                     wout_b[:, k * 1024 + cc * 512:k * 1024 + (cc + 1) * 512],
                                     k == 0, k == 3, [t_mT, t_w], [tb[bk]])
                            yield
                            P.tt("dve", ytmp[:, cc * 512:(cc + 1) * 512], banks[bk][:, :], GATE1[:, cc * 512:(cc + 1) * 512], ALU.mult, [tb[bk], t_mod], [t_ytmp])
                            yield
                        P.tt("pool", xrb, xrb, ytmp, ALU.add, [txr, t_ytmp], [txr])
                        P.ld("sp", x1_d[row0:row0 + 128, :], xrb, txr, reads=[txr], writes=[tx1[gt]])
                        yield

                def chain(*gens):
                    for g_ in gens:
                        if g_ is not None:
                            yield from g_

                if stage_lim >= 1:
                    for _ in tile_proc(0):
                        pass
                    nmap = 1 if a1 else 2
                    for c in range(NCH):
                        bg = chain(outproj(c - 1) if (c > 0 and stage_lim >= 3) else None,
                                   tile_proc(c + 1) if c + 1 < NCH else None)
                        if stage_lim >= 2:
                            nkb = 4 * c + 4
                            steps = [(h, m, kb) for h in range(4) for m in range(nmap) for kb in range(nkb)]
                            est = 24 + (140 if a1 else 80)
                            pulls = max(1, -(-est // len(steps)))
                            pendA = []
                            for (h, m, kb) in steps:
                                pendA.append(emit_ST(c, h, m, kb))
                                if len(pendA) > 2:
                                    emit_PV(*pendA.pop(0))
                                for _ in range(pulls):
                                    next(bg, None)
                            while pendA:
                                emit_PV(*pendA.pop(0))
                        for _ in bg:
                            pass
                    if stage_lim >= 3:
                        for _ in outproj(NCH - 1):
                            pass

    if do_b:
        emit_cast(NCAST)
        A.off = base_off
        P.barrier()
        NTT = TT // 128
        gB = A.f32(1024)
        t_gB = P.tok("gB")
        P.ld("sp", gB, gains_d[:, G_N2:G_N2 + 1024], t_gB)
        pwq_b = A.bf16(8 * 2048)
        keys_b = A.bf16(16 * 128)
        t_wB = P.tok("wB")
        wst = [A.f32(4096), A.f32(4096)]
        t_wst = P.toks("wstB", 2)
        stage = wst
        t_stage = t_wst
        ei = 0
        for k in range(8):
            load_cast_weight(pwq_b[:, k * 2048:(k + 1) * 2048], pwq_d[k * 128:(k + 1) * 128, :], 2048, stage, t_stage, t_wB, k, ei); ei += 1
        load_cast_weight(keys_b, keys_d.rearrange("p c n -> p (c n)"), 2048, stage, t_stage, t_wB, 0, ei); ei += 1
        G2 = A.f32(1024)
        SH2 = A.f32(1024)
        GATE2 = [A.f32(1024) for _ in range(NSEQ)]
        t_mod2 = P.tok("mod2")
        t_gate2 = P.toks("gate2", NSEQ)
        mtmp = [A.f32(512), A.f32(512)]
        t_mtmp = P.toks("mtmpB", 2)
        xt = [A.f32(1024), A.f32(1024)]
        t_xt = P.toks("xB", 2)
        junk = A.bf16(1024)
        t_junk = P.tok("junkB")
        sm = A.f32(64)
        t_sm = P.tok("smB")
        h2f = A.f32(1024)
        t_h2f = P.tok("h2f")
        h2b = [A.bf16(1024), A.bf16(1024)]
        t_h2b = P.toks("h2b", 2)
        h2T = A.bf16(1024)
        t_h2T = P.tok("h2T")
        qT = A.bf16(16 * 128)
        t_qTB = P.tok("qTB")
        sc = wst[0][:, 0:2048]
        t_sc = t_wst[0]
        sc2 = [A.f32(128) for _ in range(4)]
        t_sc2 = P.toks("sc2", 4)
        sv = A.f32(256)
        t_svc = P.toks("svc", 16)
        si = A.u32(256)
        t_sic = P.toks("sic", 16)
        sif = A.f32(256)
        t_sif = P.tok("sif")
        cand = wst[0][:, 2048:4096]
        t_cand = t_wst[0]
        cand2 = [A.f32(256) for _ in range(4)]
        t_cand2 = P.toks("cand2", 4)
        fv = A.f32(128)
        t_fvh = P.toks("fvh", 8)
        fpos = A.u32(128)
        t_fph = P.toks("fph", 8)
        k0f = A.f32(128)
        k1f = A.f32(128)
        t_k = P.tok("k01")
        oh = wst[1][:, 0:2048]
        t_oh = t_wst[1]
        e0 = A.f32(128)
        e1 = A.f32(128)
        t_e = P.tok("e01")
        eidx = [A.u32(128), A.u32(128)]
        t_eidx = P.toks("eidx", 2)
        gw = [A.f32(128), A.f32(128)]
        t_gw = P.toks("gw", 2)
        araw = A.f32(128)
        cg = A.f32(128)
        NR = 8
        t_ar = P.toks("araw", NR)
        t_cf = P.toks("cg", NR)
        NG = 12
        uvb = [A.bf16(2048) for _ in range(NG)]
        t_uvb = P.toks("uvb", NG)
        dg = [A.bf16(128) for _ in range(4)]
        t_dg = P.toks("dg", 4)
        ujunk = A.bf16(1024)
        t_ujunk = P.tok("ujunk")
        prod = [A.bf16(1024) for _ in range(4)]
        t_prod = P.toks("prod", 4)
        yout = A.f32(1024)
        t_yout = P.tok("yout")
        src = x1_d if (do_a1 or do_a2) else x_d
        modbufs = (small(8), A.f32(8 * 128), P.tok("silB"), P.tok("crepB"))

        def pre(gt):
            s, tl = divmod(gt, NT)
            par = gt % 2
            row0 = gt * 128
            if tl == 0:
                compute_mod(s, [6, 7, 8, 9, 10, 11],
                            [SH2[:, 0:512], SH2[:, 512:1024], mtmp[0], mtmp[1], GATE2[s][:, 0:512], GATE2[s][:, 512:1024]],
                            [t_mod2, t_mod2, t_mtmp[0], t_mtmp[1], t_gate2[s], t_gate2[s]], wst, t_wst, [7, 0], bufs=modbufs)
                for i in range(2):
                    P.stt(G2[:, i * 512:(i + 1) * 512], mtmp[i], 1.0, gB[:, i * 512:(i + 1) * 512], ALU.add, ALU.mult, [t_mtmp[i], t_gB], [t_mod2])
                yield
            xb = xt[par]; txb = t_xt[par]
            hb = h2b[par]; thb = t_h2b[par]
            P.ld("sp", xb, src[row0:row0 + 128, :], txb, reads=[tx1[gt]])
            P.act(junk, xb, AF.Square, [txb], [t_junk, t_sm], accum=sm[:, 0:1])
            rstd_from_ss(sm[:, 0:1], 1, t_sm, 1.0 / 1024)
            P.stt(h2f, xb, sm[:, 0:1], G2, ALU.mult, ALU.mult, [txb, t_sm, t_mod2], [t_h2f])
            P.tt("dve", hb, h2f, SH2, ALU.add, [t_h2f, t_mod2], [thb])
            yield
            Tb = bankbf(6)
            for k in range(8):
                P.tr(Tb[:, k * 128:(k + 1) * 128], hb[:, k * 128:(k + 1) * 128], identb, [thb, t_cb], [tb[6]])
            P.cp("act", h2T, Tb, [tb[6]], [t_h2T])
            yield
            for g4 in range(4):
                bk = (7, 0)[g4 % 2]
                for cc in range(4):
                    c16 = g4 * 4 + cc
                    for k in range(8):
                        P.mm(banks[bk][:, cc * 128:(cc + 1) * 128], pwq_b[:, k * 2048 + c16 * 128:k * 2048 + (c16 + 1) * 128],
                             h2T[:, k * 128:(k + 1) * 128], k == 0 and cc == 0, k == 7, [t_wB, t_h2T], [tb[bk]])
                P.cp("act", qT[:, g4 * 512:(g4 + 1) * 512], banks[bk][:, :], [tb[bk]], [t_qTB])
                yield
            for g4 in range(4):
                bk = (1, 7)[g4 % 2]
                for cc in range(4):
                    c16 = g4 * 4 + cc
                    P.mm(banks[bk][:, cc * 128:(cc + 1) * 128], qT[:, c16 * 128:(c16 + 1) * 128], keys_b[:, c16 * 128:(c16 + 1) * 128],
                         cc == 0, True, [t_qTB, t_wB], [tb[bk]])
                P.cp("act", sc[:, g4 * 512:(g4 + 1) * 512], banks[bk][:, :], [tb[bk]], [t_sc])
                yield
            svv = sv.rearrange("p (c k) -> p c k", c=16)
            siv = si.rearrange("p (c k) -> p c k", c=16)
            f_max = lambda o, i_: (lambda e: e.max(out=o, in_=i_))
            f_mr = lambda o, r, i_: (lambda e: e.match_replace(out=o, in_to_replace=r, in_values=i_, imm_value=-1e30))
            f_mi = lambda o, m_, i_: (lambda e: e.max_index(out=o, in_max=m_, in_values=i_))
            for g in range(4):
                cs4 = range(4 * g, 4 * g + 4)
                for c16 in cs4:
                    P.op("dve", f_max(svv[:, c16, 0:8], sc[:, c16 * 128:(c16 + 1) * 128]), [t_sc], [t_svc[c16]])
                for c16 in cs4:
                    P.op("dve", f_mr(sc2[c16 % 4], svv[:, c16, 0:8], sc[:, c16 * 128:(c16 + 1) * 128]), [t_sc, t_svc[c16]], [t_sc2[c16 % 4]])
                for c16 in cs4:
                    P.op("dve", f_max(svv[:, c16, 8:16], sc2[c16 % 4]), [t_sc2[c16 % 4]], [t_svc[c16]])
                for c16 in cs4:
                    P.op("dve", f_mi(siv[:, c16, 0:8], svv[:, c16, 0:8], sc[:, c16 * 128:(c16 + 1) * 128]), [t_sc, t_svc[c16]], [t_sic[c16]])
                for c16 in cs4:
                    P.op("dve", f_mi(siv[:, c16, 8:16], svv[:, c16, 8:16], sc[:, c16 * 128:(c16 + 1) * 128]), [t_sc, t_svc[c16]], [t_sic[c16]])
                yield
            P.cp("dve", sif, si, t_sic, [t_sif])
            sv4 = sv.rearrange("p (h t k) -> p h t k", h=8, t=2)
            candv = cand.rearrange("p (h a b) -> p h a b", h=8, a=16)
            P.tt("dve", candv, sv4[:, :, 0, :].unsqueeze(3).to_broadcast([128, 8, 16, 16]),
                 sv4[:, :, 1, :].unsqueeze(2).to_broadcast([128, 8, 16, 16]), ALU.add, t_svc, [t_cand])
            yield
            fvv = fv.rearrange("p (h k) -> p h k", h=8)
            fpv = fpos.rearrange("p (h k) -> p h k", h=8)
            for g in range(2):
                hs4 = range(4 * g, 4 * g + 4)
                for h in hs4:
                    P.op("dve", f_max(fvv[:, h, 0:8], cand[:, h * 256:(h + 1) * 256]), [t_cand], [t_fvh[h]])
                for h in hs4:
                    P.op("dve", f_mr(cand2[h % 4], fvv[:, h, 0:8], cand[:, h * 256:(h + 1) * 256]), [t_cand, t_fvh[h]], [t_cand2[h % 4]])
                for h in hs4:
                    P.op("dve", f_max(fvv[:, h, 8:16], cand2[h % 4]), [t_cand2[h % 4]], [t_fvh[h]])
                for h in hs4:
                    P.op("dve", f_mi(fpv[:, h, 0:8], fvv[:, h, 0:8], cand[:, h * 256:(h + 1) * 256]), [t_cand, t_fvh[h]], [t_fph[h]])
                for h in hs4:
                    P.op("dve", f_mi(fpv[:, h, 8:16], fvv[:, h, 8:16], cand[:, h * 256:(h + 1) * 256]), [t_cand, t_fvh[h]], [t_fph[h]])
                yield
            P.cp("dve", k1f, fpos, t_fph, [t_k])
            P.tt("dve", oh.rearrange("p (a b) -> p a b", b=16), k1f.unsqueeze(2).to_broadcast([128, 128, 16]),
                 thr16.unsqueeze(1).to_broadcast([128, 128, 16]), ALU.is_ge, [t_k, t_consts], [t_oh])
            P.red(k0f, oh.rearrange("p (a b) -> p a b", b=16), [t_oh], [t_k])
            P.stt(k1f, k0f, -16.0, k1f, ALU.mult, ALU.add, [t_k], [t_k])
            yield
            sif4 = sif.rearrange("p (h t k) -> p h t k", h=8, t=2)
            for t_, (kf_, e_) in enumerate(((k0f, e0), (k1f, e1))):
                P.tt("dve", oh.rearrange("p (a b) -> p a b", b=16), kf_.unsqueeze(2).to_broadcast([128, 128, 16]),
                     iota16.unsqueeze(1).to_broadcast([128, 128, 16]), ALU.is_equal, [t_k, t_consts], [t_oh])
                P.tt("dve", oh.rearrange("p (h a b) -> p h a b", h=8, a=16), oh.rearrange("p (h a b) -> p h a b", h=8, a=16),
                     sif4[:, :, t_, :].unsqueeze(2).to_broadcast([128, 8, 16, 16]), ALU.mult, [t_oh, t_sif], [t_oh])
                P.red(e_, oh.rearrange("p (a b) -> p a b", b=16), [t_oh], [t_e])
                yield
            P.stt(e0, e0, 128.0, e1, ALU.mult, ALU.add, [t_e], [t_e])
            P.cp("dve", eidx[par], e0, [t_e], [t_eidx[par]])
            gwp = gw[par]; tgw = t_gw[par]
            P.tt("dve", gwp.rearrange("p (h k) -> p h k", h=8), fvv, fvv[:, :, 0:1].to_broadcast([128, 8, 16]), ALU.subtract, t_fvh, [tgw])
            P.act(gwp, gwp, AF.Exp, [tgw], [tgw])
            P.red(sm[:, 8:16], gwp.rearrange("p (h k) -> p h k", h=8), [tgw], [t_sm])
            P.recip(sm[:, 8:16], sm[:, 8:16], [t_sm], [t_sm])
            P.tt("dve", gwp.rearrange("p (h k) -> p h k", h=8), gwp.rearrange("p (h k) -> p h k", h=8),
                 sm[:, 8:16].unsqueeze(2).to_broadcast([128, 8, 16]), ALU.mult, [tgw, t_sm], [tgw])
            yield

        cnt = {"gi": 0, "di": 0}

        def slot_front(gt, sl):
            par = gt % 2
            gi = cnt["gi"]; cnt["gi"] += 1
            ub = uvb[gi % NG]; tub = t_uvb[gi % NG]
            P.dma("pool", (lambda o, ix: (lambda e: e.indirect_dma_start(out=o, out_offset=None, in_=uv_d[:, :],
                  in_offset=bass.IndirectOffsetOnAxis(ap=ix, axis=0))))(ub, eidx[par][:, sl:sl + 1]), tub, reads=[t_eidx[par], t_uv], writes=[tub])
            tar = t_ar[sl % NR]; tcf = t_cf[sl % NR]
            pr = prod[gi % 4]; tpr = t_prod[gi % 4]
            P.tt("dve", pr, ub[:, 0:1024], h2b[par], ALU.mult, [tub, t_h2b[par]], [tpr])
            P.act(ujunk, pr, AF.Copy, [tpr], [t_ujunk, tar], accum=araw[:, sl:sl + 1])
            P.act(cg[:, sl:sl + 1], araw[:, sl:sl + 1], AF.Gelu, [tar], [tcf])
            return (gt, sl, ub, tub)

        def slot_back(gt, sl, ub, tub):
            par = gt % 2
            yb = (2, 3) if par == 0 else (4, 5)
            tcf = t_cf[sl % NR]
            di = cnt["di"]; cnt["di"] += 1
            db = dg[di % 4]; tdb = t_dg[di % 4]
            P.ts("dve", db, identb, cg[:, sl:sl + 1], ALU.mult, [t_cb, tcf, t_gw[par]], [tdb], s2=gw[par][:, sl:sl + 1], op1=ALU.mult)
            for cc in range(2):
                bk = yb[cc]
                P.mm(banks[bk][:, :], db, ub[:, 1024 + cc * 512:1024 + (cc + 1) * 512], sl == 0, sl == 127, [tdb, tub], [tb[bk]])
            if sl == 127:
                fin(gt)

        def fin(gt):
            s, tl = divmod(gt, NT)
            par = gt % 2
            yb = (2, 3) if par == 0 else (4, 5)
            row0 = gt * 128
            for cc in range(2):
                bk = yb[cc]
                P.tt("dve", yout[:, cc * 512:(cc + 1) * 512], banks[bk][:, :], GATE2[s][:, cc * 512:(cc + 1) * 512], ALU.mult, [tb[bk], t_gate2[s]], [t_yout])
            P.tt("pool", yout, yout, xt[par], ALU.add, [t_yout, t_xt[par]], [t_yout])
            P.ld("sp", out_d[row0:row0 + 128, :], yout, t_yout, reads=[t_yout], writes=[], final=True)

        for _ in pre(0):
            pass
        SKEW = 2
        pend = []
        for gt in range(NTT):
            gen = pre(gt + 1) if gt + 1 < NTT else None
            for sl in range(128):
                pend.append(slot_front(gt, sl))
                if len(pend) > SKEW:
                    slot_back(*pend.pop(0))
                if gen is not None and sl % 2 == 1 and sl >= 8:
                    next(gen, None)
            if gen is not None:
                for _ in gen:
                    pass
        while pend:
            slot_back(*pend.pop(0))
    else:
        A.off = base_off
        P.barrier()
        cb = [A.f32(1024), A.f32(1024)]
        t_cbuf = P.toks("cpb", 2)
        for gt in range(TT // 128):
            P.ld("sp", cb[gt % 2], x1_d[gt * 128:(gt + 1) * 128, :], t_cbuf[gt % 2], reads=[tx1[gt]])
            P.ld("sp", out_d[gt * 128:(gt + 1) * 128, :], cb[gt % 2], t_cbuf[gt % 2], reads=[t_cbuf[gt % 2]], writes=[], final=True)

    P.arena_hw = A.hw
    P.emit()
    return nc, P


def rope_table(S):
    half = 32
    inv = (1.0 / (10000.0 ** (np.arange(half, dtype=np.float32) / np.float32(half)))).astype(np.float32)
    ang = np.arange(S, dtype=np.float32)[:, None] * inv[None, :]
    cs = np.concatenate([np.cos(ang), np.sin(ang)], axis=-1).astype(np.float32)
    return np.ascontiguousarray(cs.reshape(S // 128, 128, 64).transpose(1, 0, 2))


def make_consts():
    c = np.zeros((128, C_TOT), np.float32)
    c[:, C_ID:C_ID + 128] = np.eye(128, dtype=np.float32)
    k = np.arange(128)
    c[:, C_TRI:C_TRI + 128] = (k[:, None] <= k[None, :]).astype(np.float32)
    c[:, C_IOTA:C_IOTA + 16] = np.arange(16, dtype=np.float32)[None, :]
    c[:, C_ONES:C_ONES + 128] = 1.0
    c[:, C_THR:C_THR + 15] = 16.0 * np.arange(1, 16, dtype=np.float32)[None, :]
    c[:, C_THR + 15] = 1e9
    return c


def host_layout(inp, S, NSEQ, ncores):
    f = lambda a: np.ascontiguousarray(np.asarray(a, dtype=np.float32))
    x = f(inp["x"])
    c = f(inp["c"])
    rep = lambda v: np.broadcast_to(f(v).reshape(1, -1), (128, f(v).size))
    mqg = f(inp["mla_q_g"])[0]
    mkg = f(inp["mla_k_g"])[0]
    gains = np.concatenate([
        rep(inp["norm1_g"][0]), rep(inp["mla_q_lat_g"][0]), rep(inp["mla_kv_lat_g"][0]),
        rep(mkg[128:192]),
        rep(np.tile(f(inp["diff_q_g"])[0], 8)), rep(np.tile(f(inp["diff_k_g"])[0], 8)),
        rep(np.tile(mqg, 4)), rep(np.tile(mkg[:128], 4)),
        rep(inp["diff_subln_g"][0]), rep(inp["norm2_g"][0]),
        rep(inp["diff_lq1"][0]), rep(inp["diff_lk1"][0]), rep(inp["diff_lq2"][0]), rep(inp["diff_lk2"][0]),
    ], axis=1)
    gains = np.ascontiguousarray(gains, dtype=np.float32)
    assert gains.shape[1] == G_TOT
    keysT = np.ascontiguousarray(f(inp["peer_sub_keys"])[0].reshape(16, 128, 128).transpose(2, 0, 1))
    shared = {
        "ada_w": f(inp["ada_w"])[0], "ada_b": f(inp["ada_b"])[0].reshape(1, -1),
        "w_in": f(inp["w_in"])[0], "w_q_up": f(inp["mla_w_q_up"])[0], "w_kv_up": f(inp["mla_w_kv_up"])[0],
        "w_out": f(inp["w_out"])[0], "peer_w_q": f(inp["peer_w_q"])[0], "keysT": keysT,
        "peer_u": f(inp["peer_u"])[0], "peer_v": f(inp["peer_v"])[0],
        "gains": gains, "rope": rope_table(S), "consts": make_consts(),
    }
    maps = []
    for i in range(ncores):
        xs = np.ascontiguousarray(x[i * NSEQ:(i + 1) * NSEQ].reshape(NSEQ * S, D))
        cs = c[i * NSEQ:(i + 1) * NSEQ]
        cT = np.ascontiguousarray(cs.reshape(NSEQ, 8, 128).transpose(2, 0, 1))
        m = dict(shared)
        m["x"] = xs
        m["cT"] = cT
        maps.append(m)
    return maps


_CACHE = {}


def kernel(**inputs):
    B, S, _ = inputs["x"].shape
    ncores = 8
    NSEQ = B // ncores
    key = (S, NSEQ)
    if key not in _CACHE:
        _CACHE[key] = build(S, NSEQ)[0]
    nc = _CACHE[key]
    maps = host_layout(inputs, S, NSEQ, ncores)
    res = run_bass_kernel_spmd(nc, maps, core_ids=list(range(ncores)))
    out = np.stack([r["out"].reshape(NSEQ, S, D) for r in res.results], axis=0).reshape(B, S, D)
    return out.astype(np.float32)
```
